# Optimizing a Trainium2 kernel written in Bass

```python
import math
import jax, jax.numpy as jnp
from jax import lax
import numpy as np

D_MODEL = 1024
BATCH = 2
SEQ = 16384
DEPTH = 2
DEC_BATCH = 8
DEC_SEQ = 2048
PAST_LEN = 128

GRID_W = 64
HEAD_DIM = 64
N_EVEN = (DEPTH + 1) // 2
N_ODD = DEPTH // 2
DIFF_HEADS = 4
DIFF_VDIM = 2 * HEAD_DIM
GQA_HEADS = 8
GQA_KV_HEADS = 2
GQA_GROUP = GQA_HEADS // GQA_KV_HEADS
NA_HEADS = D_MODEL // HEAD_DIM
NA_KH_MAX = 8
NA_KW = 16
NA_BLOCK = GRID_W
DIFF_QK_W = DIFF_HEADS * 2 * HEAD_DIM
DIFF_V_W = DIFF_HEADS * DIFF_VDIM
GQA_Q_W = GQA_HEADS * HEAD_DIM
GQA_KV_W = GQA_KV_HEADS * HEAD_DIM
EVEN_SPLITS = [DIFF_QK_W, 2 * DIFF_QK_W, 2 * DIFF_QK_W + DIFF_V_W,
               2 * DIFF_QK_W + DIFF_V_W + GQA_Q_W,
               2 * DIFF_QK_W + DIFF_V_W + GQA_Q_W + GQA_KV_W]
EVEN_IN_W = 2 * DIFF_QK_W + DIFF_V_W + GQA_Q_W + 2 * GQA_KV_W
MIX_W = DIFF_V_W + GQA_Q_W
NA_W = NA_HEADS * HEAD_DIM
D_FF = 2816
CONV_W = 3
Q_BLOCK = 128
ROPE_THETA = 10000.0
EPS = 1e-6
SUBLN_EPS = 1e-5

kernel_name = "hybrid_diffattn_gqa_natten_encoder"


def rmsnorm(x, g, eps=EPS):
    xf = x.astype(jnp.float32)
    y = xf * lax.rsqrt(jnp.mean(xf * xf, axis=-1, keepdims=True) + eps)
    return (y * g.astype(jnp.float32)).astype(x.dtype)


def rope_1d(seq, dim):
    inv = ROPE_THETA ** (-jnp.arange(0, dim, 2, dtype=jnp.float32) / dim)
    t = jnp.arange(seq, dtype=jnp.float32)
    return t[:, None] * inv[None, :]


def rope_axial(seq, dim):
    half = dim // 2
    inv = ROPE_THETA ** (-jnp.arange(0, half, 2, dtype=jnp.float32) / half)
    t = jnp.arange(seq, dtype=jnp.int32)
    row = (t // GRID_W).astype(jnp.float32)
    col = (t % GRID_W).astype(jnp.float32)
    return jnp.concatenate([row[:, None] * inv[None, :], col[:, None] * inv[None, :]], axis=-1)


def apply_rope(x, ang):
    shape = (ang.shape[0],) + (1,) * (x.ndim - 3) + (ang.shape[1],)
    cos = jnp.cos(ang).reshape(shape).astype(x.dtype)
    sin = jnp.sin(ang).reshape(shape).astype(x.dtype)
    x1, x2 = jnp.split(x, 2, axis=-1)
    return jnp.concatenate([x1 * cos - x2 * sin, x2 * cos + x1 * sin], axis=-1)


def even_mixer(h, w_in, w_out, lam_vec, subln_g, qk_g, layer_idx):
    b, s, _ = h.shape
    nb = s // Q_BLOCK
    proj = h @ w_in
    qa, ka, va, qb, kb, vb = jnp.split(proj, EVEN_SPLITS, axis=-1)
    ang1 = rope_1d(s, HEAD_DIM)
    qa = apply_rope(qa.reshape(b, s, DIFF_HEADS, 2, HEAD_DIM), ang1)
    ka = apply_rope(ka.reshape(b, s, DIFF_HEADS, 2, HEAD_DIM), ang1)
    va = va.reshape(b, s, DIFF_HEADS, DIFF_VDIM)
    lambda_init = 0.8 - 0.6 * math.exp(-0.3 * layer_idx)
    lf = lam_vec.astype(jnp.float32)
    lam = jnp.exp(jnp.sum(lf[0] * lf[1])) - jnp.exp(jnp.sum(lf[2] * lf[3])) + lambda_init
    ang2 = rope_axial(s, HEAD_DIM)
    qb = apply_rope(rmsnorm(qb.reshape(b, s, GQA_KV_HEADS, GQA_GROUP, HEAD_DIM), qk_g[0]), ang2)
    kb = apply_rope(rmsnorm(kb.reshape(b, s, GQA_KV_HEADS, HEAD_DIM), qk_g[1]), ang2)
    vb = vb.reshape(b, s, GQA_KV_HEADS, HEAD_DIM)
    scale = HEAD_DIM ** -0.5

    def block(qs):
        qa_b, qb_b = qs
        sa = jnp.einsum('bqhcd,bkhcd->bhcqk', qa_b, ka).astype(jnp.float32) * scale
        pa = jax.nn.softmax(sa, axis=-1)
        pdiff = pa[:, :, 0] - lam * pa[:, :, 1]
        oa = jnp.einsum('bhqk,bkhe->bqhe', pdiff.astype(va.dtype), va)
        oa = rmsnorm(oa, subln_g, eps=SUBLN_EPS) * (1.0 - lambda_init)
        sb = jnp.einsum('bqngd,bknd->bngqk', qb_b, kb).astype(jnp.float32) * scale
        pb = jax.nn.softmax(sb, axis=-1)
        ob = jnp.einsum('bngqk,bknd->bqngd', pb.astype(vb.dtype), vb)
        return jnp.concatenate([oa.reshape(b, Q_BLOCK, DIFF_V_W),
                                ob.reshape(b, Q_BLOCK, GQA_Q_W)], axis=-1)

    qa_blk = qa.reshape(b, nb, Q_BLOCK, DIFF_HEADS, 2, HEAD_DIM).swapaxes(0, 1)
    qb_blk = qb.reshape(b, nb, Q_BLOCK, GQA_KV_HEADS, GQA_GROUP, HEAD_DIM).swapaxes(0, 1)
    o = lax.map(block, (qa_blk, qb_blk))
    o = o.swapaxes(0, 1).reshape(b, s, MIX_W)
    return o @ w_out


def na_indices(s):
    rows = s // GRID_W
    kh = min(NA_KH_MAX, rows)
    t = jnp.arange(s, dtype=jnp.int32)
    r = t // GRID_W
    col = t % GRID_W
    rs = jnp.clip(r - kh // 2, 0, rows - kh)
    cs = jnp.clip(col - NA_KW // 2, 0, GRID_W - NA_KW)
    kr = rs[:, None, None] + jnp.arange(kh, dtype=jnp.int32)[None, :, None]
    kc = cs[:, None, None] + jnp.arange(NA_KW, dtype=jnp.int32)[None, None, :]
    idx = (kr * GRID_W + kc).reshape(s, kh * NA_KW)
    dr = kr - r[:, None, None] + (NA_KH_MAX - 1)
    dc = kc - col[:, None, None] + (NA_KW - 1)
    bias_idx = (dr * (2 * NA_KW - 1) + dc).reshape(s, kh * NA_KW)
    return idx, bias_idx


def odd_mixer(h, w_qkv, rpb, w_out):
    b, s, _ = h.shape
    nb = s // NA_BLOCK
    qkv = (h @ w_qkv).reshape(b, s, 3, NA_HEADS, HEAD_DIM)
    q, k, v = qkv[:, :, 0], qkv[:, :, 1], qkv[:, :, 2]
    idx, bias_idx = na_indices(s)
    n_keys = idx.shape[-1]
    rpb_flat = rpb.reshape(NA_HEADS, -1)
    scale = HEAD_DIM ** -0.5

    def block(xs):
        q_b, idx_b, bidx_b = xs
        kg = jnp.take(k, idx_b, axis=1)
        vg = jnp.take(v, idx_b, axis=1)
        bias = rpb_flat[:, bidx_b].astype(jnp.float32)
        sc = jnp.einsum('bqhd,bqkhd->bhqk', q_b, kg).astype(jnp.float32) * scale + bias[None]
        p = jax.nn.softmax(sc, axis=-1)
        o = jnp.einsum('bhqk,bqkhd->bqhd', p.astype(v.dtype), vg)
        return o.reshape(b, NA_BLOCK, NA_W)

    q_blk = q.reshape(b, nb, NA_BLOCK, NA_HEADS, HEAD_DIM).swapaxes(0, 1)
    o = lax.map(block, (q_blk, idx.reshape(nb, NA_BLOCK, n_keys),
                        bias_idx.reshape(nb, NA_BLOCK, n_keys)))
    o = o.swapaxes(0, 1).reshape(b, s, NA_W)
    return o @ w_out


def conv_ffn(h, w_up, conv_w, conv_b, w_down):
    u = h @ w_up
    up = jnp.pad(u, ((0, 0), (1, 1), (0, 0)))
    u = up[:, :-2] * conv_w[0] + up[:, 1:-1] * conv_w[1] + up[:, 2:] * conv_w[2] + conv_b
    gate, val = jnp.split(u, 2, axis=-1)
    return (jax.nn.silu(gate) * val) @ w_down


def trunk(x, c, ada_w, ada_b, norm_g, even_w_in, even_w_out, diff_lambda, diff_subln_g,
          gqa_qk_g, odd_w_qkv, odd_rpb, odd_w_out, ffn_w_up, ffn_conv_w, ffn_conv_b,
          ffn_w_down, final_g):
    c_act = jax.nn.silu(c)
    for l in range(DEPTH):
        mod = c_act @ ada_w[l] + ada_b[l]
        sh1, sc1, g1, sh2, sc2, g2 = [m[:, None, :] for m in jnp.split(mod, 6, axis=-1)]
        h = rmsnorm(x, norm_g[l, 0]) * (1.0 + sc1) + sh1
        if l % 2 == 0:
            e = l // 2
            m = even_mixer(h, even_w_in[e], even_w_out[e], diff_lambda[e], diff_subln_g[e],
                           gqa_qk_g[e], l)
        else:
            o = l // 2
            m = odd_mixer(h, odd_w_qkv[o], odd_rpb[o], odd_w_out[o])
        x = x + g1 * m
        h = rmsnorm(x, norm_g[l, 1]) * (1.0 + sc2) + sh2
        x = x + g2 * conv_ffn(h, ffn_w_up[l], ffn_conv_w[l], ffn_conv_b[l], ffn_w_down[l])
    return rmsnorm(x, final_g)


def setup_inputs(seed: int = 0) -> dict:
    key = jax.random.key(seed)
    ks = jax.random.split(key, 20)
    D = D_MODEL

    def nrm(k, shape, scale):
        return jax.random.normal(k, shape, jnp.float32) * scale

    return {
        "x_prompt": nrm(ks[0], (BATCH, SEQ, D), 1.0),
        "x_sample": nrm(ks[1], (DEC_BATCH, DEC_SEQ, D), 1.0),
        "c_prompt": nrm(ks[2], (BATCH, D), 1.0),
        "c_sample": nrm(ks[3], (DEC_BATCH, D), 1.0),
        "ada_w": nrm(ks[4], (DEPTH, D, 6 * D), D ** -0.5),
        "ada_b": nrm(ks[5], (DEPTH, 6 * D), 0.02),
        "norm_g": 1.0 + nrm(ks[6], (DEPTH, 2, D), 0.02),
        "even_w_in": nrm(ks[7], (N_EVEN, D, EVEN_IN_W), D ** -0.5),
        "even_w_out": nrm(ks[8], (N_EVEN, MIX_W, D), MIX_W ** -0.5),
        "diff_lambda": nrm(ks[9], (N_EVEN, 4, HEAD_DIM), 0.1),
        "diff_subln_g": 1.0 + nrm(ks[10], (N_EVEN, DIFF_VDIM), 0.02),
        "gqa_qk_g": 1.0 + nrm(ks[11], (N_EVEN, 2, HEAD_DIM), 0.02),
        "odd_w_qkv": nrm(ks[12], (N_ODD, D, 3 * NA_W), D ** -0.5),
        "odd_rpb": nrm(ks[13], (N_ODD, NA_HEADS, 2 * NA_KH_MAX - 1, 2 * NA_KW - 1), 0.5),
        "odd_w_out": nrm(ks[14], (N_ODD, NA_W, D), NA_W ** -0.5),
        "ffn_w_up": nrm(ks[15], (DEPTH, D, 2 * D_FF), D ** -0.5),
        "ffn_conv_w": nrm(ks[16], (DEPTH, CONV_W, 2 * D_FF), CONV_W ** -0.5),
        "ffn_conv_b": nrm(ks[17], (DEPTH, 2 * D_FF), 0.02),
        "ffn_w_down": nrm(ks[18], (DEPTH, D_FF, D), D_FF ** -0.5),
        "final_g": 1.0 + nrm(ks[19], (D,), 0.02),
    }


def reference(x_prompt, x_sample, c_prompt, c_sample, ada_w, ada_b, norm_g, even_w_in,
              even_w_out, diff_lambda, diff_subln_g, gqa_qk_g, odd_w_qkv, odd_rpb, odd_w_out,
              ffn_w_up, ffn_conv_w, ffn_conv_b, ffn_w_down, final_g):
    y_prompt = trunk(x_prompt, c_prompt, ada_w, ada_b, norm_g, even_w_in, even_w_out,
                     diff_lambda, diff_subln_g, gqa_qk_g, odd_w_qkv, odd_rpb, odd_w_out,
                     ffn_w_up, ffn_conv_w, ffn_conv_b, ffn_w_down, final_g)
    y_sample = trunk(x_sample, c_sample, ada_w, ada_b, norm_g, even_w_in, even_w_out,
                     diff_lambda, diff_subln_g, gqa_qk_g, odd_w_qkv, odd_rpb, odd_w_out,
                     ffn_w_up, ffn_conv_w, ffn_conv_b, ffn_w_down, final_g)
    return (y_prompt, y_sample)
```

```python
import contextlib
import math
import numpy as np
import concourse.bass as bass
import concourse.mybir as mybir
from concourse.bass_utils import run_bass_kernel_spmd

F32 = mybir.dt.float32
BF16 = mybir.dt.bfloat16
U8 = mybir.dt.uint8
AF = mybir.ActivationFunctionType
ALU = mybir.AluOpType

D = 1024
KC = 8
SEQ = 16384
DSEQ = 2048
OWN = 4096
HALO = 384
TP = OWN + 2 * HALO
TS = DSEQ
GRID_W = 64


def set_cfg(seq, own, dseq):
    global SEQ, OWN, DSEQ, TP, TS
    SEQ, OWN, DSEQ = seq, own, dseq
    TP = OWN + 2 * HALO
    TS = DSEQ

D_FF = 2816
NFC = 22
EPS = 1e-6
SUBLN_EPS = 1e-5
LAMBDA_INIT0 = 0.8 - 0.6 * math.exp(-0.3 * 0)
SCALE = 64 ** -0.5
NDELTA = 7

EPOCH = 16000
NDSEM = {'sp': 16, 'pool': 56, 'act': 4, 'pe': 4, 'dve': 4}
ARENA = 196 * 1024
PERS = 10 * 1024


class Buf:
    __slots__ = ("name", "w", "r")

    def __init__(self, name=""):
        self.name = name
        self.w = None
        self.r = {}


class Op:
    __slots__ = ("eng", "fn", "deps", "needs_inc", "is_dma", "dsem", "dval", "inc_n")

    def __init__(self, eng, fn, is_dma=False):
        self.eng = eng
        self.fn = fn
        self.deps = []
        self.needs_inc = False
        self.is_dma = is_dma
        self.dsem = None
        self.dval = 0
        self.inc_n = None


class Sched:
    ENGS = ("pe", "act", "dve", "pool", "sp")

    def __init__(self, nc, same_engine_sync=("act", "dve", "pool")):
        self.nc = nc
        self.ops = {e: [] for e in self.ENGS}
        self.same = set(same_engine_sync)
        self.dma_ring = {e: 0 for e in self.ENGS}
        self.dma_last = {}
        self.dma_val = {}

    def op(self, eng, fn, reads=(), writes=(), is_dma=False, extra_deps=()):
        o = Op(eng, fn, is_dma)
        deps = list(extra_deps)
        for b in reads:
            if b.w is not None:
                deps.append(b.w)
        for b in writes:
            if b.w is not None:
                deps.append(b.w)
            for r in b.r.values():
                deps.append(r)
        if is_dma:
            i = self.dma_ring[eng]
            self.dma_ring[eng] = (i + 1) % NDSEM[eng]
            key = (eng, i)
            prev = self.dma_last.get(key)
            if prev is not None:
                deps.append(prev)
            self.dma_last[key] = o
            self.dma_val[key] = self.dma_val.get(key, 0) + 16
            o.dsem = key
            o.dval = self.dma_val[key]
        seen = set()
        for d in deps:
            if d is o or id(d) in seen:
                continue
            seen.add(id(d))
            if (not d.is_dma) and d.eng == eng and eng not in self.same:
                continue
            o.deps.append(d)
            if not d.is_dma:
                d.needs_inc = True
        for b in reads:
            b.r[eng] = o
        for b in writes:
            b.w = o
            b.r = {}
        self.ops[eng].append(o)
        return o

    def barrier(self):
        lasts = []
        for e in self.ENGS:
            for o in reversed(self.ops[e]):
                if not o.is_dma and o.fn is not None:
                    lasts.append(o)
                    break
        lasts += list(self.dma_last.values())
        for e in self.ENGS:
            self.op(e, None, extra_deps=[d for d in lasts if d.is_dma or d.eng != e])

    def emit(self):
        nc = self.nc
        with contextlib.ExitStack() as st:
            esems = {}
            for e in self.ENGS:
                n = 0
                for o in self.ops[e]:
                    if o.needs_inc and not o.is_dma:
                        o.inc_n = n
                        n += 1
                nep = (n + EPOCH - 1) // EPOCH
                esems[e] = [st.enter_context(nc.semaphore(f"s_{e}_{k}")) for k in range(max(nep, 1))]
            dsems = {}
            for key in self.dma_val:
                dsems[key] = st.enter_context(nc.semaphore(f"d_{key[0]}_{key[1]}"))
            block = st.enter_context(nc.Block())
            handles = {"pe": block.tensor, "act": block.scalar, "dve": block.vector,
                       "pool": block.gpsimd, "sp": block.sync}
            nw = {e: 0 for e in self.ENGS}

            def make(e):
                def body(eng):
                    seen_e = {}
                    seen_d = {}
                    for o in self.ops[e]:
                        for d in o.deps:
                            if d.is_dma:
                                if seen_d.get(d.dsem, 0) >= d.dval:
                                    continue
                                seen_d[d.dsem] = d.dval
                                eng.wait_ge(dsems[d.dsem], d.dval)
                                nw[e] += 1
                            else:
                                if seen_e.get(d.eng, -1) >= d.inc_n:
                                    continue
                                seen_e[d.eng] = d.inc_n
                                eng.wait_ge(esems[d.eng][d.inc_n // EPOCH], d.inc_n % EPOCH + 1)
                                nw[e] += 1
                        if o.fn is None:
                            continue
                        inst = o.fn(eng)
                        if o.is_dma:
                            inst.then_inc(dsems[o.dsem], 16)
                        elif o.needs_inc:
                            inst.then_inc(esems[e][o.inc_n // EPOCH], 1)
                return body

            for e in self.ENGS:
                if self.ops[e]:
                    handles[e](make(e))
            self.stats = {e: (len(self.ops[e]), nw[e]) for e in self.ENGS}


class T:
    __slots__ = ("ap", "buf")

    def __init__(self, ap, name=""):
        self.ap = ap
        self.buf = Buf(name)


def split_range(lo, hi, wmax):
    n = -(-(hi - lo) // wmax)
    w = -(-(hi - lo) // n)
    w = -(-w // 2) * 2
    out = []
    a = lo
    while a < hi:
        out.append((a, min(a + w, hi)))
        a += w
    return out


class Builder:
    def __init__(self, debug=False, phases="ABCDEFGH"):
        self.debug = debug
        self.phases = phases
        nc = self.nc = bass.Bass("TRN2", target_bir_lowering=False)
        self.S = Sched(nc)
        self.din = {}
        self.dout = {}
        self.arena_ap = nc.alloc_sbuf_tensor("arena", [128, ARENA], U8).ap()
        self.arena_off = 0
        self.pers_ap = nc.alloc_sbuf_tensor("pers", [128, PERS], U8).ap()
        self.pers_off = 0
        ps = nc.alloc_psum_tensor("ps", [128, 4096], F32).ap()
        self.bank = [T(ps[:, b * 512:(b + 1) * 512], f"bank{b}") for b in range(8)]
        self.ps_all = ps
        self.rr = 0

    def inp(self, name, shape, dtype=F32):
        t = self.nc.dram_tensor(name, list(shape), dtype, kind="ExternalInput").ap()
        self.din[name] = (tuple(shape), dtype)
        return T(t, name)

    def outp(self, name, shape, dtype=F32):
        t = self.nc.dram_tensor(name, list(shape), dtype, kind="ExternalOutput").ap()
        self.dout[name] = tuple(shape)
        return T(t, name)

    def scratch(self, name, shape, dtype):
        if self.debug:
            return self.outp(name, shape, dtype)
        t = self.nc.dram_tensor(name, list(shape), dtype, kind="Internal").ap()
        return T(t, name)

    def _carve(self, which, shape, dtype, name):
        esz = 2 if dtype == BF16 else 4
        n = 1
        for s in shape:
            n *= s
        nbytes = (n * esz + 63) // 64 * 64
        if which == "arena":
            off = self.arena_off
            self.arena_off += nbytes
            assert self.arena_off <= ARENA, (name, self.arena_off)
            base = self.arena_ap
        else:
            off = self.pers_off
            self.pers_off += nbytes
            assert self.pers_off <= PERS, (name, self.pers_off)
            base = self.pers_ap
        ap = base[:, off:off + n * esz].bitcast(dtype)
        if len(shape) == 2:
            ap = ap.rearrange("p (a b) -> p a b", a=shape[0])
        elif len(shape) == 3:
            ap = ap.rearrange("p (a b c) -> p a b c", a=shape[0], b=shape[1])
        elif len(shape) == 4:
            ap = ap.rearrange("p (a b c d) -> p a b c d", a=shape[0], b=shape[1], c=shape[2])
        return T(ap, name)

    def tile(self, shape, dtype, name=""):
        return self._carve("arena", shape, dtype, name)

    def ptile(self, shape, dtype, name=""):
        return self._carve("pers", shape, dtype, name)

    def new_phase(self):
        self.S.barrier()
        self.arena_off = 0

    def op(self, eng, meth, *args, reads=(), writes=(), dma=False, **kw):
        def fn(e, meth=meth, args=args, kw=kw):
            return getattr(e, meth)(*args, **kw)
        return self.S.op(eng, fn, reads=[t.buf for t in reads], writes=[t.buf for t in writes], is_dma=dma)

    def dma(self, q, out_ap, in_ap, reads=(), writes=(), **kw):
        return self.op(q, "dma_start", out=out_ap, in_=in_ap, reads=reads, writes=writes, dma=True, **kw)

    def castload(self, dst, dst_ap, src, src_ap):
        return self.dma("pool", dst_ap, src_ap, reads=[src], writes=[dst], max_dma_last_dim=2048)

    def mm(self, out, lhsT, rhs, start, stop, reads, writes, **kw):
        return self.op("pe", "matmul", out, lhsT=lhsT, rhs=rhs, start=start, stop=stop,
                       reads=reads, writes=writes, **kw)

    def nbank(self, lst):
        b = lst[self.rr % len(lst)]
        self.rr += 1
        return self.bank[b]


def dview(t, pat, **kw):
    return t.ap.rearrange(pat, **kw)


class Seg:
    pass


def build_program(debug=False, phases="ABCDEFGH"):
    B = Builder(debug, phases)
    nc, S = B.nc, B.S
    op, dma, mm = B.op, B.dma, B.mm

    xkvT = B.inp("xkvT", [KC, 128, SEQ])
    xeT = B.inp("xeT", [KC, 128, TP])
    xsT = B.inp("xsT", [KC, 128, TS])
    cT = B.inp("cT", [128, KC, 2])
    ada_w = B.inp("ada_w", [2, D, 6 * D])
    ada_bT = B.inp("ada_bT", [128, 2, 48])
    norm_gT = B.inp("norm_gT", [128, 2, 2, KC])
    final_gT = B.inp("final_gT", [128, KC])
    w_in = B.inp("w_in", [D, 2304])
    w_in_sw = B.inp("w_in_sw", [D, 2304])
    w_out0 = B.inp("w_out0", [D, D])
    lam_in = B.inp("lam_in", [128, 256])
    subln_in = B.inp("subln_in", [128, 128])
    qkg_in = B.inp("qkg_in", [128, 4])
    w_qkv1 = B.inp("w_qkv1", [D, 3 * D])
    w_out1 = B.inp("w_out1", [D, D])
    rpbpad = B.inp("rpbpad", [16, 16, 128])
    w_up = B.inp("w_up", [2, D, 2 * D_FF])
    conv_wT = B.inp("conv_wT", [128, 2, 3, 44])
    conv_bT = B.inp("conv_bT", [128, 2, 44])
    w_down = B.inp("w_down", [2, D_FF, D])
    rt_kv = B.inp("rt_kv", [128, 4, SEQ])
    rt_e = B.inp("rt_e", [128, 4, TP])
    rt_s = B.inp("rt_s", [128, 4, TS])
    vm_e = B.inp("vm_e", [128, TP])
    rf_e = B.inp("rf_e", [128, (TP // 128) * NDELTA * 2])
    rf_s = B.inp("rf_s", [128, (TS // 128) * NDELTA * 2])
    colmask_in = B.inp("colmask_in", [128, 64])
    ident_in = B.inp("ident_in", [128, 128])
    jmat_in = B.inp("jmat_in", [64, 64])

    yT_p = B.outp("yT_p", [KC, 128, OWN])
    yT_s = B.outp("yT_s", [KC, 128, TS])

    P = Seg()
    P.name, P.T, P.LK, P.si = "P", TP, SEQ, 0
    P.xkv, P.xq, P.rt_kv, P.rt_q, P.rf = xkvT, xeT, rt_kv, rt_e, rf_e
    P.own = (HALO, HALO + OWN)
    P.y = yT_p
    P.top_s, P.bot_s = HALO // 128, (HALO + OWN) // 128 - 1
    Sg = Seg()
    Sg.name, Sg.T, Sg.LK, Sg.si = "S", TS, TS, 1
    Sg.xkv, Sg.xq, Sg.rt_kv, Sg.rt_q, Sg.rf = xsT, xsT, rt_s, rt_s, rf_s
    Sg.own = (0, TS)
    Sg.y = yT_s
    Sg.top_s, Sg.bot_s = 0, TS // 128 - 1
    segs = [P, Sg]
    for g in segs:
        n = g.name
        nkt = g.LK // 128
        g.KT0d = B.scratch(f"KT0d_{n}", [4, 128, g.LK], BF16)
        g.KT0g = B.scratch(f"KT0g_{n}", [2, 128, g.LK], BF16)
        g.V0d = B.scratch(f"V0d_{n}", [4, 128, nkt, 130], BF16)
        g.V0g = B.scratch(f"V0g_{n}", [2, 128, nkt, 66], BF16)
        g.X1a = B.scratch(f"X1a_{n}", [KC, 128, g.T], F32)
        g.X1 = B.scratch(f"X1_{n}", [KC, 128, g.T], F32)
        g.X2a = B.scratch(f"X2a_{n}", [KC, 128, g.T], F32)
        g.X2 = B.scratch(f"X2_{n}", [KC, 128, g.T], F32)
        g.Q1T = B.scratch(f"Q1T_{n}", [KC, 128, g.T], BF16)
        g.K1T = B.scratch(f"K1T_{n}", [KC, 128, g.T], BF16)
        g.V1 = B.scratch(f"V1_{n}", [128, g.T // 128, 16, 66], BF16)

    WUB = [B.scratch(f"WUB{l}", [128, KC, 2 * D_FF], BF16) for l in range(2)]
    WDBs = [B.scratch(f"WDB{l}", [KC, 128, NFC, 128], BF16) for l in range(2)]
    WQB = B.scratch("WQB", [128, KC, 3 * D], BF16)
    WOB = B.scratch("WOB", [128, KC, D], BF16)

    def precast_weights():
        for l in range(2):
            for k in range(KC):
                dma("pool", WUB[l].ap[:, k, :], w_up.ap[l, k * 128:(k + 1) * 128, :], reads=[w_up], writes=[WUB[l]],
                    max_dma_last_dim=2048)
            for oc in range(KC):
                dma("pool", WDBs[l].ap[oc], w_down.ap[l, :, oc * 128:(oc + 1) * 128].rearrange("(j p) n -> p j n", p=128),
                    reads=[w_down], writes=[WDBs[l]], max_dma_last_dim=2048)
        for k in range(KC):
            dma("pool", WQB.ap[:, k, :], w_qkv1.ap[k * 128:(k + 1) * 128, :], reads=[w_qkv1], writes=[WQB], max_dma_last_dim=2048)
            dma("pool", WOB.ap[:, k, :], w_out1.ap[k * 128:(k + 1) * 128, :], reads=[w_out1], writes=[WOB], max_dma_last_dim=2048)

    ones_bf = B.ptile([128], BF16, "ones")
    blk_bf = B.ptile([128], BF16, "blk")
    ident_bf = B.ptile([128], BF16, "ident")
    mod = B.ptile([2, 48, 2], F32, "mod")
    A12 = B.ptile([2, 2, KC, 2], F32, "A12")
    ng = B.ptile([2, 2, KC], F32, "ng")
    fg = B.ptile([KC], F32, "fg")
    lamt = B.ptile([4], F32, "lamt")
    gsub = B.ptile([128], F32, "gsub")
    qkg = B.ptile([4], F32, "qkg")
    cw = B.ptile([2, 3, 44], F32, "cw")
    cb = B.ptile([2, 44], F32, "cb")
    rfe = B.ptile([(TP // 128) * NDELTA * 2], F32, "rfe")
    rfs = B.ptile([(TS // 128) * NDELTA * 2], F32, "rfs")
    P.rft, Sg.rft = rfe, rfs

    def mod_ap(l, ch, s):
        return mod.ap[:, l, ch, s:s + 1]

    def A_ap(l, which, c, s):
        return A12.ap[:, l, which, c, s:s + 1]

    op("dve", "memset", ones_bf.ap, 1.0, writes=[ones_bf])
    op("dve", "memset", blk_bf.ap, 0.0, writes=[blk_bf])
    op("dve", "memset", blk_bf.ap[0:64, 0:64], 1.0, writes=[blk_bf])
    op("dve", "memset", blk_bf.ap[64:128, 64:128], 1.0, writes=[blk_bf])
    B.castload(ident_bf, ident_bf.ap, ident_in, ident_in.ap)
    dma("sp", ng.ap, norm_gT.ap, reads=[norm_gT], writes=[ng])
    dma("sp", fg.ap, final_gT.ap, reads=[final_gT], writes=[fg])
    dma("sp", qkg.ap, qkg_in.ap, reads=[qkg_in], writes=[qkg])
    dma("sp", cw.ap, conv_wT.ap, reads=[conv_wT], writes=[cw])
    dma("sp", cb.ap, conv_bT.ap, reads=[conv_bT], writes=[cb])
    dma("sp", rfe.ap, rf_e.ap, reads=[rf_e], writes=[rfe])
    dma("sp", rfs.ap, rf_s.ap, reads=[rf_s], writes=[rfs])
    dma("sp", gsub.ap, subln_in.ap, reads=[subln_in], writes=[gsub])
    op("dve", "tensor_scalar", gsub.ap, gsub.ap, 1.0 - LAMBDA_INIT0, None, ALU.mult, reads=[gsub], writes=[gsub])

    lamw = B.tile([256], F32, "lamw")
    lamp = B.tile([2, 64], F32, "lamp")
    lams = B.tile([2], F32, "lams")
    dma("sp", lamw.ap, lam_in.ap, reads=[lam_in], writes=[lamw])
    lv = lamw.ap.rearrange("p (a b) -> p a b", a=4)
    op("dve", "tensor_tensor", lamp.ap[:, 0, :], lv[:, 0, :], lv[:, 1, :], ALU.mult, reads=[lamw], writes=[lamp])
    op("dve", "tensor_tensor", lamp.ap[:, 1, :], lv[:, 2, :], lv[:, 3, :], ALU.mult, reads=[lamw], writes=[lamp])
    op("dve", "tensor_reduce", lams.ap, lamp.ap, mybir.AxisListType.X, ALU.add, reads=[lamp], writes=[lams])
    op("act", "activation", lams.ap, lams.ap, AF.Exp, reads=[lams], writes=[lams])
    op("dve", "tensor_tensor", lamt.ap[:, 0:1], lams.ap[:, 1:2], lams.ap[:, 0:1], ALU.subtract, reads=[lams], writes=[lamt])
    op("dve", "tensor_scalar", lamt.ap[:, 0:1], lamt.ap[:, 0:1], -LAMBDA_INIT0, None, ALU.add, reads=[lamt], writes=[lamt])

    cact = B.tile([KC, 2], F32, "cact")
    dma("sp", cact.ap, cT.ap, reads=[cT], writes=[cact])
    op("act", "activation", cact.ap, cact.ap, AF.Silu, reads=[cact], writes=[cact])
    abt = B.tile([2, 48], F32, "abt")
    dma("sp", abt.ap, ada_bT.ap, reads=[ada_bT], writes=[abt])
    awb = [B.tile([KC, 512], F32, f"aw{i}") for i in range(2)]
    it = 0
    for l in range(2):
        pb = B.bank[l]
        for g in range(12):
            w = awb[it % 2]
            it += 1
            dma("sp", w.ap, ada_w.ap[l, :, g * 512:(g + 1) * 512].rearrange("(k p) n -> p k n", p=128),
                reads=[ada_w], writes=[w])
            for oc in range(4):
                ch = g * 4 + oc
                for k in range(KC):
                    mm(pb.ap[:, ch * 2:ch * 2 + 2], w.ap[:, k, oc * 128:(oc + 1) * 128], cact.ap[:, k, :],
                       k == 0, k == KC - 1, reads=[w, cact], writes=[pb])
        op("dve", "tensor_tensor", mod.ap[:, l], pb.ap[:, 0:96].rearrange("p (a b) -> p a b", b=2),
           abt.ap[:, l, :].unsqueeze(2).broadcast_to([128, 48, 2]),
           ALU.add, reads=[pb, abt], writes=[mod])
        for which in range(2):
            sc0 = 8 + 24 * which
            op("dve", "tensor_tensor", A12.ap[:, l, which], mod.ap[:, l, sc0:sc0 + 8, :],
               ng.ap[:, l, which, :].unsqueeze(2).broadcast_to([128, KC, 2]),
               ALU.mult, reads=[mod, ng], writes=[A12])
            op("dve", "tensor_tensor", A12.ap[:, l, which], A12.ap[:, l, which],
               ng.ap[:, l, which, :].unsqueeze(2).broadcast_to([128, KC, 2]),
               ALU.add, reads=[A12, ng], writes=[A12])

    GEN = [0, 1, 2, 3, 4, 5, 6, 7]

    def norm_A(src, lo, W, xT, sqb):
        dma("sp", xT.ap[:, :, 0:W], src.ap[:, :, lo:lo + W].rearrange("c p t -> p c t"), reads=[src], writes=[xT])
        h_ = KC // 2
        op("pool", "tensor_tensor", sqb.ap[:, 0:h_, 0:W], xT.ap[:, 0:h_, 0:W], xT.ap[:, 0:h_, 0:W], ALU.mult,
           reads=[xT], writes=[sqb])
        op("act", "activation", sqb.ap[:, h_:KC, 0:W], xT.ap[:, h_:KC, 0:W], AF.Square, reads=[xT], writes=[sqb])

    def norm_B(W, sqb, rt, bank=None):
        pb = B.nbank(GEN) if bank is None else B.bank[bank]
        for c in range(KC):
            mm(pb.ap[:, 0:W], ones_bf.ap, sqb.ap[:, c, 0:W], c == 0, c == KC - 1, reads=[ones_bf, sqb], writes=[pb])
        op("act", "activation", rt.ap[:, 0:W], pb.ap[:, 0:W], AF.Sqrt, bias=EPS, scale=1.0 / D, reads=[pb], writes=[rt])

    def norm_C(W, rt):
        op("dve", "reciprocal", rt.ap[:, 0:W], rt.ap[:, 0:W], reads=[rt], writes=[rt])

    def norm_D(c, W, l, which, s, xT, hT, hcol0, tmps, rt, all_act=False):
        tm = tmps[c % 2]
        op("dve", "tensor_tensor", tm.ap[:, 0:W], xT.ap[:, c, 0:W], rt.ap[:, 0:W], ALU.mult, reads=[xT, rt], writes=[tm])
        if c % 2 == 0 or all_act:
            op("act", "activation", hT.ap[:, c, hcol0:hcol0 + W], tm.ap[:, 0:W], AF.Identity,
               bias=mod_ap(l, 24 * which + c, s), scale=A_ap(l, which, c, s), reads=[tm, mod, A12], writes=[hT])
        else:
            op("pool", "tensor_scalar", hT.ap[:, c, hcol0:hcol0 + W], tm.ap[:, 0:W],
               A_ap(l, which, c, s), mod_ap(l, 24 * which + c, s), ALU.mult, ALU.add,
               reads=[tm, mod, A12], writes=[hT])

    def norm_tile(src, lo, W, l, which, s, xT, hT, hcol0, sqb, tmps, rt):
        norm_A(src, lo, W, xT, sqb)
        norm_B(W, sqb, rt)
        norm_C(W, rt)
        for c in range(KC):
            norm_D(c, W, l, which, s, xT, hT, hcol0, tmps, rt)

    class NormPipe:
        def __init__(self, specs, xTs, hTs, sqbs, tmps, rts):
            self.specs, self.xTs, self.hTs, self.sqbs, self.tmps, self.rts = specs, xTs, hTs, sqbs, tmps, rts
            self.done = {}

        def bufs(self, i):
            return (self.xTs[i % len(self.xTs)], self.hTs[i % len(self.hTs)], self.sqbs[i % len(self.sqbs)],
                    self.rts[i % len(self.rts)])

        def step(self, i, part):
            if i < 0 or i >= len(self.specs) or (i, part) in self.done:
                return
            self.done[(i, part)] = True
            sp = self.specs[i]
            xT, hT, sqb, rt = self.bufs(i)
            W = sp["W"]
            xv = xT
            if part == "A":
                norm_A(sp["src"], sp["lo"], W, xv, sqb)
            elif part == "B":
                norm_B(W, sqb, rt, sp.get("bank"))
            elif part == "C":
                norm_C(W, rt)
            elif part == "P":
                if sp.get("post"):
                    sp["post"](xT, hT)
            else:
                norm_D(part, W, sp["l"], sp["which"], sp["s"], xv, hT, sp["hcol0"], self.tmps, rt, sp.get("all_act", False))

        def upto(self, i, parts):
            for p_ in parts:
                self.step(i, p_)

        def all(self, i):
            self.upto(i, ["A", "B", "C"] + list(range(KC)) + ["P"])

    def load_w(dst, src, row0, col0, ncols, nk=KC):
        for c0 in range(0, ncols, 512):
            c1 = min(c0 + 512, ncols)
            for k0 in range(0, nk, 4):
                k1 = min(k0 + 4, nk)
                B.castload(dst, dst.ap[:, k0:k1, c0:c1],
                           src, src.ap[row0 + k0 * 128:row0 + k1 * 128, col0 + c0:col0 + c1].rearrange("(k p) n -> p k n", p=128))

    def proj_fm(hT, hcol0, W, wsb, wc0, pb, nk=KC):
        for k in range(nk):
            mm(pb.ap[:, 0:W], wsb.ap[:, k, wc0:wc0 + 128], hT.ap[:, k, hcol0:hcol0 + W], k == 0, k == nk - 1,
               reads=[wsb, hT], writes=[pb])

    def phase_C():
        B.new_phase()
        wk = B.tile([KC, 640], BF16, "wk")
        wks = B.tile([KC, 640], BF16, "wks")
        wv = B.tile([KC, 640], BF16, "wv")
        load_w(wk, w_in, 0, 512, 512)
        load_w(wks, w_in_sw, 0, 512, 512)
        for (dst, src) in ((wk, w_in), (wks, w_in_sw)):
            B.castload(dst, dst.ap[:, :, 512:640], src, src.ap[:, 2048:2176].rearrange("(k p) n -> p k n", p=128))
        load_w(wv, w_in, 0, 1024, 512)
        B.castload(wv, wv.ap[:, :, 512:640], w_in, w_in.ap[:, 2176:2304].rearrange("(k p) n -> p k n", p=128))
        xTs = [B.tile([KC, 512], F32, f"xT{i}") for i in range(2)]
        hTs = [B.tile([KC, 512], BF16, f"hT{i}") for i in range(2)]
        sqb = B.tile([KC, 512], BF16, "sqb")
        tmps = [B.tile([512], F32, f"ntmp{i}") for i in range(2)]
        rt = B.tile([512], F32, "rt")
        rts = [B.tile([4, 512], F32, f"rope{i}") for i in range(2)]
        t1 = [B.tile([512], F32, f"t1_{i}") for i in range(2)]
        t2 = [B.tile([512], F32, f"t2_{i}") for i in range(2)]
        kts = [B.tile([512], BF16, f"kt{i}") for i in range(3)]
        sqk = B.tile([512], BF16, "sqk")
        rk = B.tile([512], F32, "rk")
        vd = [B.tile([4, 4, 130], BF16, f"vd{i}") for i in range(2)]
        vg = [B.tile([2, 4, 66], BF16, f"vg{i}") for i in range(2)]
        for i in range(2):
            op("dve", "memset", vd[i].ap[:, :, :, 128:130], 1.0, writes=[vd[i]])
            op("dve", "memset", vg[i].ap[:, :, :, 64:66], 1.0, writes=[vg[i]])
        it = 0
        kti = 0
        tiles = [(g, t0) for g in segs for t0 in range(0, g.LK, 512)]
        xTs.append(B.tile([KC, 512], F32, "xT2"))
        sqbs = [sqb, B.tile([KC, 512], BF16, "sqb2")]
        rtsn = [rt, B.tile([512], F32, "rt2")]
        NP = NormPipe([dict(src=g_.xkv, lo=t_, W=512, l=0, which=0, s=g_.si, hcol0=0, all_act=True) for g_, t_ in tiles],
                      xTs, hTs, sqbs, tmps, rtsn)
        NP.all(0)
        NP.step(1, "A")
        for g, t0 in tiles:
            if True:
                W = 512
                hT, ro = hTs[it % 2], rts[it % 2]
                NP.step(it + 2, "A")
                sched = {0: ["B", "C"], 1: [0, 1], 2: [2, 3], 3: [4, 5], 4: [6, 7, "P"], 5: [], 6: [], 7: [], 8: []}
                dma("sp", ro.ap, g.rt_kv.ap[:, :, t0:t0 + W], reads=[g.rt_kv], writes=[ro])
                for h in range(5):
                    pa, pb_ = B.nbank(GEN), B.nbank(GEN)
                    proj_fm(hT, 0, W, wk, h * 128, pa)
                    proj_fm(hT, 0, W, wks, h * 128, pb_)
                    a, b_ = t1[h % 2], t2[h % 2]
                    kt = kts[kti % 3]
                    kti += 1
                    if h < 4:
                        op("dve", "tensor_tensor", a.ap, pa.ap, ro.ap[:, 0], ALU.mult, reads=[pa, ro], writes=[a])
                        op("dve", "tensor_tensor", b_.ap, pb_.ap, ro.ap[:, 1], ALU.mult, reads=[pb_, ro], writes=[b_])
                        op("pool", "tensor_tensor", kt.ap, a.ap, b_.ap, ALU.add, reads=[a, b_], writes=[kt])
                        dma("sp", g.KT0d.ap[h, :, t0:t0 + W], kt.ap, reads=[kt], writes=[g.KT0d])
                    else:
                        op("act", "activation", sqk.ap, pa.ap, AF.Square, reads=[pa], writes=[sqk])
                        pc = B.nbank(GEN)
                        mm(pc.ap, blk_bf.ap, sqk.ap, True, True, reads=[blk_bf, sqk], writes=[pc])
                        op("act", "activation", rk.ap, pc.ap, AF.Sqrt, bias=EPS, scale=1.0 / 64, reads=[pc], writes=[rk])
                        op("dve", "reciprocal", rk.ap, rk.ap, reads=[rk], writes=[rk])
                        op("dve", "scalar_tensor_tensor", a.ap, pa.ap, qkg.ap[:, 2:3], ro.ap[:, 2], ALU.mult, ALU.mult,
                           reads=[pa, ro, qkg], writes=[a])
                        op("dve", "scalar_tensor_tensor", b_.ap, pb_.ap, qkg.ap[:, 3:4], ro.ap[:, 3], ALU.mult, ALU.mult,
                           reads=[pb_, ro, qkg], writes=[b_])
                        op("pool", "tensor_tensor", a.ap, a.ap, b_.ap, ALU.add, reads=[a, b_], writes=[a])
                        op("dve", "tensor_tensor", kt.ap, a.ap, rk.ap, ALU.mult, reads=[a, rk], writes=[kt])
                        for gi in range(2):
                            for dup in range(2):
                                dma("sp", g.KT0g.ap[gi, dup * 64:(dup + 1) * 64, t0:t0 + W],
                                    kt.ap[gi * 64:(gi + 1) * 64, :], reads=[kt], writes=[g.KT0g])
                    NP.upto(it + 1, sched[h])
                vdt, vgt = vd[it % 2], vg[it % 2]
                for j in range(4):
                    pa, pb_ = B.nbank(GEN), B.nbank(GEN)
                    for k in range(KC):
                        mm(pa.ap, hT.ap[:, k, j * 128:(j + 1) * 128], wv.ap[:, k, 0:512], k == 0, k == KC - 1,
                           reads=[hT, wv], writes=[pa])
                    for k in range(KC):
                        mm(pb_.ap[:, 0:128], hT.ap[:, k, j * 128:(j + 1) * 128], wv.ap[:, k, 512:640], k == 0, k == KC - 1,
                           reads=[hT, wv], writes=[pb_])
                    op("act", "activation", vdt.ap[:, :, j, 0:128], pa.ap.rearrange("p (h e) -> p h e", h=4), AF.Copy,
                       reads=[pa], writes=[vdt])
                    op("dve", "tensor_copy", vgt.ap[:, :, j, 0:64], pb_.ap[:, 0:128].rearrange("p (h e) -> p h e", h=2),
                       reads=[pb_], writes=[vgt])
                    NP.upto(it + 1, sched[5 + j])
                NP.all(it + 1)
                kt0 = t0 // 128
                dma("sp", g.V0d.ap[:, :, kt0:kt0 + 4, :].rearrange("h p j e -> p h j e"), vdt.ap, reads=[vdt], writes=[g.V0d])
                dma("sp", g.V0g.ap[:, :, kt0:kt0 + 4, :].rearrange("h p j e -> p h j e"), vgt.ap, reads=[vgt], writes=[g.V0g])
                it += 1

    def phase_D():
        B.new_phase()
        wq = B.tile([KC, 1024], BF16, "wq")
        wqs = B.tile([KC, 1024], BF16, "wqs")
        wo = B.tile([KC, 1024], BF16, "wo")
        for (dst, src) in ((wq, w_in), (wqs, w_in_sw)):
            load_w(dst, src, 0, 0, 512)
            for c0 in (0, 512):
                pass
            for k0 in (0, 4):
                B.castload(dst, dst.ap[:, k0:k0 + 4, 512:1024], src,
                           src.ap[k0 * 128:(k0 + 4) * 128, 1536:2048].rearrange("(k p) n -> p k n", p=128))
        load_w(wo, w_out0, 0, 0, 1024)
        xTs = [B.tile([KC, 512], F32, f"xT{i}") for i in range(2)]
        hT = B.tile([KC, 512], BF16, "hT")
        sqb = B.tile([KC, 512], BF16, "sqb")
        tmps = [B.tile([512], F32, f"ntmp{i}") for i in range(2)]
        rt = B.tile([512], F32, "rt")
        ro = B.tile([4, 512], F32, "rope")
        dtiles = [(g_, t_) for g_ in segs for t_ in range(0, g_.T, 512)]
        NPD = NormPipe([dict(src=g_.xq, lo=t_, W=min(512, g_.T - t_), l=0, which=0, s=g_.si, hcol0=0, bank=7)
                        for g_, t_ in dtiles], xTs, [hT], [sqb], tmps, [rt])
        NPD.all(0)
        dti = [0]
        t1 = [B.tile([512], F32, f"t1_{i}") for i in range(2)]
        t2 = [B.tile([512], F32, f"t2_{i}") for i in range(2)]
        sqk = B.tile([512], BF16, "sqk")
        rk = B.tile([512], F32, "rk")
        qT = B.tile([KC, 512], BF16, "qT")
        CH = 1024
        Kc = [B.tile([CH], BF16, f"Kc{i}") for i in range(3)]
        Vc = [B.tile([CH // 128, 130], BF16, f"Vc{i}") for i in range(3)]
        cnt = [0]
        did_precast = [False]
        Es = [B.tile([2, 512], BF16, f"E{i}") for i in range(3)]
        otok = B.tile([4, 1024], BF16, "otok")
        oT = B.tile([KC, 512], BF16, "oT")
        rs = B.tile([8], F32, "rs")
        r1l = B.tile([4], F32, "r1l")
        tneg = [B.tile([128], F32, f"tneg{i}") for i in range(2)]
        od = B.tile([4, 4, 128], F32, "od")
        ssq = B.tile([16], F32, "ssq")
        junk = B.tile([128], F32, "junk")
        xo = [B.tile([512], F32, f"xo{i}") for i in range(2)]
        SB = [(0, 1), (2, 3)]
        ACCB = [4, 5, 6]

        def acc_ap(s, j, w):
            idx = s * 4 + j
            bnk, r = idx // 3, idx % 3
            return B.bank[ACCB[bnk]], B.bank[ACCB[bnk]].ap[:, r * 132:r * 132 + w], r == 0

        ci = 0
        ei = 0
        for g in segs:
            for t0 in range(0, g.T, 512):
                W = min(512, g.T - t0)
                nq = W // 128
                ti_ = dti[0]
                dti[0] += 1
                xT = xTs[ti_ % 2]
                NPD.all(ti_)
                dma("sp", ro.ap[:, :, 0:W], g.rt_q.ap[:, :, t0:t0 + W], reads=[g.rt_q], writes=[ro])
                for c in range(8):
                    pa, pb_ = B.bank[7], B.bank[c % 2 + 4]
                    pa = B.bank[7] if c % 2 == 0 else B.bank[6]
                    proj_fm(hT, 0, W, wq, c * 128, pa)
                    proj_fm(hT, 0, W, wqs, c * 128, pb_)
                    a, b_ = t1[c % 2], t2[c % 2]
                    if c < 4:
                        op("dve", "tensor_tensor", a.ap[:, 0:W], pa.ap[:, 0:W], ro.ap[:, 0, 0:W], ALU.mult, reads=[pa, ro], writes=[a])
                        op("dve", "tensor_tensor", b_.ap[:, 0:W], pb_.ap[:, 0:W], ro.ap[:, 1, 0:W], ALU.mult, reads=[pb_, ro], writes=[b_])
                        op("pool", "tensor_tensor", qT.ap[:, c, 0:W], a.ap[:, 0:W], b_.ap[:, 0:W], ALU.add, reads=[a, b_], writes=[qT])
                    else:
                        op("act", "activation", sqk.ap[:, 0:W], pa.ap[:, 0:W], AF.Square, reads=[pa], writes=[sqk])
                        pc = B.bank[3]
                        mm(pc.ap[:, 0:W], blk_bf.ap, sqk.ap[:, 0:W], True, True, reads=[blk_bf, sqk], writes=[pc])
                        op("act", "activation", rk.ap[:, 0:W], pc.ap[:, 0:W], AF.Sqrt, bias=EPS, scale=1.0 / 64, reads=[pc], writes=[rk])
                        op("dve", "reciprocal", rk.ap[:, 0:W], rk.ap[:, 0:W], reads=[rk], writes=[rk])
                        op("dve", "scalar_tensor_tensor", a.ap[:, 0:W], pa.ap[:, 0:W], qkg.ap[:, 0:1], ro.ap[:, 2, 0:W], ALU.mult, ALU.mult,
                           reads=[pa, ro, qkg], writes=[a])
                        op("dve", "scalar_tensor_tensor", b_.ap[:, 0:W], pb_.ap[:, 0:W], qkg.ap[:, 1:2], ro.ap[:, 3, 0:W], ALU.mult, ALU.mult,
                           reads=[pb_, ro, qkg], writes=[b_])
                        op("pool", "tensor_tensor", a.ap[:, 0:W], a.ap[:, 0:W], b_.ap[:, 0:W], ALU.add, reads=[a, b_], writes=[a])
                        op("dve", "tensor_tensor", qT.ap[:, c, 0:W], a.ap[:, 0:W], rk.ap[:, 0:W], ALU.mult, reads=[a, rk], writes=[qT])
                if not did_precast[0]:
                    did_precast[0] = True
                    precast_weights()
                chunks = []
                items = []
                for u in range(8):
                    for k0 in range(0, g.LK, CH):
                        nk = min(CH, g.LK - k0)
                        chunks.append((u, k0, nk))
                        for kt in range(nk // 128):
                            items.append((len(chunks) - 1, u, k0, kt))
                cbuf = {}

                def load_chunk(cidx):
                    if cidx in cbuf or cidx >= len(chunks):
                        return
                    u, k0, nk = chunks[cidx]
                    kc_, vc_ = Kc[cnt[0] % 3], Vc[cnt[0] % 3]
                    cnt[0] += 1
                    cbuf[cidx] = (kc_, vc_)
                    if u < 4:
                        dma("sp", kc_.ap[:, 0:nk], g.KT0d.ap[u, :, k0:k0 + nk], reads=[g.KT0d], writes=[kc_])
                        dma("sp", vc_.ap[:, 0:nk // 128, :], g.V0d.ap[u, :, k0 // 128:(k0 + nk) // 128, :], reads=[g.V0d], writes=[vc_])
                    else:
                        gi = (u - 4) // 2
                        dma("sp", kc_.ap[:, 0:nk], g.KT0g.ap[gi, :, k0:k0 + nk], reads=[g.KT0g], writes=[kc_])
                        dma("sp", vc_.ap[:, 0:nk // 128, 0:66], g.V0g.ap[gi, :, k0 // 128:(k0 + nk) // 128, :], reads=[g.V0g], writes=[vc_])

                def stage_qk(i):
                    cidx, u, k0, kt = items[i]
                    if kt == 0:
                        load_chunk(cidx)
                        load_chunk(cidx + 1)
                    kc_, vc_ = cbuf[cidx]
                    b0, b1 = SB[i % 2]
                    E = Es[i % 3]
                    mm(B.bank[b0].ap[:, 0:W], kc_.ap[0:64, kt * 128:(kt + 1) * 128], qT.ap[0:64, u, 0:W], True, True,
                       reads=[kc_, qT], writes=[B.bank[b0]])
                    mm(B.bank[b1].ap[:, 0:W], kc_.ap[64:128, kt * 128:(kt + 1) * 128], qT.ap[64:128, u, 0:W], True, True,
                       reads=[kc_, qT], writes=[B.bank[b1]])
                    sin = B.ps_all[:, b0 * 512:(b0 + 2) * 512].rearrange("p (a b) -> p a b", a=2)[:, :, 0:W]
                    op("act", "activation", E.ap[:, :, 0:W], sin, AF.Exp, scale=SCALE,
                       reads=[B.bank[b0], B.bank[b1]], writes=[E])

                def stage_pv(i):
                    cidx, u, k0, kt = items[i]
                    kc_, vc_ = cbuf[cidx]
                    E = Es[i % 3]
                    diff = u < 4
                    vw = 129 if diff else 65
                    if k0 == 0 and kt == 0:
                        started.clear()
                    last = (k0 + (kt + 1) * 128 == g.LK)
                    for s in range(2):
                        for j in range(nq):
                            bk, ap_, _ = acc_ap(s, j, vw)
                            st_ = id(bk) not in started
                            started.add(id(bk))
                            mm(ap_, E.ap[:, s, j * 128:(j + 1) * 128], vc_.ap[:, kt, 0:vw], st_, last,
                               reads=[E, vc_], writes=[bk], skip_group_check=True)
                    if last:
                        finalize(u)

                def finalize(u):
                    diff = u < 4
                    vw = 129 if diff else 65
                    for bnk in range(3):
                        n_in = 3 if bnk < 2 else 2
                        src = B.bank[ACCB[bnk]].ap[:, 0:n_in * 132].rearrange("p (r e) -> p r e", e=132)[:, :, vw - 1:vw]
                        op("dve", "reciprocal", rs.ap[:, bnk * 3:bnk * 3 + n_in].rearrange("p (r e) -> p r e", e=1), src,
                           reads=[B.bank[ACCB[bnk]]], writes=[rs])
                    if diff:
                        op("dve", "tensor_scalar", r1l.ap, rs.ap[:, 4:8], lamt.ap[:, 0:1], None, ALU.mult, reads=[rs, lamt], writes=[r1l])
                        for j in range(nq):
                            bk0, a0, _ = acc_ap(0, j, 128)
                            bk1, a1, _ = acc_ap(1, j, 128)
                            tn = tneg[j % 2]
                            op("dve", "tensor_scalar", tn.ap, a1, r1l.ap[:, j:j + 1], None, ALU.mult, reads=[bk1, r1l], writes=[tn])
                            op("dve", "scalar_tensor_tensor", od.ap[:, u, j], a0, rs.ap[:, j:j + 1], tn.ap, ALU.mult, ALU.add,
                               reads=[bk0, rs, tn], writes=[od])
                        for j in range(nq):
                            op("dve", "scalar_tensor_tensor", junk.ap, od.ap[:, u, j], 1.0, od.ap[:, u, j], ALU.mult, ALU.mult,
                               accum_out=ssq.ap[:, u * 4 + j:u * 4 + j + 1], reads=[od], writes=[junk, ssq])
                    else:
                        for j in range(nq):
                            for s in range(2):
                                bk, a_, _ = acc_ap(s, j, 64)
                                c0 = 512 + (u - 4) * 128 + s * 64
                                op("dve", "tensor_scalar", otok.ap[:, j, c0:c0 + 64], a_, rs.ap[:, s * 4 + j:s * 4 + j + 1], None, ALU.mult,
                                   reads=[bk, rs], writes=[otok])

                started = set()
                nit = len(items)
                side = {int(nit * f): p_ for f, p_ in zip((0.10, 0.25, 0.35, 0.45, 0.50, 0.55, 0.60, 0.65, 0.70, 0.75, 0.80),
                                                           ("A", "B", "C", 0, 1, 2, 3, 4, 5, 6, 7))}
                for i in range(nit + 2):
                    if i < nit:
                        stage_qk(i)
                    if i >= 2:
                        stage_pv(i - 2)
                    if i in side:
                        NPD.step(ti_ + 1, side[i])
                op("act", "activation", ssq.ap, ssq.ap, AF.Sqrt, bias=SUBLN_EPS, scale=1.0 / 128, reads=[ssq], writes=[ssq])
                op("dve", "reciprocal", ssq.ap, ssq.ap, reads=[ssq], writes=[ssq])
                for u in range(4):
                    for j in range(nq):
                        op("dve", "scalar_tensor_tensor", otok.ap[:, j, u * 128:(u + 1) * 128], od.ap[:, u, j],
                           ssq.ap[:, u * 4 + j:u * 4 + j + 1], gsub.ap, ALU.mult, ALU.mult, reads=[od, ssq, gsub], writes=[otok])
                for j in range(nq):
                    pb_ = B.bank[7]
                    pbv = pb_.ap.bitcast(BF16)
                    for c in range(KC):
                        op("pe", "transpose", pbv[:, c * 128:(c + 1) * 128], otok.ap[:, j, c * 128:(c + 1) * 128], ident_bf.ap,
                           reads=[otok, ident_bf], writes=[pb_])
                    op("dve" if j % 2 else "act", "tensor_copy" if j % 2 else "copy", oT.ap[:, :, j * 128:(j + 1) * 128],
                       pbv.rearrange("p (c q) -> p c q", c=KC), reads=[pb_], writes=[oT])
                for oc in range(KC):
                    pb_ = B.bank[oc % 2]
                    proj_fm(oT, 0, W, wo, oc * 128, pb_)
                    x_ = xo[oc % 2]
                    op("dve", "scalar_tensor_tensor", x_.ap[:, 0:W], pb_.ap[:, 0:W], mod_ap(0, 16 + oc, g.si), xT.ap[:, oc, 0:W],
                       ALU.mult, ALU.add, reads=[pb_, mod, xT], writes=[x_])
                    dma("pool", g.X1a.ap[oc, :, t0:t0 + W], x_.ap[:, 0:W], reads=[x_], writes=[g.X1a])

    def phase_FFN(l, final):
        B.new_phase()
        wu = B.tile([KC, 2 * D_FF], BF16, "wu")
        for c0 in range(0, 2 * D_FF, 1408):
            dma("sp", wu.ap[:, :, c0:c0 + 1408], WUB[l].ap[:, :, c0:c0 + 1408], reads=[WUB[l]], writes=[wu])
        wds = [B.tile([NFC, 128], BF16, f"wd{i}") for i in range(2)]
        WDB = WDBs[l]

        def load_wd(oc, slot):
            dma("sp", wds[slot].ap, WDB.ap[oc], reads=[WDB], writes=[wds[slot]])
        xT = B.tile([KC, 512], F32, "xT")
        hT = B.tile([KC, 512], BF16, "hT")
        sqb = B.tile([KC, 512], BF16, "sqb")
        actT = B.tile([NFC, 512], BF16, "actT")
        rt = B.tile([512], F32, "rt")
        tg = [B.tile([512], F32, f"tg{i}") for i in range(2)]
        tv = [B.tile([512], F32, f"tv{i}") for i in range(2)]
        xres = [B.tile([512], F32, f"xres{i}") for i in range(2)]
        tiles = []
        for g in segs:
            src = g.X1a if l == 0 else g.X2a
            dst = g.X1 if l == 0 else g.X2
            rng = (0, g.T) if not final else g.own
            for (a, b_) in split_range(rng[0], rng[1], 510):
                tiles.append((g, src, dst, a, b_))
        specs = []
        for (g, src, dst, a, b_) in tiles:
            WO = b_ - a
            lo, hi = max(a - 1, 0), min(b_ + 1, g.T)
            c0 = lo - (a - 1)
            Wl = hi - lo

            def post(xT_, hT_, g=g, lo=lo, hi=hi, c0=c0, Wl=Wl, WO=WO, b_=b_):
                if c0 > 0:
                    op("dve", "memset", hT_.ap[:, :, 0:1], 0.0, writes=[hT_])
                if hi < b_ + 1:
                    op("dve", "memset", hT_.ap[:, :, WO + 1:WO + 2], 0.0, writes=[hT_])
                if g is P and (lo < HALO or hi > HALO + OWN):
                    vm = tv[0]
                    dma("sp", vm.ap[:, 0:Wl], vm_e.ap[:, lo:hi], reads=[vm_e], writes=[vm])
                    vb = vm.ap[:, 0:Wl].unsqueeze(1).broadcast_to([128, KC, Wl])
                    op("dve", "tensor_tensor", hT_.ap[:, :, c0:c0 + Wl], hT_.ap[:, :, c0:c0 + Wl], vb, ALU.mult,
                       reads=[hT_, vm], writes=[hT_])
            specs.append(dict(src=src, lo=lo, W=Wl, l=l, which=1, s=g.si, hcol0=c0, post=post))
        NP = NormPipe(specs, [xT], [hT], [sqb], [tg[0], tg[1]], [rt])
        NP.all(0)
        pi = 0
        for ti, (g, src, dst, a, b_) in enumerate(tiles):
            WO = b_ - a
            WI = WO + 2

            def load_xres(oc, a=a, b_=b_, src=src, WO=WO):
                dma("sp", xres[oc % 2].ap[:, 0:WO], src.ap[oc, :, a:b_], reads=[src], writes=[xres[oc % 2]])
            load_wd(0, 0)
            load_wd(1, 1)
            load_xres(0)
            load_xres(1)
            for j in range(NFC):
                pg, pv = B.bank[(pi % 2) * 2], B.bank[(pi % 2) * 2 + 1]
                g_, v_ = tg[pi % 2], tv[pi % 2]
                pi += 1
                for (pb_, tt, ch) in ((pg, g_, j), (pv, v_, NFC + j)):
                    for k in range(KC):
                        mm(pb_.ap[:, 0:WI], wu.ap[:, k, ch * 128:(ch + 1) * 128], hT.ap[:, k, 0:WI], k == 0, k == KC - 1,
                           reads=[wu, hT], writes=[pb_])
                    op("act", "activation", tt.ap[:, 0:WO], pb_.ap[:, 1:WO + 1], AF.Identity,
                       bias=cb.ap[:, l, ch:ch + 1], scale=cw.ap[:, l, 1, ch:ch + 1], reads=[pb_, cb, cw], writes=[tt])
                    op("dve", "scalar_tensor_tensor", tt.ap[:, 0:WO], pb_.ap[:, 0:WO], cw.ap[:, l, 0, ch:ch + 1], tt.ap[:, 0:WO],
                       ALU.mult, ALU.add, reads=[pb_, cw, tt], writes=[tt])
                    op("dve", "scalar_tensor_tensor", tt.ap[:, 0:WO], pb_.ap[:, 2:WO + 2], cw.ap[:, l, 2, ch:ch + 1], tt.ap[:, 0:WO],
                       ALU.mult, ALU.add, reads=[pb_, cw, tt], writes=[tt])
                op("act", "activation", g_.ap[:, 0:WO], g_.ap[:, 0:WO], AF.Silu, reads=[g_], writes=[g_])
                op("pool", "tensor_tensor", actT.ap[:, j, 0:WO], g_.ap[:, 0:WO], v_.ap[:, 0:WO], ALU.mult,
                   reads=[g_, v_], writes=[actT])
                if j == 1:
                    NP.step(ti + 1, "A")
                if j == 10:
                    NP.step(ti + 1, "B")
                if j == 13:
                    NP.step(ti + 1, "C")
            for oc in range(KC):
                wd = wds[oc % 2]
                pb_ = B.bank[4 + oc % 4]
                for j in range(NFC):
                    mm(pb_.ap[:, 0:WO], wd.ap[:, j, :], actT.ap[:, j, 0:WO], j == 0, j == NFC - 1,
                       reads=[wd, actT], writes=[pb_])
                xr = xres[oc % 2]
                op("dve", "scalar_tensor_tensor", xr.ap[:, 0:WO], pb_.ap[:, 0:WO], mod_ap(l, 40 + oc, g.si),
                   xr.ap[:, 0:WO], ALU.mult, ALU.add, reads=[pb_, mod, xr], writes=[xr])
                dma("pool", dst.ap[oc, :, a:b_], xr.ap[:, 0:WO], reads=[xr], writes=[dst])
                if oc + 2 < KC:
                    load_wd(oc + 2, oc % 2)
                    load_xres(oc + 2)
                if oc < 4:
                    NP.step(ti + 1, 2 * oc)
                    NP.step(ti + 1, 2 * oc + 1)
                if oc == 4:
                    NP.step(ti + 1, "P")
            NP.step(ti + 1, "P")

    def phase_I():
        B.new_phase()
        xTs = [B.tile([KC, 512], F32, f"xT{i}") for i in range(2)]
        sqbs = [B.tile([KC, 512], BF16, f"sqb{i}") for i in range(2)]
        yTs = [B.tile([KC, 512], F32, f"yT{i}") for i in range(2)]
        rt = B.tile([512], F32, "rt")
        tiles = [(g, a, min(a + 512, g.own[1])) for g in segs for a in range(g.own[0], g.own[1], 512)]

        def partA(i):
            if i < len(tiles):
                g_, a_, b2 = tiles[i]
                norm_A(g_.X2, a_, b2 - a_, xTs[i % 2], sqbs[i % 2])
        partA(0)
        for i, (g, a, b_) in enumerate(tiles):
            W = b_ - a
            partA(i + 1)
            norm_B(W, sqbs[i % 2], rt)
            norm_C(W, rt)
            yT = yTs[i % 2]
            for c in range(KC):
                op("dve", "scalar_tensor_tensor", yT.ap[:, c, 0:W], xTs[i % 2].ap[:, c, 0:W], fg.ap[:, c:c + 1],
                   rt.ap[:, 0:W], ALU.mult, ALU.mult, reads=[xTs[i % 2], fg, rt], writes=[yT])
            dma("pool", g.y.ap[:, :, a - g.own[0]:b_ - g.own[0]].rearrange("c p t -> p c t"), yT.ap[:, :, 0:W],
                reads=[yT], writes=[g.y])

    def phase_F():
        B.new_phase()
        wqk = B.tile([KC, 2048], BF16, "wqk")
        wv = B.tile([KC, 1024], BF16, "wv1")
        for c0 in range(0, 2048, 1024):
            dma("sp", wqk.ap[:, :, c0:c0 + 1024], WQB.ap[:, :, c0:c0 + 1024], reads=[WQB], writes=[wqk])
        dma("sp", wv.ap, WQB.ap[:, :, 2048:3072], reads=[WQB], writes=[wv])
        xTs = [B.tile([KC, 512], F32, f"xT{i}") for i in range(3)]
        hTs = [B.tile([KC, 512], BF16, f"hT{i}") for i in range(2)]
        sqbs = [B.tile([KC, 512], BF16, "sqb0")]
        tmps = [B.tile([512], F32, f"ntmp{i}") for i in range(2)]
        rtsn = [B.tile([512], F32, "rt0")]
        qk = [B.tile([KC, 512], BF16, f"qk{i}") for i in range(2)]
        v1 = [B.tile([4, 16, 66], BF16, f"v1_{i}") for i in range(2)]
        for i in range(2):
            op("dve", "memset", v1[i].ap[:, :, :, 64:66], 1.0, writes=[v1[i]])
        tiles = [(g, t0) for g in segs for t0 in range(0, g.T, 512)]
        NP = NormPipe([dict(src=g_.X1, lo=t_, W=min(512, g_.T - t_), l=1, which=0, s=g_.si, hcol0=0) for g_, t_ in tiles],
                      xTs, hTs, sqbs, tmps, rtsn)
        NP.all(0)
        NP.step(1, "A")
        it = 0
        for g, t0 in tiles:
            if True:
                W = min(512, g.T - t0)
                hT = hTs[it % 2]
                NP.step(it + 1, "B")
                NP.step(it + 2, "A")
                slot = 0
                sched = {1: ["C"], 3: [0], 4: [1], 5: [2], 6: [3], 7: [4], 8: [5], 9: [6], 10: [7, "P"]}
                for qi, dst in ((0, g.Q1T), (1, g.K1T)):
                    o_ = qk[qi]
                    for c in range(KC):
                        pb_ = B.nbank(GEN)
                        proj_fm(hT, 0, W, wqk, qi * 1024 + c * 128, pb_)
                        if c % 2 == 0:
                            op("act", "copy", o_.ap[:, c, 0:W], pb_.ap[:, 0:W], reads=[pb_], writes=[o_])
                        else:
                            op("dve", "tensor_copy", o_.ap[:, c, 0:W], pb_.ap[:, 0:W], reads=[pb_], writes=[o_])
                        NP.upto(it + 1, sched.get(slot, []))
                        slot += 1
                    dma("pool", dst.ap[:, :, t0:t0 + W].rearrange("c p t -> p c t"), o_.ap[:, :, 0:W], reads=[o_], writes=[dst])
                NP.all(it + 1)
                vt = v1[it % 2]
                it += 1
                for j in range(W // 128):
                    for half in range(2):
                        pb_ = B.nbank(GEN)
                        for k in range(KC):
                            mm(pb_.ap, hT.ap[:, k, j * 128:(j + 1) * 128], wv.ap[:, k, half * 512:(half + 1) * 512], k == 0, k == KC - 1,
                               reads=[hT, wv], writes=[pb_])
                        if half == 0:
                            op("act", "activation", vt.ap[:, j, 0:8, 0:64], pb_.ap.rearrange("p (h e) -> p h e", h=8), AF.Copy,
                               reads=[pb_], writes=[vt])
                        else:
                            op("dve", "tensor_copy", vt.ap[:, j, 8:16, 0:64], pb_.ap.rearrange("p (h e) -> p h e", h=8),
                               reads=[pb_], writes=[vt])
                dma("pool", g.V1.ap[:, t0 // 128:(t0 + W) // 128], vt.ap[:, 0:W // 128], reads=[vt], writes=[g.V1])

    def phase_G():
        B.new_phase()
        wo = B.tile([KC, 1024], BF16, "wo1")
        dma("sp", wo.ap, WOB.ap, reads=[WOB], writes=[wo])
        EB = B.tile([16, NDELTA, 128], BF16, "EB")
        EBI = B.tile([16, 2, 128], BF16, "EBI")
        hks = [B.tile([16, 64], F32, "hk0")] * 2
        hkes = [B.tile([16, 64], BF16, "hke0")] * 2
        jm = B.tile([64], BF16, "jm")
        cmk = B.tile([64], F32, "cmk")
        B.castload(jm, jm.ap[0:64], jmat_in, jmat_in.ap)
        dma("sp", cmk.ap, colmask_in.ap, reads=[colmask_in], writes=[cmk])
        for h in range(16):
            hk, hke = hks[h % 2], hkes[h % 2]
            src = bass.AP(rpbpad.ap.tensor, rpbpad.ap[h].offset, [[1, 64], [128, 16], [1, 64]])
            dma("sp", hk.ap[0:64], src, reads=[rpbpad], writes=[hk])
            op("act", "activation", hke.ap[0:64], hk.ap[0:64], AF.Exp, reads=[hk], writes=[hke])
            c, hh = h // 2, h % 2
            pos = (c // 4) * 8 + hh * 4 + (c % 4)
            pb = B.bank[2 + (h % 2)]
            pb2 = B.bank[4 + (h % 2)]
            for i in range(15):
                tgt = pb if i < 8 else pb2
                j = i if i < 8 else i - 8
                mm(tgt.ap[:, j * 64:(j + 1) * 64],
                   hke.ap[0:64, i:i + 2, :].rearrange("p a b -> p (a b)"), jm.ap[0:64], True, True,
                   reads=[hke, jm], writes=[tgt])
            for b in range(2):
                for di in range(NDELTA):
                    delta = -6 + 2 * di
                    i = delta - b + 7
                    tgt = pb if i < 8 else pb2
                    j = i if i < 8 else i - 8
                    op("dve", "tensor_tensor", EB.ap[:, pos, di, b * 64:(b + 1) * 64],
                       tgt.ap[:, j * 64:(j + 1) * 64], cmk.ap, ALU.mult, reads=[tgt, cmk], writes=[EB])
        op("dve", "tensor_copy", EBI.ap[:, :, 0, :], EB.ap[:, :, 1, :], reads=[EB], writes=[EBI])
        op("dve", "tensor_copy", EBI.ap[:, :, 1, :], EB.ap[:, :, 5, :], reads=[EB], writes=[EBI])
        op("dve", "memset", EBI.ap[0:64, :, 0, 64:128], 0.0, writes=[EBI])
        op("dve", "memset", EBI.ap[:, :, 1, 0:64], 0.0, writes=[EBI])
        op("dve", "memset", EBI.ap[64:128, :, 1, 64:128], 0.0, writes=[EBI])
        NKW = 10
        qTs = [B.tile([KC, 512], BF16, f"q1_{i}") for i in range(2)]
        Kws = [B.tile([KC, NKW * 128], BF16, f"kw{i}") for i in range(2)]
        Vws = [B.tile([NKW, 16, 66], BF16, f"vw{i}") for i in range(2)]
        Es = [B.tile([1024], BF16, f"E1_{i}") for i in range(3)]
        otok = B.tile([18 * 64], BF16, "otok1")
        oT = B.tile([KC, 512], BF16, "oT1")
        rs = B.tile([18], F32, "rs1")
        xT = B.tile([KC, 512], F32, "xT1")
        xo = [B.tile([512], F32, f"xo1_{i}") for i in range(2)]
        SBK = [(0, 1), (2, 3)]
        ACCB = [4, 5, 6]
        it = 0
        ei = 0
        for g in segs:
            nkt = g.T // 128
            for t0 in range(0, g.T, 512):
                W = min(512, g.T - t0)
                nq = W // 128
                s0 = t0 // 128
                m_lo = max(s0 - 3, 0)
                m_hi = min(s0 + nq - 1 + 3, nkt - 1) + 1
                qT, Kw, Vw = qTs[it % 2], Kws[it % 2], Vws[it % 2]
                it += 1
                dma("sp", qT.ap[:, :, 0:W], g.Q1T.ap[:, :, t0:t0 + W].rearrange("c p t -> p c t"), reads=[g.Q1T], writes=[qT])
                dma("sp", Kw.ap[:, :, 0:(m_hi - m_lo) * 128], g.K1T.ap[:, :, m_lo * 128:m_hi * 128].rearrange("c p t -> p c t"),
                    reads=[g.K1T], writes=[Kw])
                dma("sp", Vw.ap[:, 0:m_hi - m_lo], g.V1.ap[:, m_lo:m_hi], reads=[g.V1], writes=[Vw])
                dma("sp", xT.ap[:, :, 0:W], g.X1.ap[:, :, t0:t0 + W].rearrange("c p t -> p c t"), reads=[g.X1], writes=[xT])
                items = []
                for jj in range(nq):
                    s = s0 + jj
                    dis = [di for di in range(NDELTA)
                           if 0 <= s + (di - 3) < nkt and (abs(di - 3) <= 2 or (di == 6 and s == g.top_s) or (di == 0 and s == g.bot_s))]
                    for n_, di in enumerate(dis):
                        for half in range(2):
                            items.append((jj, n_, di, half, len(dis)))

                def g_qk(i):
                    jj, n_, di, half, nd = items[i]
                    s = s0 + jj
                    ml = s + di - 3 - m_lo
                    b0, b1 = SBK[i % 2]
                    E = Es[i % 3]
                    for cc in range(4):
                        for hh in range(2):
                            bk = B.bank[b0 if hh == 0 else b1]
                            c = half * 4 + cc
                            mm(bk.ap[:, cc * 128:(cc + 1) * 128], Kw.ap[hh * 64:(hh + 1) * 64, c, ml * 128:(ml + 1) * 128],
                               qT.ap[hh * 64:(hh + 1) * 64, c, jj * 128:(jj + 1) * 128], True, True,
                               reads=[Kw, qT], writes=[bk])
                    op("act", "activation", E.ap, B.ps_all[:, b0 * 512:(b0 + 2) * 512], AF.Exp, scale=SCALE,
                       reads=[B.bank[b0], B.bank[b1]], writes=[E])
                    edge = (s <= g.top_s + 1) or (s >= g.bot_s - 1)
                    if not edge:
                        tbl = EBI.ap[:, half * 8:(half + 1) * 8, 0 if di == 1 else 1, :] if di in (1, 5) else \
                            EB.ap[:, half * 8:(half + 1) * 8, di, :]
                        ev = E.ap.rearrange("p (h c) -> p h c", h=8)
                        op("dve", "tensor_tensor", ev, ev, tbl, ALU.mult, reads=[E, EB, EBI], writes=[E])
                    else:
                        for b in range(2):
                            ev = E.ap.rearrange("p (h b c) -> p h b c", h=8, b=2)[:, :, b, :]
                            col = (s * NDELTA + di) * 2 + b
                            op("dve", "scalar_tensor_tensor", ev, ev, g.rft.ap[:, col:col + 1],
                               EB.ap[:, half * 8:(half + 1) * 8, di, b * 64:(b + 1) * 64], ALU.mult, ALU.mult,
                               reads=[E, g.rft, EB], writes=[E])

                def g_pv(i):
                    jj, n_, di, half, nd = items[i]
                    s = s0 + jj
                    ml = s + di - 3 - m_lo
                    E = Es[i % 3]
                    for hh in range(2):
                        for cc in range(4):
                            c = half * 4 + cc
                            h = 2 * c + hh
                            pos8 = hh * 4 + cc
                            bnk, r = h // 6, h % 6
                            bk = B.bank[ACCB[bnk]]
                            firstmm = (n_ == 0) and (h in (0, 6, 12))
                            mm(bk.ap[:, r * 66:r * 66 + 65], E.ap[:, pos8 * 128:(pos8 + 1) * 128], Vw.ap[:, ml, h, 0:65],
                               firstmm, n_ == nd - 1, reads=[E, Vw], writes=[bk], skip_group_check=True)
                    if n_ == nd - 1 and half == 1:
                        g_fin(jj)

                def g_fin(jj):
                    accs = [B.bank[b_] for b_ in ACCB]
                    av = B.ps_all[:, ACCB[0] * 512:(ACCB[0] + 3) * 512].rearrange("p (k r) -> p k r", k=3)[:, :, 0:396]
                    av = av.rearrange("p k (h e) -> p k h e", e=66)
                    rsv = rs.ap.rearrange("p (k h e) -> p k h e", k=3, e=1)
                    op("dve", "tensor_scalar", rsv, av[:, :, :, 64:65], 1e-30, None, ALU.add, reads=accs, writes=[rs])
                    op("dve", "reciprocal", rs.ap, rs.ap, reads=[rs], writes=[rs])
                    rb = rs.ap.rearrange("p (k h) -> p k h", k=3).unsqueeze(3).broadcast_to([128, 3, 6, 64])
                    op("dve", "tensor_tensor", otok.ap.rearrange("p (k h e) -> p k h e", k=3, e=64), av[:, :, :, 0:64], rb, ALU.mult,
                       reads=accs + [rs], writes=[otok])
                    pb_ = B.bank[7]
                    pbv = pb_.ap.bitcast(BF16)
                    for c in range(KC):
                        op("pe", "transpose", pbv[:, c * 128:(c + 1) * 128], otok.ap[:, c * 128:(c + 1) * 128], ident_bf.ap,
                           reads=[otok, ident_bf], writes=[pb_])
                    op("act", "copy", oT.ap[:, :, jj * 128:(jj + 1) * 128], pbv.rearrange("p (c q) -> p c q", c=KC),
                       reads=[pb_], writes=[oT])

                for i in range(len(items) + 2):
                    if i < len(items):
                        g_qk(i)
                    if i >= 2:
                        g_pv(i - 2)
                for oc in range(KC):
                    pb_ = B.bank[oc % 2]
                    proj_fm(oT, 0, W, wo, oc * 128, pb_)
                    x_ = xo[oc % 2]
                    op("dve", "scalar_tensor_tensor", x_.ap[:, 0:W], pb_.ap[:, 0:W], mod_ap(1, 16 + oc, g.si), xT.ap[:, oc, 0:W],
                       ALU.mult, ALU.add, reads=[pb_, mod, xT], writes=[x_])
                    dma("pool", g.X2a.ap[oc, :, t0:t0 + W], x_.ap[:, 0:W], reads=[x_], writes=[g.X2a])

    if "C" in phases:
        phase_C()
    if "D" in phases:
        phase_D()
    if "E" in phases:
        phase_FFN(0, False)
    if "F" in phases:
        phase_F()
    if "G" in phases:
        phase_G()
    if "H" in phases:
        phase_FFN(1, True)
        phase_I()
    S.barrier()
    S.emit()
    return B


def _fm(x):
    return np.ascontiguousarray(x.T.reshape(KC, 128, x.shape[0]))


def _rope_tables(pos_t, rows, cols):
    theta = np.float32(10000.0)
    inv1 = (theta ** (-np.arange(0, 64, 2, dtype=np.float32) / np.float32(64))).astype(np.float32)
    inv2 = (theta ** (-np.arange(0, 32, 2, dtype=np.float32) / np.float32(32))).astype(np.float32)
    ang1 = pos_t.astype(np.float32)[:, None] * inv1[None, :]
    ang2 = np.concatenate([rows.astype(np.float32)[:, None] * inv2[None, :],
                           cols.astype(np.float32)[:, None] * inv2[None, :]], axis=-1)
    out = np.zeros((128, 4, pos_t.shape[0]), np.float32)
    for k, ang in enumerate((ang1, ang2)):
        c = np.cos(ang).astype(np.float32).T
        s = np.sin(ang).astype(np.float32).T
        C = np.concatenate([c, c], 0)
        Sg = np.concatenate([-s, s], 0)
        out[:, 2 * k] = np.concatenate([C, C], 0)
        out[:, 2 * k + 1] = np.concatenate([Sg, Sg], 0)
    return out


def _rowflags(n_sub, row0_seq, n_rows_seq):
    rf = np.zeros((128, n_sub * NDELTA * 2), np.float32)
    for s in range(n_sub):
        for di in range(NDELTA):
            m = s + di - 3
            for b in range(2):
                qr = 2 * s + b + row0_seq
                for a in range(2):
                    kr = 2 * m + a + row0_seq
                    ok = 0.0
                    if 0 <= kr < n_rows_seq and 0 <= m < n_sub:
                        if 0 <= qr < n_rows_seq:
                            rs_ = min(max(qr - 4, 0), n_rows_seq - 8)
                            ok = 1.0 if rs_ <= kr < rs_ + 8 else 0.0
                    rf[a * 64:(a + 1) * 64, (s * NDELTA + di) * 2 + b] = ok
    return rf


_CACHE = {}


def _get_builder():
    if "b" not in _CACHE:
        _CACHE["b"] = build_program()
    return _CACHE["b"]


def host_inputs(inputs):
    f = lambda a: np.ascontiguousarray(np.asarray(a, dtype=np.float32))
    xp, xs = f(inputs["x_prompt"]), f(inputs["x_sample"])
    cp, cs = f(inputs["c_prompt"]), f(inputs["c_sample"])
    w_in = f(inputs["even_w_in"])[0]
    perm = np.arange(2304)
    for base, n in ((0, 8), (512, 8), (1536, 8), (2048, 2)):
        for h in range(n):
            o = base + h * 64
            perm[o:o + 32] = np.arange(o + 32, o + 64)
            perm[o + 32:o + 64] = np.arange(o, o + 32)
    w_in_sw = np.ascontiguousarray(w_in[:, perm])
    qk_g = f(inputs["gqa_qk_g"])[0]
    sw = np.concatenate([np.arange(32, 64), np.arange(0, 32)])
    qkg = np.stack([np.tile(qk_g[0], 2), np.tile(qk_g[0][sw], 2), np.tile(qk_g[1], 2), np.tile(qk_g[1][sw], 2)], 1)
    rpb = f(inputs["odd_rpb"])[0]
    rpbpad = np.full((16, 16, 128), -30000.0, np.float32)
    rpbpad[:, 0:15, 48:79] = rpb
    ada_b = f(inputs["ada_b"])
    common = {
        "ada_w": f(inputs["ada_w"]),
        "ada_bT": np.ascontiguousarray(ada_b.reshape(2, 48, 128).transpose(2, 0, 1)),
        "norm_gT": np.ascontiguousarray(f(inputs["norm_g"]).reshape(2, 2, KC, 128).transpose(3, 0, 1, 2)),
        "final_gT": np.ascontiguousarray(f(inputs["final_g"]).reshape(KC, 128).T),
        "w_in": w_in, "w_in_sw": w_in_sw, "w_out0": f(inputs["even_w_out"])[0],
        "lam_in": np.ascontiguousarray(np.broadcast_to(f(inputs["diff_lambda"])[0].reshape(1, 256), (128, 256))),
        "subln_in": np.ascontiguousarray(np.broadcast_to(f(inputs["diff_subln_g"])[0].reshape(1, 128), (128, 128))),
        "qkg_in": np.ascontiguousarray(qkg.astype(np.float32)),
        "w_qkv1": f(inputs["odd_w_qkv"])[0], "w_out1": f(inputs["odd_w_out"])[0],
        "rpbpad": rpbpad,
        "w_up": f(inputs["ffn_w_up"]),
        "conv_wT": np.ascontiguousarray(f(inputs["ffn_conv_w"]).reshape(2, 3, 44, 128).transpose(3, 0, 1, 2)),
        "conv_bT": np.ascontiguousarray(f(inputs["ffn_conv_b"]).reshape(2, 44, 128).transpose(2, 0, 1)),
        "w_down": f(inputs["ffn_w_down"]),
        "ident_in": np.eye(128, dtype=np.float32),
        "jmat_in": np.ascontiguousarray(np.eye(64, dtype=np.float32)[::-1]),
    }
    kc = np.arange(64)[:, None]
    cc = np.arange(64)[None, :]
    cs_ = np.clip(cc - 8, 0, 48)
    cm = ((kc >= cs_) & (kc < cs_ + 16)).astype(np.float32)
    common["colmask_in"] = np.concatenate([cm, cm], 0)
    t_full = np.arange(SEQ)
    rt_kv = _rope_tables(t_full, t_full // GRID_W, t_full % GRID_W)
    t_s = np.arange(TS)
    rt_s = _rope_tables(t_s, t_s // GRID_W, t_s % GRID_W)
    rf_s = _rowflags(TS // 128, 0, TS // GRID_W)
    xkv = [_fm(xp[b]) for b in range(2)]
    in_maps = []
    for c in range(8):
        b, j = c // 4, c % 4
        t0 = j * OWN - HALO
        idx = np.arange(t0, t0 + TP)
        valid = (idx >= 0) & (idx < SEQ)
        idc = np.clip(idx, 0, SEQ - 1)
        xe = np.where(valid[:, None], xp[b][idc], np.float32(0.0)).astype(np.float32)
        m = dict(common)
        m["xkvT"] = xkv[b]
        m["xeT"] = _fm(xe)
        m["xsT"] = _fm(xs[c])
        m["cT"] = np.ascontiguousarray(np.stack([cp[b], cs[c]], 1).reshape(KC, 128, 2).transpose(1, 0, 2))
        m["rt_kv"] = rt_kv
        m["rt_e"] = _rope_tables(idc, idc // GRID_W, idc % GRID_W)
        m["rt_s"] = rt_s
        m["vm_e"] = np.ascontiguousarray(np.broadcast_to(valid.astype(np.float32)[None, :], (128, TP)))
        m["rf_e"] = _rowflags(TP // 128, t0 // GRID_W, SEQ // GRID_W)
        m["rf_s"] = rf_s
        in_maps.append(m)
    return in_maps


def kernel(**inputs):
    B = _get_builder()
    in_maps = host_inputs(inputs)
    res = run_bass_kernel_spmd(B.nc, in_maps, core_ids=list(range(8)))
    yp = np.zeros((2, SEQ, D), np.float32)
    ys = np.zeros((8, DSEQ, D), np.float32)
    for c in range(8):
        r = res.results[c]
        b, j = c // 4, c % 4
        yp[b, j * OWN:(j + 1) * OWN] = np.asarray(r["yT_p"]).reshape(D, OWN).T
        ys[c] = np.asarray(r["yT_s"]).reshape(D, DSEQ).T
    return (yp, ys)
```

```python
import contextlib
import math
import numpy as np
import concourse.bass as bass
import concourse.mybir as mybir
from concourse.bass_utils import run_bass_kernel_spmd

F32 = mybir.dt.float32
BF16 = mybir.dt.bfloat16
U8 = mybir.dt.uint8
AF = mybir.ActivationFunctionType
ALU = mybir.AluOpType

D = 1024
KC = 8
SEQ = 16384
DSEQ = 2048
OWN = 4096
HALO = 384
TP = OWN + 2 * HALO
TS = DSEQ
GRID_W = 64


def set_cfg(seq, own, dseq):
    global SEQ, OWN, DSEQ, TP, TS
    SEQ, OWN, DSEQ = seq, own, dseq
    TP = OWN + 2 * HALO
    TS = DSEQ

D_FF = 2816
NFC = 22
EPS = 1e-6
SUBLN_EPS = 1e-5
LAMBDA_INIT0 = 0.8 - 0.6 * math.exp(-0.3 * 0)
SCALE = 64 ** -0.5
NDELTA = 7

EPOCH = 16000
NDSEM = {'sp': 16, 'pool': 56, 'act': 4, 'pe': 4, 'dve': 4}
ARENA = 196 * 1024
PERS = 10 * 1024


class Buf:
    __slots__ = ("name", "w", "r")

    def __init__(self, name=""):
        self.name = name
        self.w = None
        self.r = {}


class Op:
    __slots__ = ("eng", "fn", "deps", "needs_inc", "is_dma", "dsem", "dval", "inc_n")

    def __init__(self, eng, fn, is_dma=False):
        self.eng = eng
        self.fn = fn
        self.deps = []
        self.needs_inc = False
        self.is_dma = is_dma
        self.dsem = None
        self.dval = 0
        self.inc_n = None


class Sched:
    ENGS = ("pe", "act", "dve", "pool", "sp")

    def __init__(self, nc, same_engine_sync=("act", "dve", "pool")):
        self.nc = nc
        self.ops = {e: [] for e in self.ENGS}
        self.same = set(same_engine_sync)
        self.dma_ring = {e: 0 for e in self.ENGS}
        self.dma_last = {}
        self.dma_val = {}

    def op(self, eng, fn, reads=(), writes=(), is_dma=False, extra_deps=()):
        o = Op(eng, fn, is_dma)
        deps = list(extra_deps)
        for b in reads:
            if b.w is not None:
                deps.append(b.w)
        for b in writes:
            if b.w is not None:
                deps.append(b.w)
            for r in b.r.values():
                deps.append(r)
        if is_dma:
            i = self.dma_ring[eng]
            self.dma_ring[eng] = (i + 1) % NDSEM[eng]
            key = (eng, i)
            prev = self.dma_last.get(key)
            if prev is not None:
                deps.append(prev)
            self.dma_last[key] = o
            self.dma_val[key] = self.dma_val.get(key, 0) + 16
            o.dsem = key
            o.dval = self.dma_val[key]
        seen = set()
        for d in deps:
            if d is o or id(d) in seen:
                continue
            seen.add(id(d))
            if (not d.is_dma) and d.eng == eng and eng not in self.same:
                continue
            o.deps.append(d)
            if not d.is_dma:
                d.needs_inc = True
        for b in reads:
            b.r[eng] = o
        for b in writes:
            b.w = o
            b.r = {}
        self.ops[eng].append(o)
        return o

    def barrier(self):
        lasts = []
        for e in self.ENGS:
            for o in reversed(self.ops[e]):
                if not o.is_dma and o.fn is not None:
                    lasts.append(o)
                    break
        lasts += list(self.dma_last.values())
        for e in self.ENGS:
            self.op(e, None, extra_deps=[d for d in lasts if d.is_dma or d.eng != e])

    def emit(self):
        nc = self.nc
        with contextlib.ExitStack() as st:
            esems = {}
            for e in self.ENGS:
                n = 0
                for o in self.ops[e]:
                    if o.needs_inc and not o.is_dma:
                        o.inc_n = n
                        n += 1
                nep = (n + EPOCH - 1) // EPOCH
                esems[e] = [st.enter_context(nc.semaphore(f"s_{e}_{k}")) for k in range(max(nep, 1))]
            dsems = {}
            for key in self.dma_val:
                dsems[key] = st.enter_context(nc.semaphore(f"d_{key[0]}_{key[1]}"))
            block = st.enter_context(nc.Block())
            handles = {"pe": block.tensor, "act": block.scalar, "dve": block.vector,
                       "pool": block.gpsimd, "sp": block.sync}
            nw = {e: 0 for e in self.ENGS}

            def make(e):
                def body(eng):
                    seen_e = {}
                    seen_d = {}
                    for o in self.ops[e]:
                        for d in o.deps:
                            if d.is_dma:
                                if seen_d.get(d.dsem, 0) >= d.dval:
                                    continue
                                seen_d[d.dsem] = d.dval
                                eng.wait_ge(dsems[d.dsem], d.dval)
                                nw[e] += 1
                            else:
                                if seen_e.get(d.eng, -1) >= d.inc_n:
                                    continue
                                seen_e[d.eng] = d.inc_n
                                eng.wait_ge(esems[d.eng][d.inc_n // EPOCH], d.inc_n % EPOCH + 1)
                                nw[e] += 1
                        if o.fn is None:
                            continue
                        inst = o.fn(eng)
                        if o.is_dma:
                            inst.then_inc(dsems[o.dsem], 16)
                        elif o.needs_inc:
                            inst.then_inc(esems[e][o.inc_n // EPOCH], 1)
                return body

            for e in self.ENGS:
                if self.ops[e]:
                    handles[e](make(e))
            self.stats = {e: (len(self.ops[e]), nw[e]) for e in self.ENGS}


class T:
    __slots__ = ("ap", "buf")

    def __init__(self, ap, name=""):
        self.ap = ap
        self.buf = Buf(name)


def split_range(lo, hi, wmax):
    n = -(-(hi - lo) // wmax)
    w = -(-(hi - lo) // n)
    w = -(-w // 2) * 2
    out = []
    a = lo
    while a < hi:
        out.append((a, min(a + w, hi)))
        a += w
    return out


class Builder:
    def __init__(self, debug=False, phases="ABCDEFGH"):
        self.debug = debug
        self.phases = phases
        nc = self.nc = bass.Bass("TRN2", target_bir_lowering=False)
        self.S = Sched(nc)
        self.din = {}
        self.dout = {}
        self.arena_ap = nc.alloc_sbuf_tensor("arena", [128, ARENA], U8).ap()
        self.arena_off = 0
        self.pers_ap = nc.alloc_sbuf_tensor("pers", [128, PERS], U8).ap()
        self.pers_off = 0
        ps = nc.alloc_psum_tensor("ps", [128, 4096], F32).ap()
        self.bank = [T(ps[:, b * 512:(b + 1) * 512], f"bank{b}") for b in range(8)]
        self.ps_all = ps
        self.rr = 0

    def inp(self, name, shape, dtype=F32):
        t = self.nc.dram_tensor(name, list(shape), dtype, kind="ExternalInput").ap()
        self.din[name] = (tuple(shape), dtype)
        return T(t, name)

    def outp(self, name, shape, dtype=F32):
        t = self.nc.dram_tensor(name, list(shape), dtype, kind="ExternalOutput").ap()
        self.dout[name] = tuple(shape)
        return T(t, name)

    def scratch(self, name, shape, dtype):
        if self.debug:
            return self.outp(name, shape, dtype)
        t = self.nc.dram_tensor(name, list(shape), dtype, kind="Internal").ap()
        return T(t, name)

    def _carve(self, which, shape, dtype, name):
        esz = 2 if dtype == BF16 else 4
        n = 1
        for s in shape:
            n *= s
        nbytes = (n * esz + 63) // 64 * 64
        if which == "arena":
            off = self.arena_off
            self.arena_off += nbytes
            assert self.arena_off <= ARENA, (name, self.arena_off)
            base = self.arena_ap
        else:
            off = self.pers_off
            self.pers_off += nbytes
            assert self.pers_off <= PERS, (name, self.pers_off)
            base = self.pers_ap
        ap = base[:, off:off + n * esz].bitcast(dtype)
        if len(shape) == 2:
            ap = ap.rearrange("p (a b) -> p a b", a=shape[0])
        elif len(shape) == 3:
            ap = ap.rearrange("p (a b c) -> p a b c", a=shape[0], b=shape[1])
        elif len(shape) == 4:
            ap = ap.rearrange("p (a b c d) -> p a b c d", a=shape[0], b=shape[1], c=shape[2])
        return T(ap, name)

    def tile(self, shape, dtype, name=""):
        return self._carve("arena", shape, dtype, name)

    def ptile(self, shape, dtype, name=""):
        return self._carve("pers", shape, dtype, name)

    def new_phase(self):
        self.S.barrier()
        self.arena_off = 0

    def op(self, eng, meth, *args, reads=(), writes=(), dma=False, **kw):
        def fn(e, meth=meth, args=args, kw=kw):
            return getattr(e, meth)(*args, **kw)
        return self.S.op(eng, fn, reads=[t.buf for t in reads], writes=[t.buf for t in writes], is_dma=dma)

    def dma(self, q, out_ap, in_ap, reads=(), writes=(), **kw):
        return self.op(q, "dma_start", out=out_ap, in_=in_ap, reads=reads, writes=writes, dma=True, **kw)

    def castload(self, dst, dst_ap, src, src_ap):
        return self.dma("pool", dst_ap, src_ap, reads=[src], writes=[dst], max_dma_last_dim=2048)

    def mm(self, out, lhsT, rhs, start, stop, reads, writes, **kw):
        return self.op("pe", "matmul", out, lhsT=lhsT, rhs=rhs, start=start, stop=stop,
                       reads=reads, writes=writes, **kw)

    def nbank(self, lst):
        b = lst[self.rr % len(lst)]
        self.rr += 1
        return self.bank[b]


def dview(t, pat, **kw):
    return t.ap.rearrange(pat, **kw)


class Seg:
    pass


def build_program(debug=False, phases="ABCDEFGH"):
    B = Builder(debug, phases)
    nc, S = B.nc, B.S
    op, dma, mm = B.op, B.dma, B.mm

    xkvT = B.inp("xkvT", [KC, 128, SEQ])
    xeT = B.inp("xeT", [KC, 128, TP])
    xsT = B.inp("xsT", [KC, 128, TS])
    cT = B.inp("cT", [128, KC, 2])
    ada_w = B.inp("ada_w", [2, D, 6 * D])
    ada_bT = B.inp("ada_bT", [128, 2, 48])
    norm_gT = B.inp("norm_gT", [128, 2, 2, KC])
    final_gT = B.inp("final_gT", [128, KC])
    w_in = B.inp("w_in", [D, 2304])
    w_in_sw = B.inp("w_in_sw", [D, 2304])
    w_out0 = B.inp("w_out0", [D, D])
    lam_in = B.inp("lam_in", [128, 256])
    subln_in = B.inp("subln_in", [128, 128])
    qkg_in = B.inp("qkg_in", [128, 4])
    w_qkv1 = B.inp("w_qkv1", [D, 3 * D])
    w_out1 = B.inp("w_out1", [D, D])
    rpbpad = B.inp("rpbpad", [16, 16, 128])
    w_up = B.inp("w_up", [2, D, 2 * D_FF])
    conv_wT = B.inp("conv_wT", [128, 2, 3, 44])
    conv_bT = B.inp("conv_bT", [128, 2, 44])
    w_down = B.inp("w_down", [2, D_FF, D])
    rt_kv = B.inp("rt_kv", [128, 4, SEQ])
    rt_e = B.inp("rt_e", [128, 4, TP])
    rt_s = B.inp("rt_s", [128, 4, TS])
    vm_e = B.inp("vm_e", [128, TP])
    rf_e = B.inp("rf_e", [128, (TP // 128) * NDELTA * 2])
    rf_s = B.inp("rf_s", [128, (TS // 128) * NDELTA * 2])
    colmask_in = B.inp("colmask_in", [128, 64])
    ident_in = B.inp("ident_in", [128, 128])
    jmat_in = B.inp("jmat_in", [64, 64])

    yT_p = B.outp("yT_p", [KC, 128, OWN])
    yT_s = B.outp("yT_s", [KC, 128, TS])

    P = Seg()
    P.name, P.T, P.LK, P.si = "P", TP, SEQ, 0
    P.xkv, P.xq, P.rt_kv, P.rt_q, P.rf = xkvT, xeT, rt_kv, rt_e, rf_e
    P.own = (HALO, HALO + OWN)
    P.y = yT_p
    P.top_s, P.bot_s = HALO // 128, (HALO + OWN) // 128 - 1
    Sg = Seg()
    Sg.name, Sg.T, Sg.LK, Sg.si = "S", TS, TS, 1
    Sg.xkv, Sg.xq, Sg.rt_kv, Sg.rt_q, Sg.rf = xsT, xsT, rt_s, rt_s, rf_s
    Sg.own = (0, TS)
    Sg.y = yT_s
    Sg.top_s, Sg.bot_s = 0, TS // 128 - 1
    segs = [P, Sg]
    for g in segs:
        n = g.name
        nkt = g.LK // 128
        g.KT0d = B.scratch(f"KT0d_{n}", [4, 128, g.LK], BF16)
        g.KT0g = B.scratch(f"KT0g_{n}", [2, 128, g.LK], BF16)
        g.V0d = B.scratch(f"V0d_{n}", [4, 128, nkt, 130], BF16)
        g.V0g = B.scratch(f"V0g_{n}", [2, 128, nkt, 66], BF16)
        g.X1a = B.scratch(f"X1a_{n}", [KC, 128, g.T], F32)
        g.X1 = B.scratch(f"X1_{n}", [KC, 128, g.T], F32)
        g.X2a = B.scratch(f"X2a_{n}", [KC, 128, g.T], F32)
        g.X2 = B.scratch(f"X2_{n}", [KC, 128, g.T], F32)
        g.Q1T = B.scratch(f"Q1T_{n}", [KC, 128, g.T], BF16)
        g.K1T = B.scratch(f"K1T_{n}", [KC, 128, g.T], BF16)
        g.V1 = B.scratch(f"V1_{n}", [128, g.T // 128, 16, 66], BF16)

    WUB = [B.scratch(f"WUB{l}", [128, KC, 2 * D_FF], BF16) for l in range(2)]
    WDBs = [B.scratch(f"WDB{l}", [KC, 128, NFC, 128], BF16) for l in range(2)]
    WQB = B.scratch("WQB", [128, KC, 3 * D], BF16)
    WOB = B.scratch("WOB", [128, KC, D], BF16)

    def precast_weights():
        for l in range(2):
            for k in range(KC):
                dma("pool", WUB[l].ap[:, k, :], w_up.ap[l, k * 128:(k + 1) * 128, :], reads=[w_up], writes=[WUB[l]],
                    max_dma_last_dim=2048)
            for oc in range(KC):
                dma("pool", WDBs[l].ap[oc], w_down.ap[l, :, oc * 128:(oc + 1) * 128].rearrange("(j p) n -> p j n", p=128),
                    reads=[w_down], writes=[WDBs[l]], max_dma_last_dim=2048)
        for k in range(KC):
            dma("pool", WQB.ap[:, k, :], w_qkv1.ap[k * 128:(k + 1) * 128, :], reads=[w_qkv1], writes=[WQB], max_dma_last_dim=2048)
            dma("pool", WOB.ap[:, k, :], w_out1.ap[k * 128:(k + 1) * 128, :], reads=[w_out1], writes=[WOB], max_dma_last_dim=2048)

    ones_bf = B.ptile([128], BF16, "ones")
    blk_bf = B.ptile([128], BF16, "blk")
    ident_bf = B.ptile([128], BF16, "ident")
    mod = B.ptile([2, 48, 2], F32, "mod")
    A12 = B.ptile([2, 2, KC, 2], F32, "A12")
    ng = B.ptile([2, 2, KC], F32, "ng")
    fg = B.ptile([KC], F32, "fg")
    lamt = B.ptile([4], F32, "lamt")
    gsub = B.ptile([128], F32, "gsub")
    qkg = B.ptile([4], F32, "qkg")
    cw = B.ptile([2, 3, 44], F32, "cw")
    cb = B.ptile([2, 44], F32, "cb")
    rfe = B.ptile([(TP // 128) * NDELTA * 2], F32, "rfe")
    rfs = B.ptile([(TS // 128) * NDELTA * 2], F32, "rfs")
    P.rft, Sg.rft = rfe, rfs

    def mod_ap(l, ch, s):
        return mod.ap[:, l, ch, s:s + 1]

    def A_ap(l, which, c, s):
        return A12.ap[:, l, which, c, s:s + 1]

    op("dve", "memset", ones_bf.ap, 1.0, writes=[ones_bf])
    op("dve", "memset", blk_bf.ap, 0.0, writes=[blk_bf])
    op("dve", "memset", blk_bf.ap[0:64, 0:64], 1.0, writes=[blk_bf])
    op("dve", "memset", blk_bf.ap[64:128, 64:128], 1.0, writes=[blk_bf])
    B.castload(ident_bf, ident_bf.ap, ident_in, ident_in.ap)
    dma("sp", ng.ap, norm_gT.ap, reads=[norm_gT], writes=[ng])
    dma("sp", fg.ap, final_gT.ap, reads=[final_gT], writes=[fg])
    dma("sp", qkg.ap, qkg_in.ap, reads=[qkg_in], writes=[qkg])
    dma("sp", cw.ap, conv_wT.ap, reads=[conv_wT], writes=[cw])
    dma("sp", cb.ap, conv_bT.ap, reads=[conv_bT], writes=[cb])
    dma("sp", rfe.ap, rf_e.ap, reads=[rf_e], writes=[rfe])
    dma("sp", rfs.ap, rf_s.ap, reads=[rf_s], writes=[rfs])
    dma("sp", gsub.ap, subln_in.ap, reads=[subln_in], writes=[gsub])
    op("dve", "tensor_scalar", gsub.ap, gsub.ap, 1.0 - LAMBDA_INIT0, None, ALU.mult, reads=[gsub], writes=[gsub])

    lamw = B.tile([256], F32, "lamw")
    lamp = B.tile([2, 64], F32, "lamp")
    lams = B.tile([2], F32, "lams")
    dma("sp", lamw.ap, lam_in.ap, reads=[lam_in], writes=[lamw])
    lv = lamw.ap.rearrange("p (a b) -> p a b", a=4)
    op("dve", "tensor_tensor", lamp.ap[:, 0, :], lv[:, 0, :], lv[:, 1, :], ALU.mult, reads=[lamw], writes=[lamp])
    op("dve", "tensor_tensor", lamp.ap[:, 1, :], lv[:, 2, :], lv[:, 3, :], ALU.mult, reads=[lamw], writes=[lamp])
    op("dve", "tensor_reduce", lams.ap, lamp.ap, mybir.AxisListType.X, ALU.add, reads=[lamp], writes=[lams])
    op("act", "activation", lams.ap, lams.ap, AF.Exp, reads=[lams], writes=[lams])
    op("dve", "tensor_tensor", lamt.ap[:, 0:1], lams.ap[:, 1:2], lams.ap[:, 0:1], ALU.subtract, reads=[lams], writes=[lamt])
    op("dve", "tensor_scalar", lamt.ap[:, 0:1], lamt.ap[:, 0:1], -LAMBDA_INIT0, None, ALU.add, reads=[lamt], writes=[lamt])

    cact = B.tile([KC, 2], F32, "cact")
    dma("sp", cact.ap, cT.ap, reads=[cT], writes=[cact])
    op("act", "activation", cact.ap, cact.ap, AF.Silu, reads=[cact], writes=[cact])
    abt = B.tile([2, 48], F32, "abt")
    dma("sp", abt.ap, ada_bT.ap, reads=[ada_bT], writes=[abt])
    awb = [B.tile([KC, 512], F32, f"aw{i}") for i in range(2)]
    it = 0
    for l in range(2):
        pb = B.bank[l]
        for g in range(12):
            w = awb[it % 2]
            it += 1
            dma("sp", w.ap, ada_w.ap[l, :, g * 512:(g + 1) * 512].rearrange("(k p) n -> p k n", p=128),
                reads=[ada_w], writes=[w])
            for oc in range(4):
                ch = g * 4 + oc
                for k in range(KC):
                    mm(pb.ap[:, ch * 2:ch * 2 + 2], w.ap[:, k, oc * 128:(oc + 1) * 128], cact.ap[:, k, :],
                       k == 0, k == KC - 1, reads=[w, cact], writes=[pb])
        op("dve", "tensor_tensor", mod.ap[:, l], pb.ap[:, 0:96].rearrange("p (a b) -> p a b", b=2),
           abt.ap[:, l, :].unsqueeze(2).broadcast_to([128, 48, 2]),
           ALU.add, reads=[pb, abt], writes=[mod])
        for which in range(2):
            sc0 = 8 + 24 * which
            op("dve", "tensor_tensor", A12.ap[:, l, which], mod.ap[:, l, sc0:sc0 + 8, :],
               ng.ap[:, l, which, :].unsqueeze(2).broadcast_to([128, KC, 2]),
               ALU.mult, reads=[mod, ng], writes=[A12])
            op("dve", "tensor_tensor", A12.ap[:, l, which], A12.ap[:, l, which],
               ng.ap[:, l, which, :].unsqueeze(2).broadcast_to([128, KC, 2]),
               ALU.add, reads=[A12, ng], writes=[A12])

    GEN = [0, 1, 2, 3, 4, 5, 6, 7]

    def norm_A(src, lo, W, xT, sqb):
        dma("sp", xT.ap[:, :, 0:W], src.ap[:, :, lo:lo + W].rearrange("c p t -> p c t"), reads=[src], writes=[xT])
        h_ = KC // 2
        op("pool", "tensor_tensor", sqb.ap[:, 0:h_, 0:W], xT.ap[:, 0:h_, 0:W], xT.ap[:, 0:h_, 0:W], ALU.mult,
           reads=[xT], writes=[sqb])
        op("act", "activation", sqb.ap[:, h_:KC, 0:W], xT.ap[:, h_:KC, 0:W], AF.Square, reads=[xT], writes=[sqb])

    def norm_B(W, sqb, rt, bank=None):
        pb = B.nbank(GEN) if bank is None else B.bank[bank]
        for c in range(KC):
            mm(pb.ap[:, 0:W], ones_bf.ap, sqb.ap[:, c, 0:W], c == 0, c == KC - 1, reads=[ones_bf, sqb], writes=[pb])
        op("act", "activation", rt.ap[:, 0:W], pb.ap[:, 0:W], AF.Sqrt, bias=EPS, scale=1.0 / D, reads=[pb], writes=[rt])

    def norm_C(W, rt):
        op("dve", "reciprocal", rt.ap[:, 0:W], rt.ap[:, 0:W], reads=[rt], writes=[rt])

    def norm_D(c, W, l, which, s, xT, hT, hcol0, tmps, rt, all_act=False):
        tm = tmps[c % 2]
        op("dve", "tensor_tensor", tm.ap[:, 0:W], xT.ap[:, c, 0:W], rt.ap[:, 0:W], ALU.mult, reads=[xT, rt], writes=[tm])
        if c % 2 == 0 or all_act:
            op("act", "activation", hT.ap[:, c, hcol0:hcol0 + W], tm.ap[:, 0:W], AF.Identity,
               bias=mod_ap(l, 24 * which + c, s), scale=A_ap(l, which, c, s), reads=[tm, mod, A12], writes=[hT])
        else:
            op("pool", "tensor_scalar", hT.ap[:, c, hcol0:hcol0 + W], tm.ap[:, 0:W],
               A_ap(l, which, c, s), mod_ap(l, 24 * which + c, s), ALU.mult, ALU.add,
               reads=[tm, mod, A12], writes=[hT])

    def norm_tile(src, lo, W, l, which, s, xT, hT, hcol0, sqb, tmps, rt):
        norm_A(src, lo, W, xT, sqb)
        norm_B(W, sqb, rt)
        norm_C(W, rt)
        for c in range(KC):
            norm_D(c, W, l, which, s, xT, hT, hcol0, tmps, rt)

    class NormPipe:
        def __init__(self, specs, xTs, hTs, sqbs, tmps, rts):
            self.specs, self.xTs, self.hTs, self.sqbs, self.tmps, self.rts = specs, xTs, hTs, sqbs, tmps, rts
            self.done = {}

        def bufs(self, i):
            return (self.xTs[i % len(self.xTs)], self.hTs[i % len(self.hTs)], self.sqbs[i % len(self.sqbs)],
                    self.rts[i % len(self.rts)])

        def step(self, i, part):
            if i < 0 or i >= len(self.specs) or (i, part) in self.done:
                return
            self.done[(i, part)] = True
            sp = self.specs[i]
            xT, hT, sqb, rt = self.bufs(i)
            W = sp["W"]
            xv = xT
            if part == "A":
                norm_A(sp["src"], sp["lo"], W, xv, sqb)
            elif part == "B":
                norm_B(W, sqb, rt, sp.get("bank"))
            elif part == "C":
                norm_C(W, rt)
            elif part == "P":
                if sp.get("post"):
                    sp["post"](xT, hT)
            else:
                norm_D(part, W, sp["l"], sp["which"], sp["s"], xv, hT, sp["hcol0"], self.tmps, rt, sp.get("all_act", False))

        def upto(self, i, parts):
            for p_ in parts:
                self.step(i, p_)

        def all(self, i):
            self.upto(i, ["A", "B", "C"] + list(range(KC)) + ["P"])

    def load_w(dst, src, row0, col0, ncols, nk=KC):
        for c0 in range(0, ncols, 512):
            c1 = min(c0 + 512, ncols)
            for k0 in range(0, nk, 4):
                k1 = min(k0 + 4, nk)
                B.castload(dst, dst.ap[:, k0:k1, c0:c1],
                           src, src.ap[row0 + k0 * 128:row0 + k1 * 128, col0 + c0:col0 + c1].rearrange("(k p) n -> p k n", p=128))

    def proj_fm(hT, hcol0, W, wsb, wc0, pb, nk=KC):
        for k in range(nk):
            mm(pb.ap[:, 0:W], wsb.ap[:, k, wc0:wc0 + 128], hT.ap[:, k, hcol0:hcol0 + W], k == 0, k == nk - 1,
               reads=[wsb, hT], writes=[pb])

    def phase_C():
        B.new_phase()
        wk = B.tile([KC, 640], BF16, "wk")
        wks = B.tile([KC, 640], BF16, "wks")
        wv = B.tile([KC, 640], BF16, "wv")
        load_w(wk, w_in, 0, 512, 512)
        load_w(wks, w_in_sw, 0, 512, 512)
        for (dst, src) in ((wk, w_in), (wks, w_in_sw)):
            B.castload(dst, dst.ap[:, :, 512:640], src, src.ap[:, 2048:2176].rearrange("(k p) n -> p k n", p=128))
        load_w(wv, w_in, 0, 1024, 512)
        B.castload(wv, wv.ap[:, :, 512:640], w_in, w_in.ap[:, 2176:2304].rearrange("(k p) n -> p k n", p=128))
        xTs = [B.tile([KC, 512], F32, f"xT{i}") for i in range(2)]
        hTs = [B.tile([KC, 512], BF16, f"hT{i}") for i in range(2)]
        sqb = B.tile([KC, 512], BF16, "sqb")
        tmps = [B.tile([512], F32, f"ntmp{i}") for i in range(2)]
        rt = B.tile([512], F32, "rt")
        rts = [B.tile([4, 512], F32, f"rope{i}") for i in range(2)]
        t1 = [B.tile([512], F32, f"t1_{i}") for i in range(2)]
        t2 = [B.tile([512], F32, f"t2_{i}") for i in range(2)]
        kts = [B.tile([512], BF16, f"kt{i}") for i in range(3)]
        sqk = B.tile([512], BF16, "sqk")
        rk = B.tile([512], F32, "rk")
        vd = [B.tile([4, 4, 130], BF16, f"vd{i}") for i in range(2)]
        vg = [B.tile([2, 4, 66], BF16, f"vg{i}") for i in range(2)]
        for i in range(2):
            op("dve", "memset", vd[i].ap[:, :, :, 128:130], 1.0, writes=[vd[i]])
            op("dve", "memset", vg[i].ap[:, :, :, 64:66], 1.0, writes=[vg[i]])
        it = 0
        kti = 0
        tiles = [(g, t0) for g in segs for t0 in range(0, g.LK, 512)]
        xTs.append(B.tile([KC, 512], F32, "xT2"))
        sqbs = [sqb, B.tile([KC, 512], BF16, "sqb2")]
        rtsn = [rt, B.tile([512], F32, "rt2")]
        NP = NormPipe([dict(src=g_.xkv, lo=t_, W=512, l=0, which=0, s=g_.si, hcol0=0, all_act=True) for g_, t_ in tiles],
                      xTs, hTs, sqbs, tmps, rtsn)
        NP.all(0)
        NP.step(1, "A")
        for g, t0 in tiles:
            if True:
                W = 512
                hT, ro = hTs[it % 2], rts[it % 2]
                NP.step(it + 2, "A")
                sched = {0: ["B", "C"], 1: [0, 1], 2: [2, 3], 3: [4, 5], 4: [6, 7, "P"], 5: [], 6: [], 7: [], 8: []}
                dma("sp", ro.ap, g.rt_kv.ap[:, :, t0:t0 + W], reads=[g.rt_kv], writes=[ro])
                for h in range(5):
                    pa, pb_ = B.nbank(GEN), B.nbank(GEN)
                    proj_fm(hT, 0, W, wk, h * 128, pa)
                    proj_fm(hT, 0, W, wks, h * 128, pb_)
                    a, b_ = t1[h % 2], t2[h % 2]
                    kt = kts[kti % 3]
                    kti += 1
                    if h < 4:
                        op("dve", "tensor_tensor", a.ap, pa.ap, ro.ap[:, 0], ALU.mult, reads=[pa, ro], writes=[a])
                        op("dve", "tensor_tensor", b_.ap, pb_.ap, ro.ap[:, 1], ALU.mult, reads=[pb_, ro], writes=[b_])
                        op("pool", "tensor_tensor", kt.ap, a.ap, b_.ap, ALU.add, reads=[a, b_], writes=[kt])
                        dma("sp", g.KT0d.ap[h, :, t0:t0 + W], kt.ap, reads=[kt], writes=[g.KT0d])
                    else:
                        op("act", "activation", sqk.ap, pa.ap, AF.Square, reads=[pa], writes=[sqk])
                        pc = B.nbank(GEN)
                        mm(pc.ap, blk_bf.ap, sqk.ap, True, True, reads=[blk_bf, sqk], writes=[pc])
                        op("act", "activation", rk.ap, pc.ap, AF.Sqrt, bias=EPS, scale=1.0 / 64, reads=[pc], writes=[rk])
                        op("dve", "reciprocal", rk.ap, rk.ap, reads=[rk], writes=[rk])
                        op("dve", "scalar_tensor_tensor", a.ap, pa.ap, qkg.ap[:, 2:3], ro.ap[:, 2], ALU.mult, ALU.mult,
                           reads=[pa, ro, qkg], writes=[a])
                        op("dve", "scalar_tensor_tensor", b_.ap, pb_.ap, qkg.ap[:, 3:4], ro.ap[:, 3], ALU.mult, ALU.mult,
                           reads=[pb_, ro, qkg], writes=[b_])
                        op("pool", "tensor_tensor", a.ap, a.ap, b_.ap, ALU.add, reads=[a, b_], writes=[a])
                        op("dve", "tensor_tensor", kt.ap, a.ap, rk.ap, ALU.mult, reads=[a, rk], writes=[kt])
                        for gi in range(2):
                            for dup in range(2):
                                dma("sp", g.KT0g.ap[gi, dup * 64:(dup + 1) * 64, t0:t0 + W],
                                    kt.ap[gi * 64:(gi + 1) * 64, :], reads=[kt], writes=[g.KT0g])
                    NP.upto(it + 1, sched[h])
                vdt, vgt = vd[it % 2], vg[it % 2]
                for j in range(4):
                    pa, pb_ = B.nbank(GEN), B.nbank(GEN)
                    for k in range(KC):
                        mm(pa.ap, hT.ap[:, k, j * 128:(j + 1) * 128], wv.ap[:, k, 0:512], k == 0, k == KC - 1,
                           reads=[hT, wv], writes=[pa])
                    for k in range(KC):
                        mm(pb_.ap[:, 0:128], hT.ap[:, k, j * 128:(j + 1) * 128], wv.ap[:, k, 512:640], k == 0, k == KC - 1,
                           reads=[hT, wv], writes=[pb_])
                    op("act", "activation", vdt.ap[:, :, j, 0:128], pa.ap.rearrange("p (h e) -> p h e", h=4), AF.Copy,
                       reads=[pa], writes=[vdt])
                    op("dve", "tensor_copy", vgt.ap[:, :, j, 0:64], pb_.ap[:, 0:128].rearrange("p (h e) -> p h e", h=2),
                       reads=[pb_], writes=[vgt])
                    NP.upto(it + 1, sched[5 + j])
                NP.all(it + 1)
                kt0 = t0 // 128
                dma("sp", g.V0d.ap[:, :, kt0:kt0 + 4, :].rearrange("h p j e -> p h j e"), vdt.ap, reads=[vdt], writes=[g.V0d])
                dma("sp", g.V0g.ap[:, :, kt0:kt0 + 4, :].rearrange("h p j e -> p h j e"), vgt.ap, reads=[vgt], writes=[g.V0g])
                it += 1

    def phase_D():
        B.new_phase()
        wq = B.tile([KC, 1024], BF16, "wq")
        wqs = B.tile([KC, 1024], BF16, "wqs")
        wo = B.tile([KC, 1024], BF16, "wo")
        for (dst, src) in ((wq, w_in), (wqs, w_in_sw)):
            load_w(dst, src, 0, 0, 512)
            for c0 in (0, 512):
                pass
            for k0 in (0, 4):
                B.castload(dst, dst.ap[:, k0:k0 + 4, 512:1024], src,
                           src.ap[k0 * 128:(k0 + 4) * 128, 1536:2048].rearrange("(k p) n -> p k n", p=128))
        load_w(wo, w_out0, 0, 0, 1024)
        xTs = [B.tile([KC, 512], F32, f"xT{i}") for i in range(2)]
        hT = B.tile([KC, 512], BF16, "hT")
        sqb = B.tile([KC, 512], BF16, "sqb")
        tmps = [B.tile([512], F32, f"ntmp{i}") for i in range(2)]
        rt = B.tile([512], F32, "rt")
        ro = B.tile([4, 512], F32, "rope")
        dtiles = [(g_, t_) for g_ in segs for t_ in range(0, g_.T, 512)]
        NPD = NormPipe([dict(src=g_.xq, lo=t_, W=min(512, g_.T - t_), l=0, which=0, s=g_.si, hcol0=0, bank=7)
                        for g_, t_ in dtiles], xTs, [hT], [sqb], tmps, [rt])
        NPD.all(0)
        dti = [0]
        t1 = [B.tile([512], F32, f"t1_{i}") for i in range(2)]
        t2 = [B.tile([512], F32, f"t2_{i}") for i in range(2)]
        sqk = B.tile([512], BF16, "sqk")
        rk = B.tile([512], F32, "rk")
        qTd = [B.tile([512], BF16, f"qT{c}") for c in range(KC)]
        CH = 1024
        Kc = [B.tile([CH], BF16, f"Kc{i}") for i in range(3)]
        Vc = [B.tile([CH // 128, 130], BF16, f"Vc{i}") for i in range(3)]
        cnt = [0]
        did_precast = [False]
        Es = [B.tile([2, 512], BF16, f"E{i}") for i in range(3)]
        otok = B.tile([4, 1024], BF16, "otok")
        oT = B.tile([KC, 512], BF16, "oT")
        rs = B.tile([8], F32, "rs")
        r1l = B.tile([4], F32, "r1l")
        tneg = [B.tile([128], F32, f"tneg{i}") for i in range(2)]
        od = B.tile([4, 4, 128], F32, "od")
        ssq = B.tile([16], F32, "ssq")
        junk = B.tile([128], F32, "junk")
        xo = [B.tile([512], F32, f"xo{i}") for i in range(2)]
        SB = [(0, 1), (2, 3)]
        ACCB = [4, 5, 6]

        def acc_ap(s, j, w):
            idx = s * 4 + j
            bnk, r = idx // 3, idx % 3
            return B.bank[ACCB[bnk]], B.bank[ACCB[bnk]].ap[:, r * 132:r * 132 + w], r == 0

        ci = 0
        ei = 0
        for g in segs:
            for t0 in range(0, g.T, 512):
                W = min(512, g.T - t0)
                nq = W // 128
                ti_ = dti[0]
                dti[0] += 1
                xT = xTs[ti_ % 2]
                NPD.all(ti_)
                dma("sp", ro.ap[:, :, 0:W], g.rt_q.ap[:, :, t0:t0 + W], reads=[g.rt_q], writes=[ro])
                for c in range(8):
                    pa, pb_ = B.bank[7], B.bank[c % 2 + 4]
                    pa = B.bank[7] if c % 2 == 0 else B.bank[6]
                    proj_fm(hT, 0, W, wq, c * 128, pa)
                    proj_fm(hT, 0, W, wqs, c * 128, pb_)
                    a, b_ = t1[c % 2], t2[c % 2]
                    if c < 4:
                        op("dve", "tensor_tensor", a.ap[:, 0:W], pa.ap[:, 0:W], ro.ap[:, 0, 0:W], ALU.mult, reads=[pa, ro], writes=[a])
                        op("dve", "tensor_tensor", b_.ap[:, 0:W], pb_.ap[:, 0:W], ro.ap[:, 1, 0:W], ALU.mult, reads=[pb_, ro], writes=[b_])
                        op("pool", "tensor_tensor", qTd[c].ap[:, 0:W], a.ap[:, 0:W], b_.ap[:, 0:W], ALU.add, reads=[a, b_], writes=[qTd[c]])
                    else:
                        op("act", "activation", sqk.ap[:, 0:W], pa.ap[:, 0:W], AF.Square, reads=[pa], writes=[sqk])
                        pc = B.bank[3]
                        mm(pc.ap[:, 0:W], blk_bf.ap, sqk.ap[:, 0:W], True, True, reads=[blk_bf, sqk], writes=[pc])
                        op("act", "activation", rk.ap[:, 0:W], pc.ap[:, 0:W], AF.Sqrt, bias=EPS, scale=1.0 / 64, reads=[pc], writes=[rk])
                        op("dve", "reciprocal", rk.ap[:, 0:W], rk.ap[:, 0:W], reads=[rk], writes=[rk])
                        op("dve", "scalar_tensor_tensor", a.ap[:, 0:W], pa.ap[:, 0:W], qkg.ap[:, 0:1], ro.ap[:, 2, 0:W], ALU.mult, ALU.mult,
                           reads=[pa, ro, qkg], writes=[a])
                        op("dve", "scalar_tensor_tensor", b_.ap[:, 0:W], pb_.ap[:, 0:W], qkg.ap[:, 1:2], ro.ap[:, 3, 0:W], ALU.mult, ALU.mult,
                           reads=[pb_, ro, qkg], writes=[b_])
                        op("pool", "tensor_tensor", a.ap[:, 0:W], a.ap[:, 0:W], b_.ap[:, 0:W], ALU.add, reads=[a, b_], writes=[a])
                        op("dve", "tensor_tensor", qTd[c].ap[:, 0:W], a.ap[:, 0:W], rk.ap[:, 0:W], ALU.mult, reads=[a, rk], writes=[qTd[c]])
                if not did_precast[0]:
                    did_precast[0] = True
                    precast_weights()
                chunks = []
                items = []
                for u in range(8):
                    for k0 in range(0, g.LK, CH):
                        nk = min(CH, g.LK - k0)
                        chunks.append((u, k0, nk))
                        for kt in range(nk // 128):
                            items.append((len(chunks) - 1, u, k0, kt))
                cbuf = {}

                def load_chunk(cidx):
                    if cidx in cbuf or cidx >= len(chunks):
                        return
                    u, k0, nk = chunks[cidx]
                    kc_, vc_ = Kc[cnt[0] % 3], Vc[cnt[0] % 3]
                    cnt[0] += 1
                    cbuf[cidx] = (kc_, vc_)
                    if u < 4:
                        dma("sp", kc_.ap[:, 0:nk], g.KT0d.ap[u, :, k0:k0 + nk], reads=[g.KT0d], writes=[kc_])
                        dma("sp", vc_.ap[:, 0:nk // 128, :], g.V0d.ap[u, :, k0 // 128:(k0 + nk) // 128, :], reads=[g.V0d], writes=[vc_])
                    else:
                        gi = (u - 4) // 2
                        dma("sp", kc_.ap[:, 0:nk], g.KT0g.ap[gi, :, k0:k0 + nk], reads=[g.KT0g], writes=[kc_])
                        dma("sp", vc_.ap[:, 0:nk // 128, 0:66], g.V0g.ap[gi, :, k0 // 128:(k0 + nk) // 128, :], reads=[g.V0g], writes=[vc_])

                def stage_qk(i):
                    cidx, u, k0, kt = items[i]
                    if kt == 0:
                        load_chunk(cidx)
                        load_chunk(cidx + 1)
                    kc_, vc_ = cbuf[cidx]
                    b0, b1 = SB[i % 2]
                    E = Es[i % 3]
                    mm(B.bank[b0].ap[:, 0:W], kc_.ap[0:64, kt * 128:(kt + 1) * 128], qTd[u].ap[0:64, 0:W], True, True,
                       reads=[kc_, qTd[u]], writes=[B.bank[b0]])
                    mm(B.bank[b1].ap[:, 0:W], kc_.ap[64:128, kt * 128:(kt + 1) * 128], qTd[u].ap[64:128, 0:W], True, True,
                       reads=[kc_, qTd[u]], writes=[B.bank[b1]])
                    sin = B.ps_all[:, b0 * 512:(b0 + 2) * 512].rearrange("p (a b) -> p a b", a=2)[:, :, 0:W]
                    op("act", "activation", E.ap[:, :, 0:W], sin, AF.Exp, scale=SCALE,
                       reads=[B.bank[b0], B.bank[b1]], writes=[E])

                def stage_pv(i):
                    cidx, u, k0, kt = items[i]
                    kc_, vc_ = cbuf[cidx]
                    E = Es[i % 3]
                    diff = u < 4
                    vw = 129 if diff else 65
                    if k0 == 0 and kt == 0:
                        started.clear()
                    last = (k0 + (kt + 1) * 128 == g.LK)
                    for s in range(2):
                        for j in range(nq):
                            bk, ap_, _ = acc_ap(s, j, vw)
                            st_ = id(bk) not in started
                            started.add(id(bk))
                            mm(ap_, E.ap[:, s, j * 128:(j + 1) * 128], vc_.ap[:, kt, 0:vw], st_, last,
                               reads=[E, vc_], writes=[bk], skip_group_check=True)
                    if last:
                        finalize(u)

                def finalize(u):
                    diff = u < 4
                    vw = 129 if diff else 65
                    for bnk in range(3):
                        n_in = 3 if bnk < 2 else 2
                        src = B.bank[ACCB[bnk]].ap[:, 0:n_in * 132].rearrange("p (r e) -> p r e", e=132)[:, :, vw - 1:vw]
                        op("dve", "reciprocal", rs.ap[:, bnk * 3:bnk * 3 + n_in].rearrange("p (r e) -> p r e", e=1), src,
                           reads=[B.bank[ACCB[bnk]]], writes=[rs])
                    if diff:
                        op("dve", "tensor_scalar", r1l.ap, rs.ap[:, 4:8], lamt.ap[:, 0:1], None, ALU.mult, reads=[rs, lamt], writes=[r1l])
                        for j in range(nq):
                            bk0, a0, _ = acc_ap(0, j, 128)
                            bk1, a1, _ = acc_ap(1, j, 128)
                            tn = tneg[j % 2]
                            op("dve", "tensor_scalar", tn.ap, a1, r1l.ap[:, j:j + 1], None, ALU.mult, reads=[bk1, r1l], writes=[tn])
                            op("dve", "scalar_tensor_tensor", od.ap[:, u, j], a0, rs.ap[:, j:j + 1], tn.ap, ALU.mult, ALU.add,
                               reads=[bk0, rs, tn], writes=[od])
                        for j in range(nq):
                            op("dve", "scalar_tensor_tensor", junk.ap, od.ap[:, u, j], 1.0, od.ap[:, u, j], ALU.mult, ALU.mult,
                               accum_out=ssq.ap[:, u * 4 + j:u * 4 + j + 1], reads=[od], writes=[junk, ssq])
                    else:
                        for j in range(nq):
                            for s in range(2):
                                bk, a_, _ = acc_ap(s, j, 64)
                                c0 = 512 + (u - 4) * 128 + s * 64
                                op("dve", "tensor_scalar", otok.ap[:, j, c0:c0 + 64], a_, rs.ap[:, s * 4 + j:s * 4 + j + 1], None, ALU.mult,
                                   reads=[bk, rs], writes=[otok])

                started = set()
                nit = len(items)
                side = {int(nit * f): p_ for f, p_ in zip((0.10, 0.25, 0.35, 0.45, 0.50, 0.55, 0.60, 0.65, 0.70, 0.75, 0.80),
                                                           ("A", "B", "C", 0, 1, 2, 3, 4, 5, 6, 7))}
                for i in range(nit + 2):
                    if i < nit:
                        stage_qk(i)
                    if i >= 2:
                        stage_pv(i - 2)
                    if i in side:
                        NPD.step(ti_ + 1, side[i])
                op("act", "activation", ssq.ap, ssq.ap, AF.Sqrt, bias=SUBLN_EPS, scale=1.0 / 128, reads=[ssq], writes=[ssq])
                op("dve", "reciprocal", ssq.ap, ssq.ap, reads=[ssq], writes=[ssq])
                for u in range(4):
                    for j in range(nq):
                        op("dve", "scalar_tensor_tensor", otok.ap[:, j, u * 128:(u + 1) * 128], od.ap[:, u, j],
                           ssq.ap[:, u * 4 + j:u * 4 + j + 1], gsub.ap, ALU.mult, ALU.mult, reads=[od, ssq, gsub], writes=[otok])
                for j in range(nq):
                    pb_ = B.bank[7]
                    pbv = pb_.ap.bitcast(BF16)
                    for c in range(KC):
                        op("pe", "transpose", pbv[:, c * 128:(c + 1) * 128], otok.ap[:, j, c * 128:(c + 1) * 128], ident_bf.ap,
                           reads=[otok, ident_bf], writes=[pb_])
                    op("dve" if j % 2 else "act", "tensor_copy" if j % 2 else "copy", oT.ap[:, :, j * 128:(j + 1) * 128],
                       pbv.rearrange("p (c q) -> p c q", c=KC), reads=[pb_], writes=[oT])
                for oc in range(KC):
                    pb_ = B.bank[oc % 2]
                    proj_fm(oT, 0, W, wo, oc * 128, pb_)
                    x_ = xo[oc % 2]
                    op("dve", "scalar_tensor_tensor", x_.ap[:, 0:W], pb_.ap[:, 0:W], mod_ap(0, 16 + oc, g.si), xT.ap[:, oc, 0:W],
                       ALU.mult, ALU.add, reads=[pb_, mod, xT], writes=[x_])
                    dma("pool", g.X1a.ap[oc, :, t0:t0 + W], x_.ap[:, 0:W], reads=[x_], writes=[g.X1a])

    def phase_FFN(l, final):
        B.new_phase()
        wu = B.tile([KC, 2 * D_FF], BF16, "wu")
        for c0 in range(0, 2 * D_FF, 1408):
            dma("sp", wu.ap[:, :, c0:c0 + 1408], WUB[l].ap[:, :, c0:c0 + 1408], reads=[WUB[l]], writes=[wu])
        wds = [B.tile([NFC, 128], BF16, f"wd{i}") for i in range(2)]
        WDB = WDBs[l]

        def load_wd(oc, slot):
            dma("sp", wds[slot].ap, WDB.ap[oc], reads=[WDB], writes=[wds[slot]])
        xT = B.tile([KC, 512], F32, "xT")
        hT = B.tile([KC, 512], BF16, "hT")
        sqb = B.tile([KC, 512], BF16, "sqb")
        actT = B.tile([NFC, 512], BF16, "actT")
        rt = B.tile([512], F32, "rt")
        tg = [B.tile([512], F32, f"tg{i}") for i in range(2)]
        tv = [B.tile([512], F32, f"tv{i}") for i in range(2)]
        xres = [B.tile([512], F32, f"xres{i}") for i in range(2)]
        tiles = []
        for g in segs:
            src = g.X1a if l == 0 else g.X2a
            dst = g.X1 if l == 0 else g.X2
            rng = (0, g.T) if not final else g.own
            for (a, b_) in split_range(rng[0], rng[1], 510):
                tiles.append((g, src, dst, a, b_))
        specs = []
        for (g, src, dst, a, b_) in tiles:
            WO = b_ - a
            lo, hi = max(a - 1, 0), min(b_ + 1, g.T)
            c0 = lo - (a - 1)
            Wl = hi - lo

            def post(xT_, hT_, g=g, lo=lo, hi=hi, c0=c0, Wl=Wl, WO=WO, b_=b_):
                if c0 > 0:
                    op("dve", "memset", hT_.ap[:, :, 0:1], 0.0, writes=[hT_])
                if hi < b_ + 1:
                    op("dve", "memset", hT_.ap[:, :, WO + 1:WO + 2], 0.0, writes=[hT_])
                if g is P and (lo < HALO or hi > HALO + OWN):
                    vm = tv[0]
                    dma("sp", vm.ap[:, 0:Wl], vm_e.ap[:, lo:hi], reads=[vm_e], writes=[vm])
                    vb = vm.ap[:, 0:Wl].unsqueeze(1).broadcast_to([128, KC, Wl])
                    op("dve", "tensor_tensor", hT_.ap[:, :, c0:c0 + Wl], hT_.ap[:, :, c0:c0 + Wl], vb, ALU.mult,
                       reads=[hT_, vm], writes=[hT_])
            specs.append(dict(src=src, lo=lo, W=Wl, l=l, which=1, s=g.si, hcol0=c0, post=post))
        NP = NormPipe(specs, [xT], [hT], [sqb], [tg[0], tg[1]], [rt])
        NP.all(0)
        pi = 0
        for ti, (g, src, dst, a, b_) in enumerate(tiles):
            WO = b_ - a
            WI = WO + 2

            def load_xres(oc, a=a, b_=b_, src=src, WO=WO):
                dma("sp", xres[oc % 2].ap[:, 0:WO], src.ap[oc, :, a:b_], reads=[src], writes=[xres[oc % 2]])
            load_wd(0, 0)
            load_wd(1, 1)
            load_xres(0)
            load_xres(1)
            for j in range(NFC):
                pg, pv = B.bank[(pi % 2) * 2], B.bank[(pi % 2) * 2 + 1]
                g_, v_ = tg[pi % 2], tv[pi % 2]
                pi += 1
                for (pb_, tt, ch) in ((pg, g_, j), (pv, v_, NFC + j)):
                    for k in range(KC):
                        mm(pb_.ap[:, 0:WI], wu.ap[:, k, ch * 128:(ch + 1) * 128], hT.ap[:, k, 0:WI], k == 0, k == KC - 1,
                           reads=[wu, hT], writes=[pb_])
                    op("act", "activation", tt.ap[:, 0:WO], pb_.ap[:, 1:WO + 1], AF.Identity,
                       bias=cb.ap[:, l, ch:ch + 1], scale=cw.ap[:, l, 1, ch:ch + 1], reads=[pb_, cb, cw], writes=[tt])
                    op("dve", "scalar_tensor_tensor", tt.ap[:, 0:WO], pb_.ap[:, 0:WO], cw.ap[:, l, 0, ch:ch + 1], tt.ap[:, 0:WO],
                       ALU.mult, ALU.add, reads=[pb_, cw, tt], writes=[tt])
                    op("dve", "scalar_tensor_tensor", tt.ap[:, 0:WO], pb_.ap[:, 2:WO + 2], cw.ap[:, l, 2, ch:ch + 1], tt.ap[:, 0:WO],
                       ALU.mult, ALU.add, reads=[pb_, cw, tt], writes=[tt])
                op("act", "activation", g_.ap[:, 0:WO], g_.ap[:, 0:WO], AF.Silu, reads=[g_], writes=[g_])
                op("pool", "tensor_tensor", actT.ap[:, j, 0:WO], g_.ap[:, 0:WO], v_.ap[:, 0:WO], ALU.mult,
                   reads=[g_, v_], writes=[actT])
                if j == 1:
                    NP.step(ti + 1, "A")
                if j == 10:
                    NP.step(ti + 1, "B")
                if j == 13:
                    NP.step(ti + 1, "C")
            for oc in range(KC):
                wd = wds[oc % 2]
                pb_ = B.bank[4 + oc % 4]
                for j in range(NFC):
                    mm(pb_.ap[:, 0:WO], wd.ap[:, j, :], actT.ap[:, j, 0:WO], j == 0, j == NFC - 1,
                       reads=[wd, actT], writes=[pb_])
                xr = xres[oc % 2]
                op("dve", "scalar_tensor_tensor", xr.ap[:, 0:WO], pb_.ap[:, 0:WO], mod_ap(l, 40 + oc, g.si),
                   xr.ap[:, 0:WO], ALU.mult, ALU.add, reads=[pb_, mod, xr], writes=[xr])
                dma("pool", dst.ap[oc, :, a:b_], xr.ap[:, 0:WO], reads=[xr], writes=[dst])
                if oc + 2 < KC:
                    load_wd(oc + 2, oc % 2)
                    load_xres(oc + 2)
                if oc < 4:
                    NP.step(ti + 1, 2 * oc)
                    NP.step(ti + 1, 2 * oc + 1)
                if oc == 4:
                    NP.step(ti + 1, "P")
            NP.step(ti + 1, "P")

    def phase_I():
        B.new_phase()
        xTs = [B.tile([KC, 512], F32, f"xT{i}") for i in range(2)]
        sqbs = [B.tile([KC, 512], BF16, f"sqb{i}") for i in range(2)]
        yTs = [B.tile([KC, 512], F32, f"yT{i}") for i in range(2)]
        rt = B.tile([512], F32, "rt")
        tiles = [(g, a, min(a + 512, g.own[1])) for g in segs for a in range(g.own[0], g.own[1], 512)]

        def partA(i):
            if i < len(tiles):
                g_, a_, b2 = tiles[i]
                norm_A(g_.X2, a_, b2 - a_, xTs[i % 2], sqbs[i % 2])
        partA(0)
        for i, (g, a, b_) in enumerate(tiles):
            W = b_ - a
            partA(i + 1)
            norm_B(W, sqbs[i % 2], rt)
            norm_C(W, rt)
            yT = yTs[i % 2]
            for c in range(KC):
                op("dve", "scalar_tensor_tensor", yT.ap[:, c, 0:W], xTs[i % 2].ap[:, c, 0:W], fg.ap[:, c:c + 1],
                   rt.ap[:, 0:W], ALU.mult, ALU.mult, reads=[xTs[i % 2], fg, rt], writes=[yT])
            dma("pool", g.y.ap[:, :, a - g.own[0]:b_ - g.own[0]].rearrange("c p t -> p c t"), yT.ap[:, :, 0:W],
                reads=[yT], writes=[g.y])

    def phase_F():
        B.new_phase()
        wqk = B.tile([KC, 2048], BF16, "wqk")
        wv = B.tile([KC, 1024], BF16, "wv1")
        for c0 in range(0, 2048, 1024):
            dma("sp", wqk.ap[:, :, c0:c0 + 1024], WQB.ap[:, :, c0:c0 + 1024], reads=[WQB], writes=[wqk])
        dma("sp", wv.ap, WQB.ap[:, :, 2048:3072], reads=[WQB], writes=[wv])
        xTs = [B.tile([KC, 512], F32, f"xT{i}") for i in range(3)]
        hTs = [B.tile([KC, 512], BF16, f"hT{i}") for i in range(2)]
        sqbs = [B.tile([KC, 512], BF16, "sqb0")]
        tmps = [B.tile([512], F32, f"ntmp{i}") for i in range(2)]
        rtsn = [B.tile([512], F32, "rt0")]
        qk = [B.tile([KC, 512], BF16, f"qk{i}") for i in range(2)]
        v1 = [B.tile([4, 16, 66], BF16, f"v1_{i}") for i in range(2)]
        for i in range(2):
            op("dve", "memset", v1[i].ap[:, :, :, 64:66], 1.0, writes=[v1[i]])
        tiles = [(g, t0) for g in segs for t0 in range(0, g.T, 512)]
        NP = NormPipe([dict(src=g_.X1, lo=t_, W=min(512, g_.T - t_), l=1, which=0, s=g_.si, hcol0=0) for g_, t_ in tiles],
                      xTs, hTs, sqbs, tmps, rtsn)
        NP.all(0)
        NP.step(1, "A")
        it = 0
        for g, t0 in tiles:
            if True:
                W = min(512, g.T - t0)
                hT = hTs[it % 2]
                NP.step(it + 1, "B")
                NP.step(it + 2, "A")
                slot = 0
                sched = {1: ["C"], 3: [0], 4: [1], 5: [2], 6: [3], 7: [4], 8: [5], 9: [6], 10: [7, "P"]}
                for qi, dst in ((0, g.Q1T), (1, g.K1T)):
                    o_ = qk[qi]
                    for c in range(KC):
                        pb_ = B.nbank(GEN)
                        proj_fm(hT, 0, W, wqk, qi * 1024 + c * 128, pb_)
                        if c % 2 == 0:
                            op("act", "copy", o_.ap[:, c, 0:W], pb_.ap[:, 0:W], reads=[pb_], writes=[o_])
                        else:
                            op("dve", "tensor_copy", o_.ap[:, c, 0:W], pb_.ap[:, 0:W], reads=[pb_], writes=[o_])
                        NP.upto(it + 1, sched.get(slot, []))
                        slot += 1
                    dma("pool", dst.ap[:, :, t0:t0 + W].rearrange("c p t -> p c t"), o_.ap[:, :, 0:W], reads=[o_], writes=[dst])
                NP.all(it + 1)
                vt = v1[it % 2]
                it += 1
                for j in range(W // 128):
                    for half in range(2):
                        pb_ = B.nbank(GEN)
                        for k in range(KC):
                            mm(pb_.ap, hT.ap[:, k, j * 128:(j + 1) * 128], wv.ap[:, k, half * 512:(half + 1) * 512], k == 0, k == KC - 1,
                               reads=[hT, wv], writes=[pb_])
                        if half == 0:
                            op("act", "activation", vt.ap[:, j, 0:8, 0:64], pb_.ap.rearrange("p (h e) -> p h e", h=8), AF.Copy,
                               reads=[pb_], writes=[vt])
                        else:
                            op("dve", "tensor_copy", vt.ap[:, j, 8:16, 0:64], pb_.ap.rearrange("p (h e) -> p h e", h=8),
                               reads=[pb_], writes=[vt])
                dma("pool", g.V1.ap[:, t0 // 128:(t0 + W) // 128], vt.ap[:, 0:W // 128], reads=[vt], writes=[g.V1])

    def phase_G():
        B.new_phase()
        wo = B.tile([KC, 1024], BF16, "wo1")
        dma("sp", wo.ap, WOB.ap, reads=[WOB], writes=[wo])
        EB = B.tile([16, NDELTA, 128], BF16, "EB")
        EBI = B.tile([16, 2, 128], BF16, "EBI")
        hks = [B.tile([16, 64], F32, "hk0")] * 2
        hkes = [B.tile([16, 64], BF16, "hke0")] * 2
        jm = B.tile([64], BF16, "jm")
        cmk = B.tile([64], F32, "cmk")
        B.castload(jm, jm.ap[0:64], jmat_in, jmat_in.ap)
        dma("sp", cmk.ap, colmask_in.ap, reads=[colmask_in], writes=[cmk])
        for h in range(16):
            hk, hke = hks[h % 2], hkes[h % 2]
            src = bass.AP(rpbpad.ap.tensor, rpbpad.ap[h].offset, [[1, 64], [128, 16], [1, 64]])
            dma("sp", hk.ap[0:64], src, reads=[rpbpad], writes=[hk])
            op("act", "activation", hke.ap[0:64], hk.ap[0:64], AF.Exp, reads=[hk], writes=[hke])
            c, hh = h // 2, h % 2
            pos = (c // 4) * 8 + hh * 4 + (c % 4)
            pb = B.bank[2 + (h % 2)]
            pb2 = B.bank[4 + (h % 2)]
            for i in range(15):
                tgt = pb if i < 8 else pb2
                j = i if i < 8 else i - 8
                mm(tgt.ap[:, j * 64:(j + 1) * 64],
                   hke.ap[0:64, i:i + 2, :].rearrange("p a b -> p (a b)"), jm.ap[0:64], True, True,
                   reads=[hke, jm], writes=[tgt])
            for b in range(2):
                for di in range(NDELTA):
                    delta = -6 + 2 * di
                    i = delta - b + 7
                    tgt = pb if i < 8 else pb2
                    j = i if i < 8 else i - 8
                    op("dve", "tensor_tensor", EB.ap[:, pos, di, b * 64:(b + 1) * 64],
                       tgt.ap[:, j * 64:(j + 1) * 64], cmk.ap, ALU.mult, reads=[tgt, cmk], writes=[EB])
        op("dve", "tensor_copy", EBI.ap[:, :, 0, :], EB.ap[:, :, 1, :], reads=[EB], writes=[EBI])
        op("dve", "tensor_copy", EBI.ap[:, :, 1, :], EB.ap[:, :, 5, :], reads=[EB], writes=[EBI])
        op("dve", "memset", EBI.ap[0:64, :, 0, 64:128], 0.0, writes=[EBI])
        op("dve", "memset", EBI.ap[:, :, 1, 0:64], 0.0, writes=[EBI])
        op("dve", "memset", EBI.ap[64:128, :, 1, 64:128], 0.0, writes=[EBI])
        NKW = 10
        qTs = [B.tile([KC, 512], BF16, f"q1_{i}") for i in range(2)]
        Kws = [B.tile([KC, NKW * 128], BF16, f"kw{i}") for i in range(2)]
        Vws = [B.tile([NKW, 16, 66], BF16, f"vw{i}") for i in range(2)]
        Es = [B.tile([1024], BF16, f"E1_{i}") for i in range(3)]
        otok = B.tile([18 * 64], BF16, "otok1")
        oT = B.tile([KC, 512], BF16, "oT1")
        rs = B.tile([18], F32, "rs1")
        xT = B.tile([KC, 512], F32, "xT1")
        xo = [B.tile([512], F32, f"xo1_{i}") for i in range(2)]
        SBK = [(0, 1), (2, 3)]
        ACCB = [4, 5, 6]
        it = 0
        ei = 0
        for g in segs:
            nkt = g.T // 128
            for t0 in range(0, g.T, 512):
                W = min(512, g.T - t0)
                nq = W // 128
                s0 = t0 // 128
                m_lo = max(s0 - 3, 0)
                m_hi = min(s0 + nq - 1 + 3, nkt - 1) + 1
                qT, Kw, Vw = qTs[it % 2], Kws[it % 2], Vws[it % 2]
                it += 1
                dma("sp", qT.ap[:, :, 0:W], g.Q1T.ap[:, :, t0:t0 + W].rearrange("c p t -> p c t"), reads=[g.Q1T], writes=[qT])
                dma("sp", Kw.ap[:, :, 0:(m_hi - m_lo) * 128], g.K1T.ap[:, :, m_lo * 128:m_hi * 128].rearrange("c p t -> p c t"),
                    reads=[g.K1T], writes=[Kw])
                dma("sp", Vw.ap[:, 0:m_hi - m_lo], g.V1.ap[:, m_lo:m_hi], reads=[g.V1], writes=[Vw])
                dma("sp", xT.ap[:, :, 0:W], g.X1.ap[:, :, t0:t0 + W].rearrange("c p t -> p c t"), reads=[g.X1], writes=[xT])
                items = []
                for jj in range(nq):
                    s = s0 + jj
                    if g is P and (s * 128 + 128 <= P.own[0] - 64 or s * 128 >= P.own[1] + 64):
                        continue
                    dis = [di for di in range(NDELTA)
                           if 0 <= s + (di - 3) < nkt and (abs(di - 3) <= 2 or (di == 6 and s == g.top_s) or (di == 0 and s == g.bot_s))]
                    for n_, di in enumerate(dis):
                        for half in range(2):
                            items.append((jj, n_, di, half, len(dis)))

                def g_qk(i):
                    jj, n_, di, half, nd = items[i]
                    s = s0 + jj
                    ml = s + di - 3 - m_lo
                    b0, b1 = SBK[i % 2]
                    E = Es[i % 3]
                    for cc in range(4):
                        for hh in range(2):
                            bk = B.bank[b0 if hh == 0 else b1]
                            c = half * 4 + cc
                            mm(bk.ap[:, cc * 128:(cc + 1) * 128], Kw.ap[hh * 64:(hh + 1) * 64, c, ml * 128:(ml + 1) * 128],
                               qT.ap[hh * 64:(hh + 1) * 64, c, jj * 128:(jj + 1) * 128], True, True,
                               reads=[Kw, qT], writes=[bk])
                    op("act", "activation", E.ap, B.ps_all[:, b0 * 512:(b0 + 2) * 512], AF.Exp, scale=SCALE,
                       reads=[B.bank[b0], B.bank[b1]], writes=[E])
                    edge = (s <= g.top_s + 1) or (s >= g.bot_s - 1)
                    if not edge:
                        tbl = EBI.ap[:, half * 8:(half + 1) * 8, 0 if di == 1 else 1, :] if di in (1, 5) else \
                            EB.ap[:, half * 8:(half + 1) * 8, di, :]
                        ev = E.ap.rearrange("p (h c) -> p h c", h=8)
                        op("dve", "tensor_tensor", ev, ev, tbl, ALU.mult, reads=[E, EB, EBI], writes=[E])
                    else:
                        for b in range(2):
                            ev = E.ap.rearrange("p (h b c) -> p h b c", h=8, b=2)[:, :, b, :]
                            col = (s * NDELTA + di) * 2 + b
                            op("dve", "scalar_tensor_tensor", ev, ev, g.rft.ap[:, col:col + 1],
                               EB.ap[:, half * 8:(half + 1) * 8, di, b * 64:(b + 1) * 64], ALU.mult, ALU.mult,
                               reads=[E, g.rft, EB], writes=[E])

                def g_pv(i):
                    jj, n_, di, half, nd = items[i]
                    s = s0 + jj
                    ml = s + di - 3 - m_lo
                    E = Es[i % 3]
                    for hh in range(2):
                        for cc in range(4):
                            c = half * 4 + cc
                            h = 2 * c + hh
                            pos8 = hh * 4 + cc
                            bnk, r = h // 6, h % 6
                            bk = B.bank[ACCB[bnk]]
                            firstmm = (n_ == 0) and (h in (0, 6, 12))
                            mm(bk.ap[:, r * 66:r * 66 + 65], E.ap[:, pos8 * 128:(pos8 + 1) * 128], Vw.ap[:, ml, h, 0:65],
                               firstmm, n_ == nd - 1, reads=[E, Vw], writes=[bk], skip_group_check=True)
                    if n_ == nd - 1 and half == 1:
                        g_fin(jj)

                def g_fin(jj):
                    accs = [B.bank[b_] for b_ in ACCB]
                    av = B.ps_all[:, ACCB[0] * 512:(ACCB[0] + 3) * 512].rearrange("p (k r) -> p k r", k=3)[:, :, 0:396]
                    av = av.rearrange("p k (h e) -> p k h e", e=66)
                    rsv = rs.ap.rearrange("p (k h e) -> p k h e", k=3, e=1)
                    op("dve", "tensor_scalar", rsv, av[:, :, :, 64:65], 1e-30, None, ALU.add, reads=accs, writes=[rs])
                    op("dve", "reciprocal", rs.ap, rs.ap, reads=[rs], writes=[rs])
                    rb = rs.ap.rearrange("p (k h) -> p k h", k=3).unsqueeze(3).broadcast_to([128, 3, 6, 64])
                    op("dve", "tensor_tensor", otok.ap.rearrange("p (k h e) -> p k h e", k=3, e=64), av[:, :, :, 0:64], rb, ALU.mult,
                       reads=accs + [rs], writes=[otok])
                    pb_ = B.bank[7]
                    pbv = pb_.ap.bitcast(BF16)
                    for c in range(KC):
                        op("pe", "transpose", pbv[:, c * 128:(c + 1) * 128], otok.ap[:, c * 128:(c + 1) * 128], ident_bf.ap,
                           reads=[otok, ident_bf], writes=[pb_])
                    op("act", "copy", oT.ap[:, :, jj * 128:(jj + 1) * 128], pbv.rearrange("p (c q) -> p c q", c=KC),
                       reads=[pb_], writes=[oT])

                for i in range(len(items) + 2):
                    if i < len(items):
                        g_qk(i)
                    if i >= 2:
                        g_pv(i - 2)
                for oc in range(KC):
                    pb_ = B.bank[oc % 2]
                    proj_fm(oT, 0, W, wo, oc * 128, pb_)
                    x_ = xo[oc % 2]
                    op("dve", "scalar_tensor_tensor", x_.ap[:, 0:W], pb_.ap[:, 0:W], mod_ap(1, 16 + oc, g.si), xT.ap[:, oc, 0:W],
                       ALU.mult, ALU.add, reads=[pb_, mod, xT], writes=[x_])
                    dma("pool", g.X2a.ap[oc, :, t0:t0 + W], x_.ap[:, 0:W], reads=[x_], writes=[g.X2a])

    if "C" in phases:
        phase_C()
    if "D" in phases:
        phase_D()
    if "E" in phases:
        phase_FFN(0, False)
    if "F" in phases:
        phase_F()
    if "G" in phases:
        phase_G()
    if "H" in phases:
        phase_FFN(1, True)
        phase_I()
    S.barrier()
    S.emit()
    return B


def _fm(x):
    return np.ascontiguousarray(x.T.reshape(KC, 128, x.shape[0]))


def _rope_tables(pos_t, rows, cols):
    theta = np.float32(10000.0)
    inv1 = (theta ** (-np.arange(0, 64, 2, dtype=np.float32) / np.float32(64))).astype(np.float32)
    inv2 = (theta ** (-np.arange(0, 32, 2, dtype=np.float32) / np.float32(32))).astype(np.float32)
    ang1 = pos_t.astype(np.float32)[:, None] * inv1[None, :]
    ang2 = np.concatenate([rows.astype(np.float32)[:, None] * inv2[None, :],
                           cols.astype(np.float32)[:, None] * inv2[None, :]], axis=-1)
    out = np.zeros((128, 4, pos_t.shape[0]), np.float32)
    for k, ang in enumerate((ang1, ang2)):
        c = np.cos(ang).astype(np.float32).T
        s = np.sin(ang).astype(np.float32).T
        C = np.concatenate([c, c], 0)
        Sg = np.concatenate([-s, s], 0)
        out[:, 2 * k] = np.concatenate([C, C], 0)
        out[:, 2 * k + 1] = np.concatenate([Sg, Sg], 0)
    return out


def _rowflags(n_sub, row0_seq, n_rows_seq):
    rf = np.zeros((128, n_sub * NDELTA * 2), np.float32)
    for s in range(n_sub):
        for di in range(NDELTA):
            m = s + di - 3
            for b in range(2):
                qr = 2 * s + b + row0_seq
                for a in range(2):
                    kr = 2 * m + a + row0_seq
                    ok = 0.0
                    if 0 <= kr < n_rows_seq and 0 <= m < n_sub:
                        if 0 <= qr < n_rows_seq:
                            rs_ = min(max(qr - 4, 0), n_rows_seq - 8)
                            ok = 1.0 if rs_ <= kr < rs_ + 8 else 0.0
                    rf[a * 64:(a + 1) * 64, (s * NDELTA + di) * 2 + b] = ok
    return rf


_CACHE = {}


def _get_builder():
    if "b" not in _CACHE:
        _CACHE["b"] = build_program()
    return _CACHE["b"]


def host_inputs(inputs):
    f = lambda a: np.ascontiguousarray(np.asarray(a, dtype=np.float32))
    xp, xs = f(inputs["x_prompt"]), f(inputs["x_sample"])
    cp, cs = f(inputs["c_prompt"]), f(inputs["c_sample"])
    w_in = f(inputs["even_w_in"])[0]
    perm = np.arange(2304)
    for base, n in ((0, 8), (512, 8), (1536, 8), (2048, 2)):
        for h in range(n):
            o = base + h * 64
            perm[o:o + 32] = np.arange(o + 32, o + 64)
            perm[o + 32:o + 64] = np.arange(o, o + 32)
    w_in_sw = np.ascontiguousarray(w_in[:, perm])
    qk_g = f(inputs["gqa_qk_g"])[0]
    sw = np.concatenate([np.arange(32, 64), np.arange(0, 32)])
    qkg = np.stack([np.tile(qk_g[0], 2), np.tile(qk_g[0][sw], 2), np.tile(qk_g[1], 2), np.tile(qk_g[1][sw], 2)], 1)
    rpb = f(inputs["odd_rpb"])[0]
    rpbpad = np.full((16, 16, 128), -30000.0, np.float32)
    rpbpad[:, 0:15, 48:79] = rpb
    ada_b = f(inputs["ada_b"])
    common = {
        "ada_w": f(inputs["ada_w"]),
        "ada_bT": np.ascontiguousarray(ada_b.reshape(2, 48, 128).transpose(2, 0, 1)),
        "norm_gT": np.ascontiguousarray(f(inputs["norm_g"]).reshape(2, 2, KC, 128).transpose(3, 0, 1, 2)),
        "final_gT": np.ascontiguousarray(f(inputs["final_g"]).reshape(KC, 128).T),
        "w_in": w_in, "w_in_sw": w_in_sw, "w_out0": f(inputs["even_w_out"])[0],
        "lam_in": np.ascontiguousarray(np.broadcast_to(f(inputs["diff_lambda"])[0].reshape(1, 256), (128, 256))),
        "subln_in": np.ascontiguousarray(np.broadcast_to(f(inputs["diff_subln_g"])[0].reshape(1, 128), (128, 128))),
        "qkg_in": np.ascontiguousarray(qkg.astype(np.float32)),
        "w_qkv1": f(inputs["odd_w_qkv"])[0], "w_out1": f(inputs["odd_w_out"])[0],
        "rpbpad": rpbpad,
        "w_up": f(inputs["ffn_w_up"]),
        "conv_wT": np.ascontiguousarray(f(inputs["ffn_conv_w"]).reshape(2, 3, 44, 128).transpose(3, 0, 1, 2)),
        "conv_bT": np.ascontiguousarray(f(inputs["ffn_conv_b"]).reshape(2, 44, 128).transpose(2, 0, 1)),
        "w_down": f(inputs["ffn_w_down"]),
        "ident_in": np.eye(128, dtype=np.float32),
        "jmat_in": np.ascontiguousarray(np.eye(64, dtype=np.float32)[::-1]),
    }
    kc = np.arange(64)[:, None]
    cc = np.arange(64)[None, :]
    cs_ = np.clip(cc - 8, 0, 48)
    cm = ((kc >= cs_) & (kc < cs_ + 16)).astype(np.float32)
    common["colmask_in"] = np.concatenate([cm, cm], 0)
    t_full = np.arange(SEQ)
    rt_kv = _rope_tables(t_full, t_full // GRID_W, t_full % GRID_W)
    t_s = np.arange(TS)
    rt_s = _rope_tables(t_s, t_s // GRID_W, t_s % GRID_W)
    rf_s = _rowflags(TS // 128, 0, TS // GRID_W)
    xkv = [_fm(xp[b]) for b in range(2)]
    in_maps = []
    for c in range(8):
        b, j = c // 4, c % 4
        t0 = j * OWN - HALO
        idx = np.arange(t0, t0 + TP)
        valid = (idx >= 0) & (idx < SEQ)
        idc = np.clip(idx, 0, SEQ - 1)
        xe = np.where(valid[:, None], xp[b][idc], np.float32(0.0)).astype(np.float32)
        m = dict(common)
        m["xkvT"] = xkv[b]
        m["xeT"] = _fm(xe)
        m["xsT"] = _fm(xs[c])
        m["cT"] = np.ascontiguousarray(np.stack([cp[b], cs[c]], 1).reshape(KC, 128, 2).transpose(1, 0, 2))
        m["rt_kv"] = rt_kv
        m["rt_e"] = _rope_tables(idc, idc // GRID_W, idc % GRID_W)
        m["rt_s"] = rt_s
        m["vm_e"] = np.ascontiguousarray(np.broadcast_to(valid.astype(np.float32)[None, :], (128, TP)))
        m["rf_e"] = _rowflags(TP // 128, t0 // GRID_W, SEQ // GRID_W)
        m["rf_s"] = rf_s
        in_maps.append(m)
    return in_maps


def kernel(**inputs):
    B = _get_builder()
    in_maps = host_inputs(inputs)
    res = run_bass_kernel_spmd(B.nc, in_maps, core_ids=list(range(8)))
    yp = np.zeros((2, SEQ, D), np.float32)
    ys = np.zeros((8, DSEQ, D), np.float32)
    for c in range(8):
        r = res.results[c]
        b, j = c // 4, c % 4
        yp[b, j * OWN:(j + 1) * OWN] = np.asarray(r["yT_p"]).reshape(D, OWN).T
        ys[c] = np.asarray(r["yT_s"]).reshape(D, DSEQ).T
    return (yp, ys)
```

```python
import contextlib
import math
import numpy as np
import concourse.bass as bass
import concourse.mybir as mybir
from concourse.bass_utils import run_bass_kernel_spmd

F32 = mybir.dt.float32
BF16 = mybir.dt.bfloat16
U8 = mybir.dt.uint8
AF = mybir.ActivationFunctionType
ALU = mybir.AluOpType

D = 1024
KC = 8
SEQ = 16384
DSEQ = 2048
OWN = 4096
HALO = 384
TP = OWN + 2 * HALO
TS = DSEQ
GRID_W = 64


def set_cfg(seq, own, dseq):
    global SEQ, OWN, DSEQ, TP, TS
    SEQ, OWN, DSEQ = seq, own, dseq
    TP = OWN + 2 * HALO
    TS = DSEQ

D_FF = 2816
NFC = 22
EPS = 1e-6
SUBLN_EPS = 1e-5
LAMBDA_INIT0 = 0.8 - 0.6 * math.exp(-0.3 * 0)
SCALE = 64 ** -0.5
NDELTA = 7

EPOCH = 16000
NDSEM = {'sp': 16, 'pool': 56, 'act': 4, 'pe': 4, 'dve': 4}
ARENA = 196 * 1024
PERS = 10 * 1024


class Buf:
    __slots__ = ("name", "w", "r")

    def __init__(self, name=""):
        self.name = name
        self.w = None
        self.r = {}


class Op:
    __slots__ = ("eng", "fn", "deps", "needs_inc", "is_dma", "dsem", "dval", "inc_n")

    def __init__(self, eng, fn, is_dma=False):
        self.eng = eng
        self.fn = fn
        self.deps = []
        self.needs_inc = False
        self.is_dma = is_dma
        self.dsem = None
        self.dval = 0
        self.inc_n = None


class Sched:
    ENGS = ("pe", "act", "dve", "pool", "sp")

    def __init__(self, nc, same_engine_sync=("act", "dve", "pool")):
        self.nc = nc
        self.ops = {e: [] for e in self.ENGS}
        self.same = set(same_engine_sync)
        self.dma_ring = {e: 0 for e in self.ENGS}
        self.dma_last = {}
        self.dma_val = {}

    def op(self, eng, fn, reads=(), writes=(), is_dma=False, extra_deps=()):
        o = Op(eng, fn, is_dma)
        deps = list(extra_deps)
        for b in reads:
            if b.w is not None:
                deps.append(b.w)
        for b in writes:
            if b.w is not None:
                deps.append(b.w)
            for r in b.r.values():
                deps.append(r)
        if is_dma:
            i = self.dma_ring[eng]
            self.dma_ring[eng] = (i + 1) % NDSEM[eng]
            key = (eng, i)
            prev = self.dma_last.get(key)
            if prev is not None:
                deps.append(prev)
            self.dma_last[key] = o
            self.dma_val[key] = self.dma_val.get(key, 0) + 16
            o.dsem = key
            o.dval = self.dma_val[key]
        seen = set()
        for d in deps:
            if d is o or id(d) in seen:
                continue
            seen.add(id(d))
            if (not d.is_dma) and d.eng == eng and eng not in self.same:
                continue
            o.deps.append(d)
            if not d.is_dma:
                d.needs_inc = True
        for b in reads:
            b.r[eng] = o
        for b in writes:
            b.w = o
            b.r = {}
        self.ops[eng].append(o)
        return o

    def barrier(self):
        lasts = []
        for e in self.ENGS:
            for o in reversed(self.ops[e]):
                if not o.is_dma and o.fn is not None:
                    lasts.append(o)
                    break
        lasts += list(self.dma_last.values())
        for e in self.ENGS:
            self.op(e, None, extra_deps=[d for d in lasts if d.is_dma or d.eng != e])

    def emit(self):
        nc = self.nc
        with contextlib.ExitStack() as st:
            esems = {}
            for e in self.ENGS:
                n = 0
                for o in self.ops[e]:
                    if o.needs_inc and not o.is_dma:
                        o.inc_n = n
                        n += 1
                nep = (n + EPOCH - 1) // EPOCH
                esems[e] = [st.enter_context(nc.semaphore(f"s_{e}_{k}")) for k in range(max(nep, 1))]
            dsems = {}
            for key in self.dma_val:
                dsems[key] = st.enter_context(nc.semaphore(f"d_{key[0]}_{key[1]}"))
            block = st.enter_context(nc.Block())
            handles = {"pe": block.tensor, "act": block.scalar, "dve": block.vector,
                       "pool": block.gpsimd, "sp": block.sync}
            nw = {e: 0 for e in self.ENGS}

            def make(e):
                def body(eng):
                    seen_e = {}
                    seen_d = {}
                    for o in self.ops[e]:
                        for d in o.deps:
                            if d.is_dma:
                                if seen_d.get(d.dsem, 0) >= d.dval:
                                    continue
                                seen_d[d.dsem] = d.dval
                                eng.wait_ge(dsems[d.dsem], d.dval)
                                nw[e] += 1
                            else:
                                if seen_e.get(d.eng, -1) >= d.inc_n:
                                    continue
                                seen_e[d.eng] = d.inc_n
                                eng.wait_ge(esems[d.eng][d.inc_n // EPOCH], d.inc_n % EPOCH + 1)
                                nw[e] += 1
                        if o.fn is None:
                            continue
                        inst = o.fn(eng)
                        if o.is_dma:
                            inst.then_inc(dsems[o.dsem], 16)
                        elif o.needs_inc:
                            inst.then_inc(esems[e][o.inc_n // EPOCH], 1)
                return body

            for e in self.ENGS:
                if self.ops[e]:
                    handles[e](make(e))
            self.stats = {e: (len(self.ops[e]), nw[e]) for e in self.ENGS}


class T:
    __slots__ = ("ap", "buf")

    def __init__(self, ap, name=""):
        self.ap = ap
        self.buf = Buf(name)


def split_range(lo, hi, wmax):
    n = -(-(hi - lo) // wmax)
    w = -(-(hi - lo) // n)
    w = -(-w // 2) * 2
    out = []
    a = lo
    while a < hi:
        out.append((a, min(a + w, hi)))
        a += w
    return out


class Builder:
    def __init__(self, debug=False, phases="ABCDEFGH"):
        self.debug = debug
        self.phases = phases
        nc = self.nc = bass.Bass("TRN2", target_bir_lowering=False)
        self.S = Sched(nc)
        self.din = {}
        self.dout = {}
        self.arena_ap = nc.alloc_sbuf_tensor("arena", [128, ARENA], U8).ap()
        self.arena_off = 0
        self.pers_ap = nc.alloc_sbuf_tensor("pers", [128, PERS], U8).ap()
        self.pers_off = 0
        ps = nc.alloc_psum_tensor("ps", [128, 4096], F32).ap()
        self.bank = [T(ps[:, b * 512:(b + 1) * 512], f"bank{b}") for b in range(8)]
        self.ps_all = ps
        self.rr = 0

    def inp(self, name, shape, dtype=F32):
        t = self.nc.dram_tensor(name, list(shape), dtype, kind="ExternalInput").ap()
        self.din[name] = (tuple(shape), dtype)
        return T(t, name)

    def outp(self, name, shape, dtype=F32):
        t = self.nc.dram_tensor(name, list(shape), dtype, kind="ExternalOutput").ap()
        self.dout[name] = tuple(shape)
        return T(t, name)

    def scratch(self, name, shape, dtype):
        if self.debug:
            return self.outp(name, shape, dtype)
        t = self.nc.dram_tensor(name, list(shape), dtype, kind="Internal").ap()
        return T(t, name)

    def _carve(self, which, shape, dtype, name):
        esz = 2 if dtype == BF16 else 4
        n = 1
        for s in shape:
            n *= s
        nbytes = (n * esz + 63) // 64 * 64
        if which == "arena":
            off = self.arena_off
            self.arena_off += nbytes
            assert self.arena_off <= ARENA, (name, self.arena_off)
            base = self.arena_ap
        else:
            off = self.pers_off
            self.pers_off += nbytes
            assert self.pers_off <= PERS, (name, self.pers_off)
            base = self.pers_ap
        ap = base[:, off:off + n * esz].bitcast(dtype)
        if len(shape) == 2:
            ap = ap.rearrange("p (a b) -> p a b", a=shape[0])
        elif len(shape) == 3:
            ap = ap.rearrange("p (a b c) -> p a b c", a=shape[0], b=shape[1])
        elif len(shape) == 4:
            ap = ap.rearrange("p (a b c d) -> p a b c d", a=shape[0], b=shape[1], c=shape[2])
        return T(ap, name)

    def tile(self, shape, dtype, name=""):
        return self._carve("arena", shape, dtype, name)

    def ptile(self, shape, dtype, name=""):
        return self._carve("pers", shape, dtype, name)

    def new_phase(self):
        self.S.barrier()
        self.arena_off = 0

    def op(self, eng, meth, *args, reads=(), writes=(), dma=False, **kw):
        def fn(e, meth=meth, args=args, kw=kw):
            return getattr(e, meth)(*args, **kw)
        return self.S.op(eng, fn, reads=[t.buf for t in reads], writes=[t.buf for t in writes], is_dma=dma)

    def dma(self, q, out_ap, in_ap, reads=(), writes=(), **kw):
        return self.op(q, "dma_start", out=out_ap, in_=in_ap, reads=reads, writes=writes, dma=True, **kw)

    def castload(self, dst, dst_ap, src, src_ap):
        return self.dma("pool", dst_ap, src_ap, reads=[src], writes=[dst], max_dma_last_dim=2048)

    def mm(self, out, lhsT, rhs, start, stop, reads, writes, **kw):
        return self.op("pe", "matmul", out, lhsT=lhsT, rhs=rhs, start=start, stop=stop,
                       reads=reads, writes=writes, **kw)

    def nbank(self, lst):
        b = lst[self.rr % len(lst)]
        self.rr += 1
        return self.bank[b]


def dview(t, pat, **kw):
    return t.ap.rearrange(pat, **kw)


class Seg:
    pass


def build_program(debug=False, phases="ABCDEFGH"):
    B = Builder(debug, phases)
    nc, S = B.nc, B.S
    op, dma, mm = B.op, B.dma, B.mm

    xkvT = B.inp("xkvT", [KC, 128, SEQ])
    xeT = B.inp("xeT", [KC, 128, TP])
    xsT = B.inp("xsT", [KC, 128, TS])
    cT = B.inp("cT", [128, KC, 2])
    ada_w = B.inp("ada_w", [2, D, 6 * D])
    ada_bT = B.inp("ada_bT", [128, 2, 48])
    norm_gT = B.inp("norm_gT", [128, 2, 2, KC])
    final_gT = B.inp("final_gT", [128, KC])
    w_in = B.inp("w_in", [D, 2304])
    w_in_sw = B.inp("w_in_sw", [D, 2304])
    w_out0 = B.inp("w_out0", [D, D])
    lam_in = B.inp("lam_in", [128, 256])
    subln_in = B.inp("subln_in", [128, 128])
    qkg_in = B.inp("qkg_in", [128, 4])
    w_qkv1 = B.inp("w_qkv1", [D, 3 * D])
    w_out1 = B.inp("w_out1", [D, D])
    rpbpad = B.inp("rpbpad", [16, 16, 128])
    w_up = B.inp("w_up", [2, D, 2 * D_FF])
    conv_wT = B.inp("conv_wT", [128, 2, 3, 44])
    conv_bT = B.inp("conv_bT", [128, 2, 44])
    w_down = B.inp("w_down", [2, D_FF, D])
    rt_kv = B.inp("rt_kv", [128, 4, SEQ])
    rt_e = B.inp("rt_e", [128, 4, TP])
    rt_s = B.inp("rt_s", [128, 4, TS])
    vm_e = B.inp("vm_e", [128, TP])
    rf_e = B.inp("rf_e", [128, (TP // 128) * NDELTA * 2])
    rf_s = B.inp("rf_s", [128, (TS // 128) * NDELTA * 2])
    colmask_in = B.inp("colmask_in", [128, 64])
    ident_in = B.inp("ident_in", [128, 128])
    jmat_in = B.inp("jmat_in", [64, 64])

    yT_p = B.outp("yT_p", [KC, 128, OWN])
    yT_s = B.outp("yT_s", [KC, 128, TS])

    P = Seg()
    P.name, P.T, P.LK, P.si = "P", TP, SEQ, 0
    P.xkv, P.xq, P.rt_kv, P.rt_q, P.rf = xkvT, xeT, rt_kv, rt_e, rf_e
    P.own = (HALO, HALO + OWN)
    P.y = yT_p
    P.top_s, P.bot_s = HALO // 128, (HALO + OWN) // 128 - 1
    Sg = Seg()
    Sg.name, Sg.T, Sg.LK, Sg.si = "S", TS, TS, 1
    Sg.xkv, Sg.xq, Sg.rt_kv, Sg.rt_q, Sg.rf = xsT, xsT, rt_s, rt_s, rf_s
    Sg.own = (0, TS)
    Sg.y = yT_s
    Sg.top_s, Sg.bot_s = 0, TS // 128 - 1
    segs = [P, Sg]
    for g in segs:
        n = g.name
        nkt = g.LK // 128
        g.KT0d = B.scratch(f"KT0d_{n}", [4, 128, g.LK], BF16)
        g.KT0g = B.scratch(f"KT0g_{n}", [2, 128, g.LK], BF16)
        g.V0d = B.scratch(f"V0d_{n}", [4, 128, nkt, 130], BF16)
        g.V0g = B.scratch(f"V0g_{n}", [2, 128, nkt, 66], BF16)
        g.X1a = B.scratch(f"X1a_{n}", [KC, 128, g.T], F32)
        g.X1 = B.scratch(f"X1_{n}", [KC, 128, g.T], F32)
        g.X2a = B.scratch(f"X2a_{n}", [KC, 128, g.T], F32)
        g.X2 = B.scratch(f"X2_{n}", [KC, 128, g.T], F32)
        g.Q1T = B.scratch(f"Q1T_{n}", [KC, 128, g.T], BF16)
        g.K1T = B.scratch(f"K1T_{n}", [KC, 128, g.T], BF16)
        g.V1 = B.scratch(f"V1_{n}", [128, g.T // 128, 16, 66], BF16)

    WUB = [B.scratch(f"WUB{l}", [128, KC, 2 * D_FF], BF16) for l in range(2)]
    WDBs = [B.scratch(f"WDB{l}", [KC, 128, NFC, 128], BF16) for l in range(2)]
    WQB = B.scratch("WQB", [128, KC, 3 * D], BF16)
    WOB = B.scratch("WOB", [128, KC, D], BF16)

    def precast_weights():
        for l in range(2):
            for k in range(KC):
                dma("pool", WUB[l].ap[:, k, :], w_up.ap[l, k * 128:(k + 1) * 128, :], reads=[w_up], writes=[WUB[l]],
                    max_dma_last_dim=2048)
            for oc in range(KC):
                dma("pool", WDBs[l].ap[oc], w_down.ap[l, :, oc * 128:(oc + 1) * 128].rearrange("(j p) n -> p j n", p=128),
                    reads=[w_down], writes=[WDBs[l]], max_dma_last_dim=2048)
        for k in range(KC):
            dma("pool", WQB.ap[:, k, :], w_qkv1.ap[k * 128:(k + 1) * 128, :], reads=[w_qkv1], writes=[WQB], max_dma_last_dim=2048)
            dma("pool", WOB.ap[:, k, :], w_out1.ap[k * 128:(k + 1) * 128, :], reads=[w_out1], writes=[WOB], max_dma_last_dim=2048)

    ones_bf = B.ptile([128], BF16, "ones")
    blk_bf = B.ptile([128], BF16, "blk")
    ident_bf = B.ptile([128], BF16, "ident")
    mod = B.ptile([2, 48, 2], F32, "mod")
    A12 = B.ptile([2, 2, KC, 2], F32, "A12")
    ng = B.ptile([2, 2, KC], F32, "ng")
    fg = B.ptile([KC], F32, "fg")
    lamt = B.ptile([4], F32, "lamt")
    gsub = B.ptile([128], F32, "gsub")
    qkg = B.ptile([4], F32, "qkg")
    cw = B.ptile([2, 3, 44], F32, "cw")
    cb = B.ptile([2, 44], F32, "cb")
    rfe = B.ptile([(TP // 128) * NDELTA * 2], F32, "rfe")
    rfs = B.ptile([(TS // 128) * NDELTA * 2], F32, "rfs")
    P.rft, Sg.rft = rfe, rfs

    def mod_ap(l, ch, s):
        return mod.ap[:, l, ch, s:s + 1]

    def A_ap(l, which, c, s):
        return A12.ap[:, l, which, c, s:s + 1]

    op("dve", "memset", ones_bf.ap, 1.0, writes=[ones_bf])
    op("dve", "memset", blk_bf.ap, 0.0, writes=[blk_bf])
    op("dve", "memset", blk_bf.ap[0:64, 0:64], 1.0, writes=[blk_bf])
    op("dve", "memset", blk_bf.ap[64:128, 64:128], 1.0, writes=[blk_bf])
    B.castload(ident_bf, ident_bf.ap, ident_in, ident_in.ap)
    dma("sp", ng.ap, norm_gT.ap, reads=[norm_gT], writes=[ng])
    dma("sp", fg.ap, final_gT.ap, reads=[final_gT], writes=[fg])
    dma("sp", qkg.ap, qkg_in.ap, reads=[qkg_in], writes=[qkg])
    dma("sp", cw.ap, conv_wT.ap, reads=[conv_wT], writes=[cw])
    dma("sp", cb.ap, conv_bT.ap, reads=[conv_bT], writes=[cb])
    dma("sp", rfe.ap, rf_e.ap, reads=[rf_e], writes=[rfe])
    dma("sp", rfs.ap, rf_s.ap, reads=[rf_s], writes=[rfs])
    dma("sp", gsub.ap, subln_in.ap, reads=[subln_in], writes=[gsub])
    op("dve", "tensor_scalar", gsub.ap, gsub.ap, 1.0 - LAMBDA_INIT0, None, ALU.mult, reads=[gsub], writes=[gsub])

    lamw = B.tile([256], F32, "lamw")
    lamp = B.tile([2, 64], F32, "lamp")
    lams = B.tile([2], F32, "lams")
    dma("sp", lamw.ap, lam_in.ap, reads=[lam_in], writes=[lamw])
    lv = lamw.ap.rearrange("p (a b) -> p a b", a=4)
    op("dve", "tensor_tensor", lamp.ap[:, 0, :], lv[:, 0, :], lv[:, 1, :], ALU.mult, reads=[lamw], writes=[lamp])
    op("dve", "tensor_tensor", lamp.ap[:, 1, :], lv[:, 2, :], lv[:, 3, :], ALU.mult, reads=[lamw], writes=[lamp])
    op("dve", "tensor_reduce", lams.ap, lamp.ap, mybir.AxisListType.X, ALU.add, reads=[lamp], writes=[lams])
    op("act", "activation", lams.ap, lams.ap, AF.Exp, reads=[lams], writes=[lams])
    op("dve", "tensor_tensor", lamt.ap[:, 0:1], lams.ap[:, 1:2], lams.ap[:, 0:1], ALU.subtract, reads=[lams], writes=[lamt])
    op("dve", "tensor_scalar", lamt.ap[:, 0:1], lamt.ap[:, 0:1], -LAMBDA_INIT0, None, ALU.add, reads=[lamt], writes=[lamt])

    cact = B.tile([KC, 2], F32, "cact")
    dma("sp", cact.ap, cT.ap, reads=[cT], writes=[cact])
    op("act", "activation", cact.ap, cact.ap, AF.Silu, reads=[cact], writes=[cact])
    abt = B.tile([2, 48], F32, "abt")
    dma("sp", abt.ap, ada_bT.ap, reads=[ada_bT], writes=[abt])
    awb = [B.tile([KC, 512], F32, f"aw{i}") for i in range(2)]
    it = 0
    for l in range(2):
        pb = B.bank[l]
        for g in range(12):
            w = awb[it % 2]
            it += 1
            dma("sp", w.ap, ada_w.ap[l, :, g * 512:(g + 1) * 512].rearrange("(k p) n -> p k n", p=128),
                reads=[ada_w], writes=[w])
            for oc in range(4):
                ch = g * 4 + oc
                for k in range(KC):
                    mm(pb.ap[:, ch * 2:ch * 2 + 2], w.ap[:, k, oc * 128:(oc + 1) * 128], cact.ap[:, k, :],
                       k == 0, k == KC - 1, reads=[w, cact], writes=[pb])
        op("dve", "tensor_tensor", mod.ap[:, l], pb.ap[:, 0:96].rearrange("p (a b) -> p a b", b=2),
           abt.ap[:, l, :].unsqueeze(2).broadcast_to([128, 48, 2]),
           ALU.add, reads=[pb, abt], writes=[mod])
        for which in range(2):
            sc0 = 8 + 24 * which
            op("dve", "tensor_tensor", A12.ap[:, l, which], mod.ap[:, l, sc0:sc0 + 8, :],
               ng.ap[:, l, which, :].unsqueeze(2).broadcast_to([128, KC, 2]),
               ALU.mult, reads=[mod, ng], writes=[A12])
            op("dve", "tensor_tensor", A12.ap[:, l, which], A12.ap[:, l, which],
               ng.ap[:, l, which, :].unsqueeze(2).broadcast_to([128, KC, 2]),
               ALU.add, reads=[A12, ng], writes=[A12])

    GEN = [0, 1, 2, 3, 4, 5, 6, 7]

    def norm_A(src, lo, W, xT, sqb):
        dma("sp", xT.ap[:, :, 0:W], src.ap[:, :, lo:lo + W].rearrange("c p t -> p c t"), reads=[src], writes=[xT])
        h_ = KC // 2
        op("pool", "tensor_tensor", sqb.ap[:, 0:h_, 0:W], xT.ap[:, 0:h_, 0:W], xT.ap[:, 0:h_, 0:W], ALU.mult,
           reads=[xT], writes=[sqb])
        op("act", "activation", sqb.ap[:, h_:KC, 0:W], xT.ap[:, h_:KC, 0:W], AF.Square, reads=[xT], writes=[sqb])

    def norm_B(W, sqb, rt, bank=None):
        pb = B.nbank(GEN) if bank is None else B.bank[bank]
        for c in range(KC):
            mm(pb.ap[:, 0:W], ones_bf.ap, sqb.ap[:, c, 0:W], c == 0, c == KC - 1, reads=[ones_bf, sqb], writes=[pb])
        op("act", "activation", rt.ap[:, 0:W], pb.ap[:, 0:W], AF.Sqrt, bias=EPS, scale=1.0 / D, reads=[pb], writes=[rt])

    def norm_C(W, rt):
        op("dve", "reciprocal", rt.ap[:, 0:W], rt.ap[:, 0:W], reads=[rt], writes=[rt])

    def norm_D(c, W, l, which, s, xT, hT, hcol0, tmps, rt, all_act=False):
        tm = tmps[c % 2]
        op("dve", "tensor_tensor", tm.ap[:, 0:W], xT.ap[:, c, 0:W], rt.ap[:, 0:W], ALU.mult, reads=[xT, rt], writes=[tm])
        if c % 2 == 0 or all_act:
            op("act", "activation", hT.ap[:, c, hcol0:hcol0 + W], tm.ap[:, 0:W], AF.Identity,
               bias=mod_ap(l, 24 * which + c, s), scale=A_ap(l, which, c, s), reads=[tm, mod, A12], writes=[hT])
        else:
            op("pool", "tensor_scalar", hT.ap[:, c, hcol0:hcol0 + W], tm.ap[:, 0:W],
               A_ap(l, which, c, s), mod_ap(l, 24 * which + c, s), ALU.mult, ALU.add,
               reads=[tm, mod, A12], writes=[hT])

    def norm_tile(src, lo, W, l, which, s, xT, hT, hcol0, sqb, tmps, rt):
        norm_A(src, lo, W, xT, sqb)
        norm_B(W, sqb, rt)
        norm_C(W, rt)
        for c in range(KC):
            norm_D(c, W, l, which, s, xT, hT, hcol0, tmps, rt)

    class NormPipe:
        def __init__(self, specs, xTs, hTs, sqbs, tmps, rts):
            self.specs, self.xTs, self.hTs, self.sqbs, self.tmps, self.rts = specs, xTs, hTs, sqbs, tmps, rts
            self.done = {}

        def bufs(self, i):
            return (self.xTs[i % len(self.xTs)], self.hTs[i % len(self.hTs)], self.sqbs[i % len(self.sqbs)],
                    self.rts[i % len(self.rts)])

        def step(self, i, part):
            if i < 0 or i >= len(self.specs) or (i, part) in self.done:
                return
            self.done[(i, part)] = True
            sp = self.specs[i]
            xT, hT, sqb, rt = self.bufs(i)
            W = sp["W"]
            xv = xT
            if part == "A":
                norm_A(sp["src"], sp["lo"], W, xv, sqb)
            elif part == "B":
                norm_B(W, sqb, rt, sp.get("bank"))
            elif part == "C":
                norm_C(W, rt)
            elif part == "P":
                if sp.get("post"):
                    sp["post"](xT, hT)
            else:
                norm_D(part, W, sp["l"], sp["which"], sp["s"], xv, hT, sp["hcol0"], self.tmps, rt, sp.get("all_act", False))

        def upto(self, i, parts):
            for p_ in parts:
                self.step(i, p_)

        def all(self, i):
            self.upto(i, ["A", "B", "C"] + list(range(KC)) + ["P"])

    def load_w(dst, src, row0, col0, ncols, nk=KC):
        for c0 in range(0, ncols, 512):
            c1 = min(c0 + 512, ncols)
            for k0 in range(0, nk, 4):
                k1 = min(k0 + 4, nk)
                B.castload(dst, dst.ap[:, k0:k1, c0:c1],
                           src, src.ap[row0 + k0 * 128:row0 + k1 * 128, col0 + c0:col0 + c1].rearrange("(k p) n -> p k n", p=128))

    def proj_fm(hT, hcol0, W, wsb, wc0, pb, nk=KC):
        for k in range(nk):
            mm(pb.ap[:, 0:W], wsb.ap[:, k, wc0:wc0 + 128], hT.ap[:, k, hcol0:hcol0 + W], k == 0, k == nk - 1,
               reads=[wsb, hT], writes=[pb])

    def phase_C():
        B.new_phase()
        wk = B.tile([KC, 640], BF16, "wk")
        wks = B.tile([KC, 640], BF16, "wks")
        wv = B.tile([KC, 640], BF16, "wv")
        load_w(wk, w_in, 0, 512, 512)
        load_w(wks, w_in_sw, 0, 512, 512)
        for (dst, src) in ((wk, w_in), (wks, w_in_sw)):
            B.castload(dst, dst.ap[:, :, 512:640], src, src.ap[:, 2048:2176].rearrange("(k p) n -> p k n", p=128))
        load_w(wv, w_in, 0, 1024, 512)
        B.castload(wv, wv.ap[:, :, 512:640], w_in, w_in.ap[:, 2176:2304].rearrange("(k p) n -> p k n", p=128))
        xTs = [B.tile([KC, 512], F32, f"xT{i}") for i in range(2)]
        hTs = [B.tile([KC, 512], BF16, f"hT{i}") for i in range(2)]
        sqb = B.tile([KC, 512], BF16, "sqb")
        tmps = [B.tile([512], F32, f"ntmp{i}") for i in range(2)]
        rt = B.tile([512], F32, "rt")
        rts = [B.tile([4, 512], F32, f"rope{i}") for i in range(2)]
        t1 = [B.tile([512], F32, f"t1_{i}") for i in range(2)]
        t2 = [B.tile([512], F32, f"t2_{i}") for i in range(2)]
        kts = [B.tile([512], BF16, f"kt{i}") for i in range(3)]
        sqk = B.tile([512], BF16, "sqk")
        rk = B.tile([512], F32, "rk")
        vd = [B.tile([4, 4, 130], BF16, f"vd{i}") for i in range(2)]
        vg = [B.tile([2, 4, 66], BF16, f"vg{i}") for i in range(2)]
        for i in range(2):
            op("dve", "memset", vd[i].ap[:, :, :, 128:130], 1.0, writes=[vd[i]])
            op("dve", "memset", vg[i].ap[:, :, :, 64:66], 1.0, writes=[vg[i]])
        it = 0
        kti = 0
        tiles = [(g, t0) for g in segs for t0 in range(0, g.LK, 512)]
        xTs.append(B.tile([KC, 512], F32, "xT2"))
        sqbs = [sqb, B.tile([KC, 512], BF16, "sqb2")]
        rtsn = [rt, B.tile([512], F32, "rt2")]
        NP = NormPipe([dict(src=g_.xkv, lo=t_, W=512, l=0, which=0, s=g_.si, hcol0=0, all_act=True) for g_, t_ in tiles],
                      xTs, hTs, sqbs, tmps, rtsn)
        NP.all(0)
        NP.step(1, "A")
        for g, t0 in tiles:
            if True:
                W = 512
                hT, ro = hTs[it % 2], rts[it % 2]
                NP.step(it + 2, "A")
                sched = {0: ["B", "C"], 1: [0, 1], 2: [2, 3], 3: [4, 5], 4: [6, 7, "P"], 5: [], 6: [], 7: [], 8: []}
                dma("sp", ro.ap, g.rt_kv.ap[:, :, t0:t0 + W], reads=[g.rt_kv], writes=[ro])
                for h in range(5):
                    pa, pb_ = B.nbank(GEN), B.nbank(GEN)
                    proj_fm(hT, 0, W, wk, h * 128, pa)
                    proj_fm(hT, 0, W, wks, h * 128, pb_)
                    a, b_ = t1[h % 2], t2[h % 2]
                    kt = kts[kti % 3]
                    kti += 1
                    if h < 4:
                        op("dve", "tensor_tensor", a.ap, pa.ap, ro.ap[:, 0], ALU.mult, reads=[pa, ro], writes=[a])
                        op("dve", "tensor_tensor", b_.ap, pb_.ap, ro.ap[:, 1], ALU.mult, reads=[pb_, ro], writes=[b_])
                        op("pool", "tensor_tensor", kt.ap, a.ap, b_.ap, ALU.add, reads=[a, b_], writes=[kt])
                        dma("sp", g.KT0d.ap[h, :, t0:t0 + W], kt.ap, reads=[kt], writes=[g.KT0d])
                    else:
                        op("act", "activation", sqk.ap, pa.ap, AF.Square, reads=[pa], writes=[sqk])
                        pc = B.nbank(GEN)
                        mm(pc.ap, blk_bf.ap, sqk.ap, True, True, reads=[blk_bf, sqk], writes=[pc])
                        op("act", "activation", rk.ap, pc.ap, AF.Sqrt, bias=EPS, scale=1.0 / 64, reads=[pc], writes=[rk])
                        op("dve", "reciprocal", rk.ap, rk.ap, reads=[rk], writes=[rk])
                        op("dve", "scalar_tensor_tensor", a.ap, pa.ap, qkg.ap[:, 2:3], ro.ap[:, 2], ALU.mult, ALU.mult,
                           reads=[pa, ro, qkg], writes=[a])
                        op("dve", "scalar_tensor_tensor", b_.ap, pb_.ap, qkg.ap[:, 3:4], ro.ap[:, 3], ALU.mult, ALU.mult,
                           reads=[pb_, ro, qkg], writes=[b_])
                        op("pool", "tensor_tensor", a.ap, a.ap, b_.ap, ALU.add, reads=[a, b_], writes=[a])
                        op("dve", "tensor_tensor", kt.ap, a.ap, rk.ap, ALU.mult, reads=[a, rk], writes=[kt])
                        for gi in range(2):
                            for dup in range(2):
                                dma("sp", g.KT0g.ap[gi, dup * 64:(dup + 1) * 64, t0:t0 + W],
                                    kt.ap[gi * 64:(gi + 1) * 64, :], reads=[kt], writes=[g.KT0g])
                    NP.upto(it + 1, sched[h])
                vdt, vgt = vd[it % 2], vg[it % 2]
                for j in range(4):
                    pa, pb_ = B.nbank(GEN), B.nbank(GEN)
                    for k in range(KC):
                        mm(pa.ap, hT.ap[:, k, j * 128:(j + 1) * 128], wv.ap[:, k, 0:512], k == 0, k == KC - 1,
                           reads=[hT, wv], writes=[pa])
                    for k in range(KC):
                        mm(pb_.ap[:, 0:128], hT.ap[:, k, j * 128:(j + 1) * 128], wv.ap[:, k, 512:640], k == 0, k == KC - 1,
                           reads=[hT, wv], writes=[pb_])
                    op("act", "activation", vdt.ap[:, :, j, 0:128], pa.ap.rearrange("p (h e) -> p h e", h=4), AF.Copy,
                       reads=[pa], writes=[vdt])
                    op("dve", "tensor_copy", vgt.ap[:, :, j, 0:64], pb_.ap[:, 0:128].rearrange("p (h e) -> p h e", h=2),
                       reads=[pb_], writes=[vgt])
                    NP.upto(it + 1, sched[5 + j])
                NP.all(it + 1)
                kt0 = t0 // 128
                dma("sp", g.V0d.ap[:, :, kt0:kt0 + 4, :].rearrange("h p j e -> p h j e"), vdt.ap, reads=[vdt], writes=[g.V0d])
                dma("sp", g.V0g.ap[:, :, kt0:kt0 + 4, :].rearrange("h p j e -> p h j e"), vgt.ap, reads=[vgt], writes=[g.V0g])
                it += 1

    def phase_D():
        B.new_phase()
        wq = B.tile([KC, 1024], BF16, "wq")
        wqs = B.tile([KC, 1024], BF16, "wqs")
        wo = B.tile([KC, 1024], BF16, "wo")
        for (dst, src) in ((wq, w_in), (wqs, w_in_sw)):
            load_w(dst, src, 0, 0, 512)
            for c0 in (0, 512):
                pass
            for k0 in (0, 4):
                B.castload(dst, dst.ap[:, k0:k0 + 4, 512:1024], src,
                           src.ap[k0 * 128:(k0 + 4) * 128, 1536:2048].rearrange("(k p) n -> p k n", p=128))
        load_w(wo, w_out0, 0, 0, 1024)
        xTs = [B.tile([KC, 512], F32, f"xT{i}") for i in range(2)]
        hT = B.tile([KC, 512], BF16, "hT")
        sqb = B.tile([KC, 512], BF16, "sqb")
        tmps = [B.tile([512], F32, f"ntmp{i}") for i in range(2)]
        rt = B.tile([512], F32, "rt")
        ro = B.tile([4, 512], F32, "rope")
        dtiles = [(g_, t_) for g_ in segs for t_ in range(0, g_.T, 512)]
        NPD = NormPipe([dict(src=g_.xq, lo=t_, W=min(512, g_.T - t_), l=0, which=0, s=g_.si, hcol0=0, bank=7)
                        for g_, t_ in dtiles], xTs, [hT], [sqb], tmps, [rt])
        NPD.all(0)
        dti = [0]
        t1 = [B.tile([512], F32, f"t1_{i}") for i in range(2)]
        t2 = [B.tile([512], F32, f"t2_{i}") for i in range(2)]
        sqk = B.tile([512], BF16, "sqk")
        rk = B.tile([512], F32, "rk")
        qTd = [B.tile([512], BF16, f"qT{c}") for c in range(KC)]
        CH = 1024
        Kc = [B.tile([CH], BF16, f"Kc{i}") for i in range(3)]
        Vc = [B.tile([CH // 128, 130], BF16, f"Vc{i}") for i in range(3)]
        cnt = [0]
        did_precast = [False]
        Es = [B.tile([2, 512], BF16, f"E{i}") for i in range(3)]
        otok = B.tile([4, 1024], BF16, "otok")
        oT = B.tile([KC, 512], BF16, "oT")
        rs = B.tile([8], F32, "rs")
        r1l = B.tile([4], F32, "r1l")
        tneg = [B.tile([128], F32, f"tneg{i}") for i in range(2)]
        od = B.tile([4, 4, 128], F32, "od")
        ssq = B.tile([16], F32, "ssq")
        junk = B.tile([128], F32, "junk")
        xo = [B.tile([512], F32, f"xo{i}") for i in range(2)]
        SB = [(0, 1), (2, 3)]
        ACCB = [4, 5, 6]

        def acc_ap(s, j, w):
            idx = s * 4 + j
            bnk, r = idx // 3, idx % 3
            return B.bank[ACCB[bnk]], B.bank[ACCB[bnk]].ap[:, r * 132:r * 132 + w], r == 0

        ci = 0
        ei = 0
        for g in segs:
            for t0 in range(0, g.T, 512):
                W = min(512, g.T - t0)
                nq = W // 128
                ti_ = dti[0]
                dti[0] += 1
                xT = xTs[ti_ % 2]
                NPD.all(ti_)
                dma("sp", ro.ap[:, :, 0:W], g.rt_q.ap[:, :, t0:t0 + W], reads=[g.rt_q], writes=[ro])
                for c in range(8):
                    pa, pb_ = B.bank[7], B.bank[c % 2 + 4]
                    pa = B.bank[7] if c % 2 == 0 else B.bank[6]
                    proj_fm(hT, 0, W, wq, c * 128, pa)
                    proj_fm(hT, 0, W, wqs, c * 128, pb_)
                    a, b_ = t1[c % 2], t2[c % 2]
                    if c < 4:
                        op("dve", "tensor_tensor", a.ap[:, 0:W], pa.ap[:, 0:W], ro.ap[:, 0, 0:W], ALU.mult, reads=[pa, ro], writes=[a])
                        op("dve", "tensor_tensor", b_.ap[:, 0:W], pb_.ap[:, 0:W], ro.ap[:, 1, 0:W], ALU.mult, reads=[pb_, ro], writes=[b_])
                        op("pool", "tensor_tensor", qTd[c].ap[:, 0:W], a.ap[:, 0:W], b_.ap[:, 0:W], ALU.add, reads=[a, b_], writes=[qTd[c]])
                    else:
                        op("act", "activation", sqk.ap[:, 0:W], pa.ap[:, 0:W], AF.Square, reads=[pa], writes=[sqk])
                        pc = B.bank[3]
                        mm(pc.ap[:, 0:W], blk_bf.ap, sqk.ap[:, 0:W], True, True, reads=[blk_bf, sqk], writes=[pc])
                        op("act", "activation", rk.ap[:, 0:W], pc.ap[:, 0:W], AF.Sqrt, bias=EPS, scale=1.0 / 64, reads=[pc], writes=[rk])
                        op("dve", "reciprocal", rk.ap[:, 0:W], rk.ap[:, 0:W], reads=[rk], writes=[rk])
                        op("dve", "scalar_tensor_tensor", a.ap[:, 0:W], pa.ap[:, 0:W], qkg.ap[:, 0:1], ro.ap[:, 2, 0:W], ALU.mult, ALU.mult,
                           reads=[pa, ro, qkg], writes=[a])
                        op("dve", "scalar_tensor_tensor", b_.ap[:, 0:W], pb_.ap[:, 0:W], qkg.ap[:, 1:2], ro.ap[:, 3, 0:W], ALU.mult, ALU.mult,
                           reads=[pb_, ro, qkg], writes=[b_])
                        op("pool", "tensor_tensor", a.ap[:, 0:W], a.ap[:, 0:W], b_.ap[:, 0:W], ALU.add, reads=[a, b_], writes=[a])
                        op("dve", "tensor_tensor", qTd[c].ap[:, 0:W], a.ap[:, 0:W], rk.ap[:, 0:W], ALU.mult, reads=[a, rk], writes=[qTd[c]])
                if not did_precast[0]:
                    did_precast[0] = True
                    precast_weights()
                chunks = []
                items = []
                for u in range(8):
                    for k0 in range(0, g.LK, CH):
                        nk = min(CH, g.LK - k0)
                        chunks.append((u, k0, nk))
                        for kt in range(nk // 128):
                            items.append((len(chunks) - 1, u, k0, kt))
                cbuf = {}

                def load_chunk(cidx):
                    if cidx in cbuf or cidx >= len(chunks):
                        return
                    u, k0, nk = chunks[cidx]
                    kc_, vc_ = Kc[cnt[0] % 3], Vc[cnt[0] % 3]
                    cnt[0] += 1
                    cbuf[cidx] = (kc_, vc_)
                    if u < 4:
                        dma("sp", kc_.ap[:, 0:nk], g.KT0d.ap[u, :, k0:k0 + nk], reads=[g.KT0d], writes=[kc_])
                        dma("sp", vc_.ap[:, 0:nk // 128, :], g.V0d.ap[u, :, k0 // 128:(k0 + nk) // 128, :], reads=[g.V0d], writes=[vc_])
                    else:
                        gi = (u - 4) // 2
                        dma("sp", kc_.ap[:, 0:nk], g.KT0g.ap[gi, :, k0:k0 + nk], reads=[g.KT0g], writes=[kc_])
                        dma("sp", vc_.ap[:, 0:nk // 128, 0:66], g.V0g.ap[gi, :, k0 // 128:(k0 + nk) // 128, :], reads=[g.V0g], writes=[vc_])

                def stage_qk(i):
                    cidx, u, k0, kt = items[i]
                    if kt == 0:
                        load_chunk(cidx)
                        load_chunk(cidx + 1)
                    kc_, vc_ = cbuf[cidx]
                    b0, b1 = SB[i % 2]
                    E = Es[i % 3]
                    mm(B.bank[b0].ap[:, 0:W], kc_.ap[0:64, kt * 128:(kt + 1) * 128], qTd[u].ap[0:64, 0:W], True, True,
                       reads=[kc_, qTd[u]], writes=[B.bank[b0]])
                    mm(B.bank[b1].ap[:, 0:W], kc_.ap[64:128, kt * 128:(kt + 1) * 128], qTd[u].ap[64:128, 0:W], True, True,
                       reads=[kc_, qTd[u]], writes=[B.bank[b1]])
                    sin = B.ps_all[:, b0 * 512:(b0 + 2) * 512].rearrange("p (a b) -> p a b", a=2)[:, :, 0:W]
                    op("act", "activation", E.ap[:, :, 0:W], sin, AF.Exp, scale=SCALE,
                       reads=[B.bank[b0], B.bank[b1]], writes=[E])

                def stage_pv(i):
                    cidx, u, k0, kt = items[i]
                    kc_, vc_ = cbuf[cidx]
                    E = Es[i % 3]
                    diff = u < 4
                    vw = 129 if diff else 65
                    if k0 == 0 and kt == 0:
                        started.clear()
                    last = (k0 + (kt + 1) * 128 == g.LK)
                    for s in range(2):
                        for j in range(nq):
                            bk, ap_, _ = acc_ap(s, j, vw)
                            st_ = id(bk) not in started
                            started.add(id(bk))
                            mm(ap_, E.ap[:, s, j * 128:(j + 1) * 128], vc_.ap[:, kt, 0:vw], st_, last,
                               reads=[E, vc_], writes=[bk], skip_group_check=True)
                    if last:
                        finalize(u)

                def finalize(u):
                    diff = u < 4
                    vw = 129 if diff else 65
                    for bnk in range(3):
                        n_in = 3 if bnk < 2 else 2
                        src = B.bank[ACCB[bnk]].ap[:, 0:n_in * 132].rearrange("p (r e) -> p r e", e=132)[:, :, vw - 1:vw]
                        op("dve", "reciprocal", rs.ap[:, bnk * 3:bnk * 3 + n_in].rearrange("p (r e) -> p r e", e=1), src,
                           reads=[B.bank[ACCB[bnk]]], writes=[rs])
                    if diff:
                        op("dve", "tensor_scalar", r1l.ap, rs.ap[:, 4:8], lamt.ap[:, 0:1], None, ALU.mult, reads=[rs, lamt], writes=[r1l])
                        for j in range(nq):
                            bk0, a0, _ = acc_ap(0, j, 128)
                            bk1, a1, _ = acc_ap(1, j, 128)
                            tn = tneg[j % 2]
                            op("dve", "tensor_scalar", tn.ap, a1, r1l.ap[:, j:j + 1], None, ALU.mult, reads=[bk1, r1l], writes=[tn])
                            op("dve", "scalar_tensor_tensor", od.ap[:, u, j], a0, rs.ap[:, j:j + 1], tn.ap, ALU.mult, ALU.add,
                               reads=[bk0, rs, tn], writes=[od])
                        for j in range(nq):
                            op("dve", "scalar_tensor_tensor", junk.ap, od.ap[:, u, j], 1.0, od.ap[:, u, j], ALU.mult, ALU.mult,
                               accum_out=ssq.ap[:, u * 4 + j:u * 4 + j + 1], reads=[od], writes=[junk, ssq])
                    else:
                        for j in range(nq):
                            for s in range(2):
                                bk, a_, _ = acc_ap(s, j, 64)
                                c0 = 512 + (u - 4) * 128 + s * 64
                                op("dve", "tensor_scalar", otok.ap[:, j, c0:c0 + 64], a_, rs.ap[:, s * 4 + j:s * 4 + j + 1], None, ALU.mult,
                                   reads=[bk, rs], writes=[otok])

                started = set()
                nit = len(items)
                side = {int(nit * f): p_ for f, p_ in zip((0.10, 0.25, 0.35, 0.45, 0.50, 0.55, 0.60, 0.65, 0.70, 0.75, 0.80),
                                                           ("A", "B", "C", 0, 1, 2, 3, 4, 5, 6, 7))}
                for i in range(nit + 2):
                    if i < nit:
                        stage_qk(i)
                    if i >= 2:
                        stage_pv(i - 2)
                    if i in side:
                        NPD.step(ti_ + 1, side[i])
                op("act", "activation", ssq.ap, ssq.ap, AF.Sqrt, bias=SUBLN_EPS, scale=1.0 / 128, reads=[ssq], writes=[ssq])
                op("dve", "reciprocal", ssq.ap, ssq.ap, reads=[ssq], writes=[ssq])
                for j in range(nq):
                    for u in range(4):
                        op("dve", "scalar_tensor_tensor", otok.ap[:, j, u * 128:(u + 1) * 128], od.ap[:, u, j],
                           ssq.ap[:, u * 4 + j:u * 4 + j + 1], gsub.ap, ALU.mult, ALU.mult, reads=[od, ssq, gsub], writes=[otok])
                    pb_ = B.bank[7]
                    pbv = pb_.ap.bitcast(BF16)
                    for c in range(KC):
                        op("pe", "transpose", pbv[:, c * 128:(c + 1) * 128], otok.ap[:, j, c * 128:(c + 1) * 128], ident_bf.ap,
                           reads=[otok, ident_bf], writes=[pb_])
                    op("dve" if j % 2 else "act", "tensor_copy" if j % 2 else "copy", oT.ap[:, :, j * 128:(j + 1) * 128],
                       pbv.rearrange("p (c q) -> p c q", c=KC), reads=[pb_], writes=[oT])
                for oc in range(KC):
                    pb_ = B.bank[oc % 2]
                    proj_fm(oT, 0, W, wo, oc * 128, pb_)
                    x_ = xo[oc % 2]
                    op("dve", "scalar_tensor_tensor", x_.ap[:, 0:W], pb_.ap[:, 0:W], mod_ap(0, 16 + oc, g.si), xT.ap[:, oc, 0:W],
                       ALU.mult, ALU.add, reads=[pb_, mod, xT], writes=[x_])
                    dma("pool", g.X1a.ap[oc, :, t0:t0 + W], x_.ap[:, 0:W], reads=[x_], writes=[g.X1a])

    def phase_FFN(l, final):
        B.new_phase()
        wu = B.tile([KC, 2 * D_FF], BF16, "wu")
        for c0 in range(0, 2 * D_FF, 1408):
            dma("sp", wu.ap[:, :, c0:c0 + 1408], WUB[l].ap[:, :, c0:c0 + 1408], reads=[WUB[l]], writes=[wu])
        wds = [B.tile([NFC, 128], BF16, f"wd{i}") for i in range(2)]
        WDB = WDBs[l]

        def load_wd(oc, slot):
            dma("sp", wds[slot].ap, WDB.ap[oc], reads=[WDB], writes=[wds[slot]])
        xT = B.tile([KC, 512], F32, "xT")
        hT = B.tile([KC, 512], BF16, "hT")
        sqb = B.tile([KC, 512], BF16, "sqb")
        actT = B.tile([NFC, 512], BF16, "actT")
        rt = B.tile([512], F32, "rt")
        tg = [B.tile([512], F32, f"tg{i}") for i in range(2)]
        tv = [B.tile([512], F32, f"tv{i}") for i in range(2)]
        xres = [B.tile([512], F32, f"xres{i}") for i in range(2)]
        tiles = []
        for g in segs:
            src = g.X1a if l == 0 else g.X2a
            dst = g.X1 if l == 0 else g.X2
            rng = (0, g.T) if not final else g.own
            for (a, b_) in split_range(rng[0], rng[1], 510):
                tiles.append((g, src, dst, a, b_))
        specs = []
        for (g, src, dst, a, b_) in tiles:
            WO = b_ - a
            lo, hi = max(a - 1, 0), min(b_ + 1, g.T)
            c0 = lo - (a - 1)
            Wl = hi - lo

            def post(xT_, hT_, g=g, lo=lo, hi=hi, c0=c0, Wl=Wl, WO=WO, b_=b_):
                if c0 > 0:
                    op("dve", "memset", hT_.ap[:, :, 0:1], 0.0, writes=[hT_])
                if hi < b_ + 1:
                    op("dve", "memset", hT_.ap[:, :, WO + 1:WO + 2], 0.0, writes=[hT_])
                if g is P and (lo < HALO or hi > HALO + OWN):
                    vm = tv[0]
                    dma("sp", vm.ap[:, 0:Wl], vm_e.ap[:, lo:hi], reads=[vm_e], writes=[vm])
                    vb = vm.ap[:, 0:Wl].unsqueeze(1).broadcast_to([128, KC, Wl])
                    op("dve", "tensor_tensor", hT_.ap[:, :, c0:c0 + Wl], hT_.ap[:, :, c0:c0 + Wl], vb, ALU.mult,
                       reads=[hT_, vm], writes=[hT_])
            specs.append(dict(src=src, lo=lo, W=Wl, l=l, which=1, s=g.si, hcol0=c0, post=post))
        NP = NormPipe(specs, [xT], [hT], [sqb], [tg[0], tg[1]], [rt])
        NP.all(0)
        pi = 0
        for ti, (g, src, dst, a, b_) in enumerate(tiles):
            WO = b_ - a
            WI = WO + 2

            def load_xres(oc, a=a, b_=b_, src=src, WO=WO):
                dma("sp", xres[oc % 2].ap[:, 0:WO], src.ap[oc, :, a:b_], reads=[src], writes=[xres[oc % 2]])
            load_wd(0, 0)
            load_wd(1, 1)
            load_xres(0)
            load_xres(1)
            for j in range(NFC):
                pg, pv = B.bank[(pi % 2) * 2], B.bank[(pi % 2) * 2 + 1]
                g_, v_ = tg[pi % 2], tv[pi % 2]
                pi += 1
                for (pb_, tt, ch) in ((pg, g_, j), (pv, v_, NFC + j)):
                    for k in range(KC):
                        mm(pb_.ap[:, 0:WI], wu.ap[:, k, ch * 128:(ch + 1) * 128], hT.ap[:, k, 0:WI], k == 0, k == KC - 1,
                           reads=[wu, hT], writes=[pb_])
                    op("act", "activation", tt.ap[:, 0:WO], pb_.ap[:, 1:WO + 1], AF.Identity,
                       bias=cb.ap[:, l, ch:ch + 1], scale=cw.ap[:, l, 1, ch:ch + 1], reads=[pb_, cb, cw], writes=[tt])
                    op("dve", "scalar_tensor_tensor", tt.ap[:, 0:WO], pb_.ap[:, 0:WO], cw.ap[:, l, 0, ch:ch + 1], tt.ap[:, 0:WO],
                       ALU.mult, ALU.add, reads=[pb_, cw, tt], writes=[tt])
                    op("dve", "scalar_tensor_tensor", tt.ap[:, 0:WO], pb_.ap[:, 2:WO + 2], cw.ap[:, l, 2, ch:ch + 1], tt.ap[:, 0:WO],
                       ALU.mult, ALU.add, reads=[pb_, cw, tt], writes=[tt])
                op("act", "activation", g_.ap[:, 0:WO], g_.ap[:, 0:WO], AF.Silu, reads=[g_], writes=[g_])
                op("pool", "tensor_tensor", actT.ap[:, j, 0:WO], g_.ap[:, 0:WO], v_.ap[:, 0:WO], ALU.mult,
                   reads=[g_, v_], writes=[actT])
                if j == 1:
                    NP.step(ti + 1, "A")
                if j == 10:
                    NP.step(ti + 1, "B")
                if j == 13:
                    NP.step(ti + 1, "C")
            for oc in range(KC):
                wd = wds[oc % 2]
                pb_ = B.bank[4 + oc % 4]
                for j in range(NFC):
                    mm(pb_.ap[:, 0:WO], wd.ap[:, j, :], actT.ap[:, j, 0:WO], j == 0, j == NFC - 1,
                       reads=[wd, actT], writes=[pb_])
                xr = xres[oc % 2]
                op("dve", "scalar_tensor_tensor", xr.ap[:, 0:WO], pb_.ap[:, 0:WO], mod_ap(l, 40 + oc, g.si),
                   xr.ap[:, 0:WO], ALU.mult, ALU.add, reads=[pb_, mod, xr], writes=[xr])
                dma("pool", dst.ap[oc, :, a:b_], xr.ap[:, 0:WO], reads=[xr], writes=[dst])
                if oc + 2 < KC:
                    load_wd(oc + 2, oc % 2)
                    load_xres(oc + 2)
                if oc < 4:
                    NP.step(ti + 1, 2 * oc)
                    NP.step(ti + 1, 2 * oc + 1)
                if oc == 4:
                    NP.step(ti + 1, "P")
            NP.step(ti + 1, "P")

    def phase_I():
        B.new_phase()
        xTs = [B.tile([KC, 512], F32, f"xT{i}") for i in range(2)]
        sqbs = [B.tile([KC, 512], BF16, f"sqb{i}") for i in range(2)]
        yTs = [B.tile([KC, 512], F32, f"yT{i}") for i in range(2)]
        rt = B.tile([512], F32, "rt")
        itmp = [B.tile([512], F32, f"itmp{i}") for i in range(2)]
        tiles = [(g, a, min(a + 512, g.own[1])) for g in segs for a in range(g.own[0], g.own[1], 512)]

        def partA(i):
            if i < len(tiles):
                g_, a_, b2 = tiles[i]
                norm_A(g_.X2, a_, b2 - a_, xTs[i % 2], sqbs[i % 2])
        partA(0)
        for i, (g, a, b_) in enumerate(tiles):
            W = b_ - a
            partA(i + 1)
            norm_B(W, sqbs[i % 2], rt)
            norm_C(W, rt)
            yT = yTs[i % 2]
            for c in range(KC):
                if c % 2 == 0:
                    op("dve", "scalar_tensor_tensor", yT.ap[:, c, 0:W], xTs[i % 2].ap[:, c, 0:W], fg.ap[:, c:c + 1],
                       rt.ap[:, 0:W], ALU.mult, ALU.mult, reads=[xTs[i % 2], fg, rt], writes=[yT])
                else:
                    tm = itmp[(c // 2) % 2]
                    op("act", "activation", tm.ap[:, 0:W], xTs[i % 2].ap[:, c, 0:W], AF.Identity, scale=fg.ap[:, c:c + 1],
                       reads=[xTs[i % 2], fg], writes=[tm])
                    op("pool", "tensor_tensor", yT.ap[:, c, 0:W], tm.ap[:, 0:W], rt.ap[:, 0:W], ALU.mult,
                       reads=[tm, rt], writes=[yT])
            dma("pool", g.y.ap[:, :, a - g.own[0]:b_ - g.own[0]].rearrange("c p t -> p c t"), yT.ap[:, :, 0:W],
                reads=[yT], writes=[g.y])

    def phase_F():
        B.new_phase()
        wqk = B.tile([KC, 2048], BF16, "wqk")
        wv = B.tile([KC, 1024], BF16, "wv1")
        for c0 in range(0, 2048, 1024):
            dma("sp", wqk.ap[:, :, c0:c0 + 1024], WQB.ap[:, :, c0:c0 + 1024], reads=[WQB], writes=[wqk])
        dma("sp", wv.ap, WQB.ap[:, :, 2048:3072], reads=[WQB], writes=[wv])
        xTs = [B.tile([KC, 512], F32, f"xT{i}") for i in range(3)]
        hTs = [B.tile([KC, 512], BF16, f"hT{i}") for i in range(2)]
        sqbs = [B.tile([KC, 512], BF16, "sqb0")]
        tmps = [B.tile([512], F32, f"ntmp{i}") for i in range(2)]
        rtsn = [B.tile([512], F32, "rt0")]
        qk = [B.tile([KC, 512], BF16, f"qk{i}") for i in range(2)]
        v1 = [B.tile([4, 16, 66], BF16, f"v1_{i}") for i in range(2)]
        for i in range(2):
            op("dve", "memset", v1[i].ap[:, :, :, 64:66], 1.0, writes=[v1[i]])
        tiles = [(g, t0) for g in segs for t0 in range(0, g.T, 512)]
        NP = NormPipe([dict(src=g_.X1, lo=t_, W=min(512, g_.T - t_), l=1, which=0, s=g_.si, hcol0=0) for g_, t_ in tiles],
                      xTs, hTs, sqbs, tmps, rtsn)
        NP.all(0)
        NP.step(1, "A")
        it = 0
        for g, t0 in tiles:
            if True:
                W = min(512, g.T - t0)
                hT = hTs[it % 2]
                NP.step(it + 1, "B")
                NP.step(it + 2, "A")
                slot = 0
                sched = {1: ["C"], 3: [0], 4: [1], 5: [2], 6: [3], 7: [4], 8: [5], 9: [6], 10: [7, "P"]}
                for qi, dst in ((0, g.Q1T), (1, g.K1T)):
                    o_ = qk[qi]
                    for c in range(KC):
                        pb_ = B.nbank(GEN)
                        proj_fm(hT, 0, W, wqk, qi * 1024 + c * 128, pb_)
                        if c % 2 == 0:
                            op("act", "copy", o_.ap[:, c, 0:W], pb_.ap[:, 0:W], reads=[pb_], writes=[o_])
                        else:
                            op("dve", "tensor_copy", o_.ap[:, c, 0:W], pb_.ap[:, 0:W], reads=[pb_], writes=[o_])
                        NP.upto(it + 1, sched.get(slot, []))
                        slot += 1
                    dma("pool", dst.ap[:, :, t0:t0 + W].rearrange("c p t -> p c t"), o_.ap[:, :, 0:W], reads=[o_], writes=[dst])
                NP.all(it + 1)
                vt = v1[it % 2]
                it += 1
                for j in range(W // 128):
                    for half in range(2):
                        pb_ = B.nbank(GEN)
                        for k in range(KC):
                            mm(pb_.ap, hT.ap[:, k, j * 128:(j + 1) * 128], wv.ap[:, k, half * 512:(half + 1) * 512], k == 0, k == KC - 1,
                               reads=[hT, wv], writes=[pb_])
                        if half == 0:
                            op("act", "activation", vt.ap[:, j, 0:8, 0:64], pb_.ap.rearrange("p (h e) -> p h e", h=8), AF.Copy,
                               reads=[pb_], writes=[vt])
                        else:
                            op("dve", "tensor_copy", vt.ap[:, j, 8:16, 0:64], pb_.ap.rearrange("p (h e) -> p h e", h=8),
                               reads=[pb_], writes=[vt])
                dma("pool", g.V1.ap[:, t0 // 128:(t0 + W) // 128], vt.ap[:, 0:W // 128], reads=[vt], writes=[g.V1])

    def phase_G():
        B.new_phase()
        wo = B.tile([KC, 1024], BF16, "wo1")
        dma("sp", wo.ap, WOB.ap, reads=[WOB], writes=[wo])
        EB = B.tile([16, NDELTA, 128], BF16, "EB")
        EBI = B.tile([16, 2, 128], BF16, "EBI")
        hks = [B.tile([16, 64], F32, "hk0")] * 2
        hkes = [B.tile([16, 64], BF16, "hke0")] * 2
        jm = B.tile([64], BF16, "jm")
        cmk = B.tile([64], F32, "cmk")
        B.castload(jm, jm.ap[0:64], jmat_in, jmat_in.ap)
        dma("sp", cmk.ap, colmask_in.ap, reads=[colmask_in], writes=[cmk])
        for h in range(16):
            hk, hke = hks[h % 2], hkes[h % 2]
            src = bass.AP(rpbpad.ap.tensor, rpbpad.ap[h].offset, [[1, 64], [128, 16], [1, 64]])
            dma("sp", hk.ap[0:64], src, reads=[rpbpad], writes=[hk])
            op("act", "activation", hke.ap[0:64], hk.ap[0:64], AF.Exp, reads=[hk], writes=[hke])
            c, hh = h // 2, h % 2
            pos = (c // 4) * 8 + hh * 4 + (c % 4)
            pb = B.bank[2 + (h % 2)]
            pb2 = B.bank[4 + (h % 2)]
            for i in range(15):
                tgt = pb if i < 8 else pb2
                j = i if i < 8 else i - 8
                mm(tgt.ap[:, j * 64:(j + 1) * 64],
                   hke.ap[0:64, i:i + 2, :].rearrange("p a b -> p (a b)"), jm.ap[0:64], True, True,
                   reads=[hke, jm], writes=[tgt])
            for b in range(2):
                for di in range(NDELTA):
                    delta = -6 + 2 * di
                    i = delta - b + 7
                    tgt = pb if i < 8 else pb2
                    j = i if i < 8 else i - 8
                    op("dve", "tensor_tensor", EB.ap[:, pos, di, b * 64:(b + 1) * 64],
                       tgt.ap[:, j * 64:(j + 1) * 64], cmk.ap, ALU.mult, reads=[tgt, cmk], writes=[EB])
        op("dve", "tensor_copy", EBI.ap[:, :, 0, :], EB.ap[:, :, 1, :], reads=[EB], writes=[EBI])
        op("dve", "tensor_copy", EBI.ap[:, :, 1, :], EB.ap[:, :, 5, :], reads=[EB], writes=[EBI])
        op("dve", "memset", EBI.ap[0:64, :, 0, 64:128], 0.0, writes=[EBI])
        op("dve", "memset", EBI.ap[:, :, 1, 0:64], 0.0, writes=[EBI])
        op("dve", "memset", EBI.ap[64:128, :, 1, 64:128], 0.0, writes=[EBI])
        NKW = 10
        qTs = [B.tile([KC, 512], BF16, f"q1_{i}") for i in range(2)]
        Kws = [B.tile([KC, NKW * 128], BF16, f"kw{i}") for i in range(2)]
        Vws = [B.tile([NKW, 16, 66], BF16, f"vw{i}") for i in range(2)]
        Es = [B.tile([1024], BF16, f"E1_{i}") for i in range(3)]
        otok = B.tile([18 * 64], BF16, "otok1")
        oT = B.tile([KC, 512], BF16, "oT1")
        rs = B.tile([18], F32, "rs1")
        xT = B.tile([KC, 512], F32, "xT1")
        xo = [B.tile([512], F32, f"xo1_{i}") for i in range(2)]
        SBK = [(0, 1), (2, 3)]
        ACCB = [4, 5, 6]
        it = 0
        ei = 0
        for g in segs:
            nkt = g.T // 128
            for t0 in range(0, g.T, 512):
                W = min(512, g.T - t0)
                nq = W // 128
                s0 = t0 // 128
                m_lo = max(s0 - 3, 0)
                m_hi = min(s0 + nq - 1 + 3, nkt - 1) + 1
                qT, Kw, Vw = qTs[it % 2], Kws[it % 2], Vws[it % 2]
                it += 1
                dma("sp", qT.ap[:, :, 0:W], g.Q1T.ap[:, :, t0:t0 + W].rearrange("c p t -> p c t"), reads=[g.Q1T], writes=[qT])
                dma("sp", Kw.ap[:, :, 0:(m_hi - m_lo) * 128], g.K1T.ap[:, :, m_lo * 128:m_hi * 128].rearrange("c p t -> p c t"),
                    reads=[g.K1T], writes=[Kw])
                dma("sp", Vw.ap[:, 0:m_hi - m_lo], g.V1.ap[:, m_lo:m_hi], reads=[g.V1], writes=[Vw])
                dma("sp", xT.ap[:, :, 0:W], g.X1.ap[:, :, t0:t0 + W].rearrange("c p t -> p c t"), reads=[g.X1], writes=[xT])
                items = []
                for jj in range(nq):
                    s = s0 + jj
                    if g is P and (s * 128 + 128 <= P.own[0] - 64 or s * 128 >= P.own[1] + 64):
                        continue
                    dis = [di for di in range(NDELTA)
                           if 0 <= s + (di - 3) < nkt and (abs(di - 3) <= 2 or (di == 6 and s == g.top_s) or (di == 0 and s == g.bot_s))]
                    for n_, di in enumerate(dis):
                        for half in range(2):
                            items.append((jj, n_, di, half, len(dis)))

                def g_qk(i):
                    jj, n_, di, half, nd = items[i]
                    s = s0 + jj
                    ml = s + di - 3 - m_lo
                    b0, b1 = SBK[i % 2]
                    E = Es[i % 3]
                    for cc in range(4):
                        for hh in range(2):
                            bk = B.bank[b0 if hh == 0 else b1]
                            c = half * 4 + cc
                            mm(bk.ap[:, cc * 128:(cc + 1) * 128], Kw.ap[hh * 64:(hh + 1) * 64, c, ml * 128:(ml + 1) * 128],
                               qT.ap[hh * 64:(hh + 1) * 64, c, jj * 128:(jj + 1) * 128], True, True,
                               reads=[Kw, qT], writes=[bk])
                    op("act", "activation", E.ap, B.ps_all[:, b0 * 512:(b0 + 2) * 512], AF.Exp, scale=SCALE,
                       reads=[B.bank[b0], B.bank[b1]], writes=[E])
                    edge = (s <= g.top_s + 1) or (s >= g.bot_s - 1)
                    if not edge:
                        tbl = EBI.ap[:, half * 8:(half + 1) * 8, 0 if di == 1 else 1, :] if di in (1, 5) else \
                            EB.ap[:, half * 8:(half + 1) * 8, di, :]
                        ev = E.ap.rearrange("p (h c) -> p h c", h=8)
                        op("dve", "tensor_tensor", ev, ev, tbl, ALU.mult, reads=[E, EB, EBI], writes=[E])
                    else:
                        for b in range(2):
                            ev = E.ap.rearrange("p (h b c) -> p h b c", h=8, b=2)[:, :, b, :]
                            col = (s * NDELTA + di) * 2 + b
                            op("dve", "scalar_tensor_tensor", ev, ev, g.rft.ap[:, col:col + 1],
                               EB.ap[:, half * 8:(half + 1) * 8, di, b * 64:(b + 1) * 64], ALU.mult, ALU.mult,
                               reads=[E, g.rft, EB], writes=[E])

                def g_pv(i):
                    jj, n_, di, half, nd = items[i]
                    s = s0 + jj
                    ml = s + di - 3 - m_lo
                    E = Es[i % 3]
                    for hh in range(2):
                        for cc in range(4):
                            c = half * 4 + cc
                            h = 2 * c + hh
                            pos8 = hh * 4 + cc
                            bnk, r = h // 6, h % 6
                            bk = B.bank[ACCB[bnk]]
                            firstmm = (n_ == 0) and (h in (0, 6, 12))
                            mm(bk.ap[:, r * 66:r * 66 + 65], E.ap[:, pos8 * 128:(pos8 + 1) * 128], Vw.ap[:, ml, h, 0:65],
                               firstmm, n_ == nd - 1, reads=[E, Vw], writes=[bk], skip_group_check=True)
                    if n_ == nd - 1 and half == 1:
                        g_fin(jj)

                def g_fin(jj):
                    accs = [B.bank[b_] for b_ in ACCB]
                    av = B.ps_all[:, ACCB[0] * 512:(ACCB[0] + 3) * 512].rearrange("p (k r) -> p k r", k=3)[:, :, 0:396]
                    av = av.rearrange("p k (h e) -> p k h e", e=66)
                    rsv = rs.ap.rearrange("p (k h e) -> p k h e", k=3, e=1)
                    op("dve", "tensor_scalar", rsv, av[:, :, :, 64:65], 1e-30, None, ALU.add, reads=accs, writes=[rs])
                    op("dve", "reciprocal", rs.ap, rs.ap, reads=[rs], writes=[rs])
                    rb = rs.ap.rearrange("p (k h) -> p k h", k=3).unsqueeze(3).broadcast_to([128, 3, 6, 64])
                    op("dve", "tensor_tensor", otok.ap.rearrange("p (k h e) -> p k h e", k=3, e=64), av[:, :, :, 0:64], rb, ALU.mult,
                       reads=accs + [rs], writes=[otok])
                    pb_ = B.bank[7]
                    pbv = pb_.ap.bitcast(BF16)
                    for c in range(KC):
                        op("pe", "transpose", pbv[:, c * 128:(c + 1) * 128], otok.ap[:, c * 128:(c + 1) * 128], ident_bf.ap,
                           reads=[otok, ident_bf], writes=[pb_])
                    op("act", "copy", oT.ap[:, :, jj * 128:(jj + 1) * 128], pbv.rearrange("p (c q) -> p c q", c=KC),
                       reads=[pb_], writes=[oT])

                for i in range(len(items) + 2):
                    if i < len(items):
                        g_qk(i)
                    if i >= 2:
                        g_pv(i - 2)
                for oc in range(KC):
                    pb_ = B.bank[oc % 2]
                    proj_fm(oT, 0, W, wo, oc * 128, pb_)
                    x_ = xo[oc % 2]
                    op("dve", "scalar_tensor_tensor", x_.ap[:, 0:W], pb_.ap[:, 0:W], mod_ap(1, 16 + oc, g.si), xT.ap[:, oc, 0:W],
                       ALU.mult, ALU.add, reads=[pb_, mod, xT], writes=[x_])
                    dma("pool", g.X2a.ap[oc, :, t0:t0 + W], x_.ap[:, 0:W], reads=[x_], writes=[g.X2a])

    if "C" in phases:
        phase_C()
    if "D" in phases:
        phase_D()
    if "E" in phases:
        phase_FFN(0, False)
    if "F" in phases:
        phase_F()
    if "G" in phases:
        phase_G()
    if "H" in phases:
        phase_FFN(1, True)
        phase_I()
    S.barrier()
    S.emit()
    return B


def _fm(x):
    return np.ascontiguousarray(x.T.reshape(KC, 128, x.shape[0]))


def _rope_tables(pos_t, rows, cols):
    theta = np.float32(10000.0)
    inv1 = (theta ** (-np.arange(0, 64, 2, dtype=np.float32) / np.float32(64))).astype(np.float32)
    inv2 = (theta ** (-np.arange(0, 32, 2, dtype=np.float32) / np.float32(32))).astype(np.float32)
    ang1 = pos_t.astype(np.float32)[:, None] * inv1[None, :]
    ang2 = np.concatenate([rows.astype(np.float32)[:, None] * inv2[None, :],
                           cols.astype(np.float32)[:, None] * inv2[None, :]], axis=-1)
    out = np.zeros((128, 4, pos_t.shape[0]), np.float32)
    for k, ang in enumerate((ang1, ang2)):
        c = np.cos(ang).astype(np.float32).T
        s = np.sin(ang).astype(np.float32).T
        C = np.concatenate([c, c], 0)
        Sg = np.concatenate([-s, s], 0)
        out[:, 2 * k] = np.concatenate([C, C], 0)
        out[:, 2 * k + 1] = np.concatenate([Sg, Sg], 0)
    return out


def _rowflags(n_sub, row0_seq, n_rows_seq):
    rf = np.zeros((128, n_sub * NDELTA * 2), np.float32)
    for s in range(n_sub):
        for di in range(NDELTA):
            m = s + di - 3
            for b in range(2):
                qr = 2 * s + b + row0_seq
                for a in range(2):
                    kr = 2 * m + a + row0_seq
                    ok = 0.0
                    if 0 <= kr < n_rows_seq and 0 <= m < n_sub:
                        if 0 <= qr < n_rows_seq:
                            rs_ = min(max(qr - 4, 0), n_rows_seq - 8)
                            ok = 1.0 if rs_ <= kr < rs_ + 8 else 0.0
                    rf[a * 64:(a + 1) * 64, (s * NDELTA + di) * 2 + b] = ok
    return rf


_CACHE = {}


def _get_builder():
    if "b" not in _CACHE:
        _CACHE["b"] = build_program()
    return _CACHE["b"]


def host_inputs(inputs):
    f = lambda a: np.ascontiguousarray(np.asarray(a, dtype=np.float32))
    xp, xs = f(inputs["x_prompt"]), f(inputs["x_sample"])
    cp, cs = f(inputs["c_prompt"]), f(inputs["c_sample"])
    w_in = f(inputs["even_w_in"])[0]
    perm = np.arange(2304)
    for base, n in ((0, 8), (512, 8), (1536, 8), (2048, 2)):
        for h in range(n):
            o = base + h * 64
            perm[o:o + 32] = np.arange(o + 32, o + 64)
            perm[o + 32:o + 64] = np.arange(o, o + 32)
    w_in_sw = np.ascontiguousarray(w_in[:, perm])
    qk_g = f(inputs["gqa_qk_g"])[0]
    sw = np.concatenate([np.arange(32, 64), np.arange(0, 32)])
    qkg = np.stack([np.tile(qk_g[0], 2), np.tile(qk_g[0][sw], 2), np.tile(qk_g[1], 2), np.tile(qk_g[1][sw], 2)], 1)
    rpb = f(inputs["odd_rpb"])[0]
    rpbpad = np.full((16, 16, 128), -30000.0, np.float32)
    rpbpad[:, 0:15, 48:79] = rpb
    ada_b = f(inputs["ada_b"])
    common = {
        "ada_w": f(inputs["ada_w"]),
        "ada_bT": np.ascontiguousarray(ada_b.reshape(2, 48, 128).transpose(2, 0, 1)),
        "norm_gT": np.ascontiguousarray(f(inputs["norm_g"]).reshape(2, 2, KC, 128).transpose(3, 0, 1, 2)),
        "final_gT": np.ascontiguousarray(f(inputs["final_g"]).reshape(KC, 128).T),
        "w_in": w_in, "w_in_sw": w_in_sw, "w_out0": f(inputs["even_w_out"])[0],
        "lam_in": np.ascontiguousarray(np.broadcast_to(f(inputs["diff_lambda"])[0].reshape(1, 256), (128, 256))),
        "subln_in": np.ascontiguousarray(np.broadcast_to(f(inputs["diff_subln_g"])[0].reshape(1, 128), (128, 128))),
        "qkg_in": np.ascontiguousarray(qkg.astype(np.float32)),
        "w_qkv1": f(inputs["odd_w_qkv"])[0], "w_out1": f(inputs["odd_w_out"])[0],
        "rpbpad": rpbpad,
        "w_up": f(inputs["ffn_w_up"]),
        "conv_wT": np.ascontiguousarray(f(inputs["ffn_conv_w"]).reshape(2, 3, 44, 128).transpose(3, 0, 1, 2)),
        "conv_bT": np.ascontiguousarray(f(inputs["ffn_conv_b"]).reshape(2, 44, 128).transpose(2, 0, 1)),
        "w_down": f(inputs["ffn_w_down"]),
        "ident_in": np.eye(128, dtype=np.float32),
        "jmat_in": np.ascontiguousarray(np.eye(64, dtype=np.float32)[::-1]),
    }
    kc = np.arange(64)[:, None]
    cc = np.arange(64)[None, :]
    cs_ = np.clip(cc - 8, 0, 48)
    cm = ((kc >= cs_) & (kc < cs_ + 16)).astype(np.float32)
    common["colmask_in"] = np.concatenate([cm, cm], 0)
    t_full = np.arange(SEQ)
    rt_kv = _rope_tables(t_full, t_full // GRID_W, t_full % GRID_W)
    t_s = np.arange(TS)
    rt_s = _rope_tables(t_s, t_s // GRID_W, t_s % GRID_W)
    rf_s = _rowflags(TS // 128, 0, TS // GRID_W)
    xkv = [_fm(xp[b]) for b in range(2)]
    in_maps = []
    for c in range(8):
        b, j = c // 4, c % 4
        t0 = j * OWN - HALO
        idx = np.arange(t0, t0 + TP)
        valid = (idx >= 0) & (idx < SEQ)
        idc = np.clip(idx, 0, SEQ - 1)
        xe = np.where(valid[:, None], xp[b][idc], np.float32(0.0)).astype(np.float32)
        m = dict(common)
        m["xkvT"] = xkv[b]
        m["xeT"] = _fm(xe)
        m["xsT"] = _fm(xs[c])
        m["cT"] = np.ascontiguousarray(np.stack([cp[b], cs[c]], 1).reshape(KC, 128, 2).transpose(1, 0, 2))
        m["rt_kv"] = rt_kv
        m["rt_e"] = _rope_tables(idc, idc // GRID_W, idc % GRID_W)
        m["rt_s"] = rt_s
        m["vm_e"] = np.ascontiguousarray(np.broadcast_to(valid.astype(np.float32)[None, :], (128, TP)))
        m["rf_e"] = _rowflags(TP // 128, t0 // GRID_W, SEQ // GRID_W)
        m["rf_s"] = rf_s
        in_maps.append(m)
    return in_maps


def kernel(**inputs):
    B = _get_builder()
    in_maps = host_inputs(inputs)
    res = run_bass_kernel_spmd(B.nc, in_maps, core_ids=list(range(8)))
    yp = np.zeros((2, SEQ, D), np.float32)
    ys = np.zeros((8, DSEQ, D), np.float32)
    for c in range(8):
        r = res.results[c]
        b, j = c // 4, c % 4
        yp[b, j * OWN:(j + 1) * OWN] = np.asarray(r["yT_p"]).reshape(D, OWN).T
        ys[c] = np.asarray(r["yT_s"]).reshape(D, DSEQ).T
    return (yp, ys)
```
